# Optimizing a Trainium2 kernel written in Bass

```python
import jax, jax.numpy as jnp
from jax import lax
import numpy as np

D_MODEL = 1024
BATCH = 8
SEQ = 2048
DEPTH = 1
DEC_BATCH = 128
DEC_SEQ = 4
PAST_LEN = 16384
PAGE_SIZE = 128

A_W = 512
CONV_A_K = 3
DN_HEADS = 4
DN_DK = 128
DN_DV = 128
DN_QK = DN_HEADS * DN_DK
DN_V = DN_HEADS * DN_DV
DN_CONV_CH = 2 * DN_QK + DN_V
DN_CONV_K = 4
DN_CHUNK = 64
MEM_TOKENS = 256
XA_HEADS = 4
XA_DH = 128
XA_W = XA_HEADS * XA_DH
N_BRANCH = 3
MIX_WIDTH = A_W + DN_V + XA_W
IN_WIDTH = 3 * A_W + DN_CONV_CH + DN_V + 2 * DN_HEADS + XA_W + N_BRANCH * D_MODEL
D_FF = 256 * (-(-(8 * D_MODEL) // (3 * 256)))
EPS = 1e-6

kernel_name = 'hybrid_gated_conv_deltanet_memxattn_step'


def rmsnorm(x, g):
    xf = x.astype(jnp.float32)
    y = xf * lax.rsqrt(jnp.mean(xf * xf, axis=-1, keepdims=True) + EPS)
    return (y * g.astype(jnp.float32)).astype(x.dtype)


def l2norm(x):
    xf = x.astype(jnp.float32)
    return xf * lax.rsqrt(jnp.sum(xf * xf, axis=-1, keepdims=True) + EPS)


def causal_conv(x, buf, w):
    width = w.shape[0]
    length = x.shape[1]
    xp = jnp.concatenate([buf.astype(x.dtype), x], axis=1)
    y = sum(xp[:, i:i + length] * w[i].astype(x.dtype) for i in range(width))
    return y, xp[:, length:]


def gated_delta_chunked(q, k, v, g, beta, s0):
    bsz, length = q.shape[0], q.shape[1]
    csz = min(DN_CHUNK, length)
    n = -(-length // csz)
    pad = n * csz - length

    def prep(t):
        t = jnp.pad(t, [(0, 0), (0, pad)] + [(0, 0)] * (t.ndim - 2))
        t = t.reshape((bsz, n, csz) + t.shape[2:])
        return jnp.swapaxes(jnp.moveaxis(t, 1, 0), 2, 3)

    qc, kc, vc, gc, bc = prep(q), prep(k), prep(v), prep(g), prep(beta)
    d = jnp.cumsum(gc, axis=-1)
    idx = jnp.arange(csz)
    causal = idx[:, None] >= idx[None, :]
    strict = idx[:, None] > idx[None, :]
    diff = d[..., :, None] - d[..., None, :]
    gamma = jnp.where(causal, jnp.exp(jnp.where(causal, diff, 0.0)), 0.0)
    kk = jnp.einsum('nbhid,nbhjd->nbhij', kc, kc)
    a_mat = jnp.where(strict, bc[..., :, None] * kk * gamma, 0.0)
    eye = jnp.eye(csz, dtype=a_mat.dtype)
    rhs = jnp.concatenate([vc * bc[..., None], kc * (bc * jnp.exp(d))[..., None]], axis=-1)
    sol = lax.linalg.triangular_solve(a_mat + eye, rhs, left_side=True, lower=True, unit_diagonal=True)
    u, w = sol[..., :DN_DV], sol[..., DN_DV:]
    qk = jnp.where(causal, jnp.einsum('nbhid,nbhjd->nbhij', qc, kc) * gamma, 0.0)

    def step(s, inp):
        q_i, k_i, u_i, w_i, qk_i, d_i = inp
        v_new = u_i - jnp.einsum('bhcd,bhde->bhce', w_i, s)
        o = (jnp.einsum('bhcd,bhde->bhce', q_i * jnp.exp(d_i)[..., None], s)
             + jnp.einsum('bhij,bhje->bhie', qk_i, v_new))
        d_last = d_i[..., -1:]
        s = (s * jnp.exp(d_last)[..., None]
             + jnp.einsum('bhcd,bhce->bhde', k_i * jnp.exp(d_last - d_i)[..., None], v_new))
        return s, o

    s_fin, o = lax.scan(step, s0.astype(jnp.float32), (qc, kc, u, w, qk, d))
    o = jnp.moveaxis(jnp.swapaxes(o, 2, 3), 0, 1).reshape(bsz, n * csz, DN_HEADS, DN_DV)[:, :length]
    return o, s_fin.astype(s0.dtype)


def mem_kv(mem, norm_mem_l, w_mem_kv_l):
    kv = rmsnorm(mem, norm_mem_l) @ w_mem_kv_l
    k, v = jnp.split(kv, 2, axis=-1)
    shp = mem.shape[:2] + (XA_HEADS, XA_DH)
    return k.reshape(shp), v.reshape(shp)


def mem_attention(q, mk, mv):
    s = jnp.einsum('blhd,bmhd->bhlm', q, mk.astype(q.dtype)).astype(jnp.float32) * (XA_DH ** -0.5)
    p = jax.nn.softmax(s, axis=-1)
    o = jnp.einsum('bhlm,bmhd->blhd', p.astype(q.dtype), mv.astype(q.dtype))
    return o.reshape(q.shape[0], q.shape[1], XA_W)


def trunk_layer(x, conv_a_buf, dn_conv_buf, dn_s, mk, mv, norm_mix_l, w_in_l, conv_a_w_l,
                dn_conv_w_l, a_log_l, dt_bias_l, dn_norm_l, w_branch_l, w_o_l, norm_ffn_l,
                w_up_l, w_down_l):
    bsz, length = x.shape[0], x.shape[1]
    xn = rmsnorm(x, norm_mix_l)
    proj = xn @ w_in_l
    sizes = (A_W, A_W, A_W, DN_CONV_CH, DN_V, DN_HEADS, DN_HEADS, XA_W, N_BRANCH * D_MODEL)
    offs = [int(o) for o in np.cumsum(sizes)[:-1]]
    b_gate, c_gate, h_in, qkv, z, a_raw, b_raw, xq, gates = jnp.split(proj, offs, axis=-1)
    g_a, g_dn, g_m = jnp.split(gates, N_BRANCH, axis=-1)

    conv_out, conv_a_new = causal_conv(c_gate * h_in, conv_a_buf, conv_a_w_l)
    y_a = b_gate * conv_out

    qkv_c, dn_conv_new = causal_conv(qkv, dn_conv_buf, dn_conv_w_l)
    qkv_c = jax.nn.silu(qkv_c)
    q, k, v = jnp.split(qkv_c, [DN_QK, 2 * DN_QK], axis=-1)
    q = l2norm(q.reshape(bsz, length, DN_HEADS, DN_DK)) * (DN_DK ** -0.5)
    k = l2norm(k.reshape(bsz, length, DN_HEADS, DN_DK))
    v = v.reshape(bsz, length, DN_HEADS, DN_DV).astype(jnp.float32)
    beta = jax.nn.sigmoid(b_raw.astype(jnp.float32))
    g = -jnp.exp(a_log_l.astype(jnp.float32)) * jax.nn.softplus(
        a_raw.astype(jnp.float32) + dt_bias_l.astype(jnp.float32))
    o_dn, dn_s_new = gated_delta_chunked(q, k, v, g, beta, dn_s)
    o_dn = rmsnorm(o_dn, dn_norm_l).astype(x.dtype).reshape(bsz, length, DN_V)
    y_dn = o_dn * jax.nn.silu(z)

    y_m = mem_attention(xq.reshape(bsz, length, XA_HEADS, XA_DH), mk, mv)

    merged = (jax.nn.sigmoid(g_a) * (y_a @ w_branch_l[:A_W])
              + jax.nn.sigmoid(g_dn) * (y_dn @ w_branch_l[A_W:A_W + DN_V])
              + jax.nn.sigmoid(g_m) * (y_m @ w_branch_l[A_W + DN_V:]))
    x = x + merged @ w_o_l

    gate, up = jnp.split(rmsnorm(x, norm_ffn_l) @ w_up_l, 2, axis=-1)
    x = x + (jax.nn.silu(gate) * up) @ w_down_l
    return x, conv_a_new, dn_conv_new, dn_s_new


def setup_inputs(seed: int = 0) -> dict:
    key = jax.random.key(seed)
    ks = jax.random.split(key, 24)
    f32 = jnp.float32

    def nrm(k, shape, scale):
        return jax.random.normal(k, shape, f32) * scale

    def gain(k, shape):
        return 1.0 + 0.05 * jax.random.normal(k, shape, f32)

    dt = jnp.exp(jax.random.uniform(ks[13], (DEPTH, DN_HEADS), f32, np.log(1e-3), np.log(1e-1)))
    return {
        'x_prompt': nrm(ks[0], (BATCH, SEQ, D_MODEL), 1.0),
        'x_sample': nrm(ks[1], (DEC_BATCH, DEC_SEQ, D_MODEL), 1.0),
        'mem_prompt': nrm(ks[2], (BATCH, MEM_TOKENS, D_MODEL), 1.0),
        'state_conv_a': nrm(ks[3], (DEPTH, DEC_BATCH, CONV_A_K - 1, A_W), 1.0),
        'state_dn_conv': nrm(ks[4], (DEPTH, DEC_BATCH, DN_CONV_K - 1, DN_CONV_CH), 1.0),
        'state_dn': nrm(ks[5], (DEPTH, DEC_BATCH, DN_HEADS, DN_DK, DN_DV), 0.1),
        'cache_mem_k': nrm(ks[6], (DEPTH, DEC_BATCH, MEM_TOKENS, XA_HEADS, XA_DH), 1.0),
        'cache_mem_v': nrm(ks[7], (DEPTH, DEC_BATCH, MEM_TOKENS, XA_HEADS, XA_DH), 1.0),
        'norm_mix': gain(ks[8], (DEPTH, D_MODEL)),
        'w_in': nrm(ks[9], (DEPTH, D_MODEL, IN_WIDTH), D_MODEL ** -0.5),
        'conv_a_w': nrm(ks[10], (DEPTH, CONV_A_K, A_W), CONV_A_K ** -0.5),
        'dn_conv_w': nrm(ks[11], (DEPTH, DN_CONV_K, DN_CONV_CH), DN_CONV_K ** -0.5),
        'dn_a_log': jnp.log(jax.random.uniform(ks[12], (DEPTH, DN_HEADS), f32, 1.0, 16.0)),
        'dn_dt_bias': dt + jnp.log(-jnp.expm1(-dt)),
        'dn_norm': gain(ks[14], (DEPTH, DN_DV)),
        'norm_mem': gain(ks[15], (DEPTH, D_MODEL)),
        'w_mem_kv': nrm(ks[16], (DEPTH, D_MODEL, 2 * XA_W), D_MODEL ** -0.5),
        'w_branch': nrm(ks[17], (DEPTH, MIX_WIDTH, D_MODEL), A_W ** -0.5),
        'w_o': nrm(ks[18], (DEPTH, D_MODEL, D_MODEL), D_MODEL ** -0.5),
        'norm_ffn': gain(ks[19], (DEPTH, D_MODEL)),
        'w_ffn_up': nrm(ks[20], (DEPTH, D_MODEL, 2 * D_FF), D_MODEL ** -0.5),
        'w_ffn_down': nrm(ks[21], (DEPTH, D_FF, D_MODEL), D_FF ** -0.5),
        'norm_final': gain(ks[22], (D_MODEL,)),
    }


def reference(x_prompt, x_sample, mem_prompt, state_conv_a, state_dn_conv, state_dn, cache_mem_k,
              cache_mem_v, norm_mix, w_in, conv_a_w, dn_conv_w, dn_a_log, dn_dt_bias, dn_norm,
              norm_mem, w_mem_kv, w_branch, w_o, norm_ffn, w_ffn_up, w_ffn_down, norm_final):
    bp = x_prompt.shape[0]
    hp, hs = x_prompt, x_sample
    ca_p, dc_p, s_p, mk_p, mv_p = [], [], [], [], []
    ca_s, dc_s, s_s = [], [], []
    for l in range(DEPTH):
        layer_w = (norm_mix[l], w_in[l], conv_a_w[l], dn_conv_w[l], dn_a_log[l], dn_dt_bias[l],
                   dn_norm[l], w_branch[l], w_o[l], norm_ffn[l], w_ffn_up[l], w_ffn_down[l])
        mk, mv = mem_kv(mem_prompt, norm_mem[l], w_mem_kv[l])
        zero_ca = jnp.zeros((bp, CONV_A_K - 1, A_W), x_prompt.dtype)
        zero_dc = jnp.zeros((bp, DN_CONV_K - 1, DN_CONV_CH), x_prompt.dtype)
        zero_s = jnp.zeros((bp, DN_HEADS, DN_DK, DN_DV), state_dn.dtype)
        hp, a_new, d_new, st_new = trunk_layer(hp, zero_ca, zero_dc, zero_s, mk, mv, *layer_w)
        ca_p.append(a_new); dc_p.append(d_new); s_p.append(st_new); mk_p.append(mk); mv_p.append(mv)
        hs, a_new, d_new, st_new = trunk_layer(hs, state_conv_a[l], state_dn_conv[l], state_dn[l],
                                               cache_mem_k[l], cache_mem_v[l], *layer_w)
        ca_s.append(a_new); dc_s.append(d_new); s_s.append(st_new)
    y_prompt = rmsnorm(hp, norm_final)
    y_sample = rmsnorm(hs, norm_final)
    return (y_prompt, y_sample, jnp.stack(ca_p), jnp.stack(dc_p), jnp.stack(s_p), jnp.stack(mk_p),
            jnp.stack(mv_p), jnp.stack(ca_s), jnp.stack(dc_s), jnp.stack(s_s))
```

```python
import contextlib
import numpy as np
import concourse.bass as bass
import concourse.mybir as mybir
from concourse.bass_utils import run_bass_kernel_spmd

F32 = mybir.dt.float32
BF16 = mybir.dt.bfloat16
F32R = mybir.dt.float32r
AF = mybir.ActivationFunctionType
ALU = mybir.AluOpType
AX = mybir.AxisListType

NCORES = 8
D = 1024
TP = 2048
NS = 16
LS = 4
T = TP + NS * LS
SLOTW = T
NSLOT = 32
BLKS = [(0, 512), (512, 512), (1024, 512), (1536, 512), (2048, 64)]
EPS = 1e-6
NRING = 32
WBUF = 6144
INW = 7176
C_B, C_C, C_H, C_Q, C_K, C_V, C_Z, C_AB, C_XQ, C_GA = 0, 512, 1024, 1536, 2048, 2560, 3072, 3584, 3592, 4104
V_GMIX, V_GMEM, V_GFFN, V_CAW, V_DCW, V_ALOG, V_DTB, V_DNN = 0, 8, 16, 24, 36, 84, 88, 92
NVEC = 220
K_ID, K_ONE, K_ML, K_MG, K_MUS, K_LV = 0, 128, 256, 384, 512, 640
NCST = 640 + 14 * 128


class Cell:
    __slots__ = ("w", "r", "x")

    def __init__(self, x=False):
        self.w = None
        self.r = {}
        self.x = x


class V:
    def __init__(self, ap, cells):
        self.ap = ap
        self.cells = cells

    def __getitem__(self, idx):
        return V(self.ap[idx], self.cells)

    def bc(self, shape):
        return V(self.ap.to_broadcast(shape), self.cells)

    def us(self, ax):
        return V(self.ap.unsqueeze(ax), self.cells)


class KB:
    def __init__(self, nc, es):
        self.nc = nc
        self.es = es
        self.eng = dict(pe=nc.tensor, act=nc.scalar, dve=nc.vector, pool=nc.gpsimd, sp=nc.sync)
        self.sem = {e: es.enter_context(nc.semaphore("s_" + e)) for e in ("pe", "act", "dve", "pool")}
        self.cnt = dict.fromkeys(self.sem, 0)
        self.known = {e: {} for e in self.eng}
        self.ring = {q: [es.enter_context(nc.semaphore("r_%s%d" % (q, i))) for i in range(NRING)] for q in ("sp", "pool")}
        self.ring_val = {q: [0] * NRING for q in self.ring}
        self.ring_pos = {q: 0 for q in self.ring}
        self.out_events = []
        self.npe = 0
        self.nalloc = 0

    def sb(self, es, shape, dt, name=None):
        self.nalloc += 1
        t = es.enter_context(self.nc.sbuf_tensor("%s_%d" % (name or "t", self.nalloc), list(shape), dt))
        return V(t[:], [Cell()])

    def need(self, e, ev):
        if ev is None:
            return
        key, sem, val = ev
        k = self.known[e]
        if k.get(key, 0) >= val:
            return
        self.eng[e].wait_ge(sem, val)
        k[key] = val

    def deps(self, e, outs, ins):
        for v in ins:
            for c in v.cells:
                self.need(e, c.w)
                if c.x:
                    for ev in c.r.values():
                        if ev[0] != e:
                            self.need(e, ev)
        for v in outs:
            for c in v.cells:
                self.need(e, c.w)
                for ev in c.r.values():
                    self.need(e, ev)

    def commit(self, ev, outs, ins):
        for v in ins:
            for c in v.cells:
                c.r[ev[0]] = ev
        for v in outs:
            for c in v.cells:
                c.w = ev
                c.r = {}

    def op(self, e, fn, outs, ins):
        self.deps(e, outs, ins)
        i = fn(self.eng[e])
        self.cnt[e] += 1
        i.then_inc(self.sem[e], 1)
        self.commit((e, self.sem[e], self.cnt[e]), outs, ins)

    def pe(self, fns, outs, ins):
        self.deps("pe", outs, ins)
        i = None
        for fn in fns:
            i = fn(self.nc.tensor)
            self.npe += 1
        self.cnt["pe"] += 1
        i.then_inc(self.sem["pe"], 1)
        self.commit(("pe", self.sem["pe"], self.cnt["pe"]), outs, ins)

    def dma(self, q, out, in_, is_output=False, **kw):
        self.deps(q, [out], [in_])
        i = self.ring_pos[q]
        self.ring_pos[q] = (i + 1) % NRING
        sem = self.ring[q][i]
        key = "r_%s%d" % (q, i)
        prev = self.ring_val[q][i]
        if prev:
            self.need(q, (key, sem, prev))
        val = prev + 16
        self.eng[q].dma_start(out=out.ap, in_=in_.ap, **kw).then_inc(sem, 16)
        self.ring_val[q][i] = val
        ev = (key, sem, val)
        self.commit(ev, [out], [in_])
        if is_output:
            self.out_events.append(ev)
        return ev

    def barrier(self, engines=("pe", "act", "dve", "sp", "pool")):
        evs = [(e, self.sem[e], self.cnt[e]) for e in ("pe", "act", "dve") if self.cnt[e]]
        for q in ("sp", "pool"):
            for i in range(NRING):
                if self.ring_val[q][i]:
                    evs.append(("r_%s%d" % (q, i), self.ring[q][i], self.ring_val[q][i]))
        for e in engines:
            for ev in evs:
                if ev[0] != e:
                    self.need(e, ev)

    def act(self, out, in_, func, scale=1.0, bias=0.0, accum=None, extra_in=()):
        kw = {}
        ins = [in_] + list(extra_in)
        if isinstance(scale, V):
            ins.append(scale)
            scale = scale.ap
        if isinstance(bias, V):
            ins.append(bias)
            bias = bias.ap
        outs = [out]
        if accum is not None:
            outs.append(accum)
            kw["accum_out"] = accum.ap
        self.op("act", lambda e: e.activation(out=out.ap, in_=in_.ap, func=func, scale=scale, bias=bias, **kw), outs, ins)

    def tt(self, out, a, b, op, e="dve"):
        self.op(e, lambda en: en.tensor_tensor(out=out.ap, in0=a.ap, in1=b.ap, op=op), [out], [a, b])

    def ts(self, out, a, s1, op0, s2=None, op1=None, e="dve"):
        ins = [a]
        if isinstance(s1, V):
            ins.append(s1)
            s1 = s1.ap
        if isinstance(s2, V):
            ins.append(s2)
            s2 = s2.ap
        kw = {}
        if op1 is not None:
            kw["op1"] = op1
        self.op(e, lambda en: en.tensor_scalar(out=out.ap, in0=a.ap, scalar1=s1, scalar2=s2, op0=op0, **kw), [out], ins)

    def stt(self, out, a, s, b, op0, op1):
        ins = [a, b]
        if isinstance(s, V):
            ins.append(s)
            s = s.ap
        self.op("dve", lambda en: en.scalar_tensor_tensor(out=out.ap, in0=a.ap, scalar=s, in1=b.ap, op0=op0, op1=op1), [out], ins)

    def copy(self, out, in_, e="dve"):
        if e == "act":
            self.op("act", lambda en: en.copy(out=out.ap, in_=in_.ap), [out], [in_])
        else:
            self.op(e, lambda en: en.tensor_copy(out=out.ap, in_=in_.ap), [out], [in_])

    def recip(self, out, in_):
        self.op("dve", lambda en: en.reciprocal(out=out.ap, in_=in_.ap), [out], [in_])

    def memset(self, out, val, e="dve"):
        self.op(e, lambda en: en.memset(out.ap, val), [out], [])

    def rsum(self, out, in_):
        self.op("dve", lambda en: en.reduce_sum(out=out.ap, in_=in_.ap, axis=AX.X), [out], [in_])


def run_window(gens, width):
    pending = list(gens)
    active = []
    while pending or active:
        while pending and len(active) < width:
            active.append(pending.pop(0))
        for g in list(active):
            try:
                next(g)
            except StopIteration:
                active.remove(g)


def mmf(out, lhsT, rhs, start, stop):
    return lambda pe: pe.matmul(out.ap, lhsT=lhsT.ap, rhs=rhs.ap, start=start, stop=stop)


def trf(out, in_, ident):
    return lambda pe: pe.transpose(out=out.ap, in_=in_.ap, identity=ident.ap)


class _Stop(Exception):
    pass


STOP = None
SUB = None
DBG_CORES = None
DBG_TAPS = False
LAST_R = None
PHASE_MARKS = []


def build_nc():
    nc = bass.Bass("TRN2", target_bir_lowering=False)

    def din(name, shape):
        return V(nc.dram_tensor(name, list(shape), F32, kind="ExternalInput").ap(), [])

    def dout(name, shape):
        return V(nc.dram_tensor(name, list(shape), F32, kind="ExternalOutput").ap(), [])

    xin = din("xin", [T, D])
    mem = din("mem", [256, D])
    sca = din("sca", [NS * 2, 512])
    sdc = din("sdc", [NS * 3, 1536])
    sdn = din("sdn", [NS * 4 * 128, 128])
    ckv = din("ckv", [NS * 256, 1024])
    w_in = din("w_in", [D, INW])
    w_kv = din("w_kv", [D, 1024])
    w_br = din("w_br", [1536, D])
    w_o = din("w_o", [D, D])
    w_up = din("w_up", [D, 5632])
    w_dn = din("w_dn", [2816, D])
    vecs_d = din("vecs", [128, NVEC])
    cst_d = din("cst", [128, NCST])
    gfin_d = din("gfin", [128, D])
    y_o = dout("y", [T, D])
    cap_o = dout("ca_p", [2, 512])
    dcp_o = dout("dc_p", [3, 1536])
    sp_o = dout("s_p", [512, 128])
    mk_o = dout("mk", [256, 512])
    mv_o = dout("mv", [256, 512])
    cas_o = dout("ca_s", [NS * 2, 512])
    dcs_o = dout("dc_s", [NS * 3, 1536])
    ss_o = dout("s_s", [NS * 512, 128])

    with contextlib.ExitStack() as es:
        k = KB(nc, es)
        arena_t = es.enter_context(nc.sbuf_tensor("arena", [128, NSLOT * SLOTW], BF16))
        arena = arena_t[:]
        arena3 = arena.rearrange("p (s t) -> p s t", t=SLOTW)
        arena32 = arena.bitcast(F32)
        acells = [[Cell() for _ in range(5)] for _ in range(NSLOT)]

        def cells_bytes(slot, b0, b1):
            out = []
            while b0 < b1:
                s = slot + b0 // (SLOTW * 2)
                bb = b0 % (SLOTW * 2)
                ci = min(bb // 1024, 4)
                out.append(acells[s][ci])
                nxt = (bb // 1024 + 1) * 1024 if ci < 4 else SLOTW * 2
                b0 += nxt - bb
            return out

        def SL(slot, c0, n, nslots=1):
            cs = []
            for s in range(slot, slot + nslots):
                cs += cells_bytes(s, c0 * 2, (c0 + n) * 2)
            if nslots == 1:
                return V(arena3[:, slot, c0:c0 + n], cs)
            return V(arena3[:, slot:slot + nslots, c0:c0 + n], cs)

        def SL32(slot, c0, n):
            base = slot * (SLOTW // 2)
            return V(arena32[:, base + c0: base + c0 + n], cells_bytes(slot, c0 * 4, (c0 + n) * 4))

        ps_t = es.enter_context(nc.psum_tensor("ps", [128, 8, 512], F32))
        psap = ps_t[:]
        pcells = [Cell(True) for _ in range(8)]

        def PS(bank, n=128, w=512, c0=0):
            return V(psap[0:n, bank, c0:c0 + w], [pcells[bank]])

        def PSB(bank, n=128):
            return V(psap[0:n, bank, :].bitcast(BF16), [pcells[bank]])

        def PS2(bank, n=128):
            return V(psap[0:n, bank:bank + 2, :], [pcells[bank], pcells[bank + 1]])

        PM = [0, 1, 2]
        PT = 3
        PD = [4, 5, 6, 7]

        ring = [k.sb(es, [128, WBUF], BF16, "wring") for _ in range(2)]
        vecs = k.sb(es, [128, NVEC], F32, "vecs")
        cst = k.sb(es, [128, NCST], F32, "cst")
        cstb = k.sb(es, [128, 256], BF16, "cstb")
        gb = k.sb(es, [128, 17, 8], F32, "gb")
        gbs = k.sb(es, [LS, NS, 8], F32, "gbs")
        rq = k.sb(es, [128, 17, 4], F32, "rq")
        rqs = k.sb(es, [LS, NS, 4], F32, "rqs")
        negA = k.sb(es, [128, 4], F32, "negA")
        KT = k.sb(es, [128, 4, 256], BF16, "KT")
        Vtok = k.sb(es, [128, 2, 512], BF16, "Vtok")

        k.dma("sp", vecs, vecs_d)
        k.dma("sp", cst, cst_d)
        k.copy(cstb, cst[:, 0:256])
        identf = cst[:, K_ID:K_ID + 128]
        onesf = cst[:, K_ONE:K_ONE + 128]
        ML = cst[:, K_ML:K_ML + 128]
        MG = cst[:, K_MG:K_MG + 128]
        MUS = cst[:, K_MUS:K_MUS + 128]
        identb = cstb[:, 0:128]
        onesb = cstb[:, 128:256]

        def LV(l):
            return cst[:, K_LV + l * 128: K_LV + (l + 1) * 128]

        def LVT(l):
            return cst[:, K_LV + (7 + l) * 128: K_LV + (8 + l) * 128]

        k.act(negA, vecs[:, V_ALOG:V_ALOG + 4], AF.Exp)
        k.ts(negA, negA, -1.0, ALU.mult)

        jobs = []

        def wsrc(w, r0, nk, c0, nc_):
            return V(w.ap[r0:r0 + nk * 128, c0:c0 + nc_].rearrange("(kc p) n -> p kc n", p=128), [])

        jobs.append([(wsrc(w_kv, 0, 8, 0, 512), 8, 512)])
        jobs.append([(wsrc(w_kv, 0, 8, 512, 512), 8, 512)])
        for mp in range(2):
            jobs.append([(wsrc(w_in, 0, 8, C_H + mp * 256, 256), 8, 256),
                         (wsrc(w_in, 0, 8, C_C + mp * 256, 256), 8, 256),
                         (wsrc(w_in, 0, 8, C_B + mp * 256, 256), 8, 256)])
        for j in range(3):
            jobs.append([(wsrc(w_in, 0, 8, C_Q + j * 512, 512), 8, 512)])
        jobs.append([(wsrc(w_in, 0, 8, C_XQ, 512), 8, 512)])
        jobs.append([(wsrc(w_in, 0, 8, C_Z, 520), 8, 520)])
        for mp in range(4):
            for b in range(3):
                jobs.append([(wsrc(w_in, 0, 8, C_GA + b * 1024 + mp * 256, 256), 8, 256),
                             (wsrc(w_br, b * 512, 4, mp * 256, 256), 4, 256)])
        for j in range(2):
            jobs.append([(wsrc(w_o, 0, 8, j * 512, 512), 8, 512)])
        FFN_PARTS = [(0, 8), (8, 8), (16, 6)]
        for (j0, nj) in FFN_PARTS:
            for j in range(j0, j0 + nj, 2):
                jobs.append([(wsrc(w_up, 0, 8, j * 128, 256), 8, 256),
                             (wsrc(w_up, 0, 8, 2816 + j * 128, 256), 8, 256)])
            for half in range(2):
                jobs.append([(wsrc(w_dn, j0 * 128, nj, half * 512, 512), nj, 512)])
        wstate = dict(issued=0, used=0)

        def w_issue(upto):
            while wstate["issued"] < min(upto, len(jobs)):
                ji = wstate["issued"]
                buf = ring[ji % 2]
                off = 0
                for (src, nk, ncol) in jobs[ji]:
                    dst = V(buf.ap[:, off:off + nk * ncol].rearrange("p (k n) -> p k n", n=ncol), buf.cells)
                    k.dma("pool", dst, src)
                    off += nk * ncol
                wstate["issued"] += 1

        def w_next(prefetch=True):
            ji = wstate["used"]
            w_issue(ji + (2 if prefetch else 1))
            buf = ring[ji % 2]
            views = []
            off = 0
            for (src, nk, ncol) in jobs[ji]:
                views.append(V(buf.ap[:, off:off + nk * ncol].rearrange("p (k n) -> p k n", n=ncol), buf.cells))
                off += nk * ncol
            wstate["used"] += 1
            return views

        def w_prefetch():
            w_issue(wstate["used"] + 1)

        w_issue(2)

        def tap(name, v):
            if not DBG_TAPS:
                return
            shp = list(v.ap.shape)
            d_ = V(nc.dram_tensor("tap_" + name, shp, v.ap.dtype, kind="ExternalOutput").ap(), [])
            k.dma("sp", d_, v, is_output=True)

        pm_rr = [0]

        def pm_next():
            b = PM[pm_rr[0] % 3]
            pm_rr[0] += 1
            return b

        try:
            XN = 0
            def _ph0(p0):
                xt = [k.sb(p0, [128, D], F32, "xt") for _ in range(4)]
                xs = [k.sb(p0, [128, D], BF16, "xs") for _ in range(2)]
                junk = k.sb(p0, [128, D], BF16, "junk")
                ssq = [k.sb(p0, [128, 1], F32, "ssq") for _ in range(2)]
                rst = [k.sb(p0, [128, 1], F32, "rst") for _ in range(2)]
                memT = k.sb(p0, [128, 8, 256], BF16, "memT")
                mko = k.sb(p0, [128, 2, 1024], F32, "mko")

                def norm_T(src_d, r0, n, gcol, dst, i):
                    x_, s_, q_, r_ = xt[i % 4], xs[i % 2], ssq[i % 2], rst[i % 2]
                    k.act(junk[0:n, :], x_[0:n, :], AF.Square, accum=q_[0:n, :])
                    k.act(r_[0:n, :], q_[0:n, :], AF.Sqrt, scale=1.0 / D, bias=EPS)
                    yield
                    k.recip(r_[0:n, :], r_[0:n, :])
                    k.ts(s_[0:n, :], x_[0:n, :], r_[0:n, 0:1], ALU.mult)
                    yield
                    pt = PSB(PT if i % 2 == 0 else PD[0])
                    k.pe([trf(pt[:, kk * 128: kk * 128 + n], s_[0:n, kk * 128:(kk + 1) * 128], identb[0:n, 0:n]) for kk in range(8)],
                         [pt], [s_, identb])
                    ptv = V(pt.ap.rearrange("p (k t) -> p k t", t=128)[:, :, 0:n], pt.cells)
                    k.tt(dst, ptv, vecs[:, gcol:gcol + 8].us(2).bc([128, 8, n]), ALU.mult)
                    xload(i + 4)
                    yield

                srcs0 = [(xin, tt_ * 128, 128 if tt_ < 16 else 64) for tt_ in range(17)] + [(mem, mc * 128, 128) for mc in range(2)]

                def xload(i):
                    if i < len(srcs0):
                        sd, r0, n = srcs0[i]
                        k.dma("sp", xt[i % 4][0:n, :], V(sd.ap[r0:r0 + n, :], []))

                for i_ in range(4):
                    xload(i_)
                if SUB and '0' in SUB:
                    return
                gens0 = []
                for tt_ in range(17):
                    n = 128 if tt_ < 16 else 64
                    gens0.append(norm_T(xin, tt_ * 128, n, V_GMIX, SL(XN, tt_ * 128, n, nslots=8), tt_))
                if SUB and 'A' in SUB:
                    return
                for mc in range(2):
                    gens0.append(norm_T(mem, mc * 128, 128, V_GMEM, memT[:, :, mc * 128:(mc + 1) * 128], 17 + mc))
                run_window(gens0, 2)
                if SUB and 'B' in SUB:
                    return

                for part in range(2):
                    (wv,) = w_next()
                    if SUB and 'F' in SUB and part == 1:
                        return
                    for mc in range(2):
                        b = pm_next()
                        k.pe([mmf(PS(b), memT[:, kk, mc * 128:(mc + 1) * 128], wv[:, kk, :], kk == 0, kk == 7) for kk in range(8)],
                             [PS(b)], [memT, wv])
                        if not (SUB and 'H' in SUB and part == 1):
                            k.copy(mko[:, mc, part * 512:(part + 1) * 512], PS(b), e=("dve" if (SUB and 'I' in SUB) else "act"))
                        if part == 1 and not (SUB and 'G' in SUB):
                            k.copy(Vtok[:, mc, :], mko[:, mc, 512:1024])
                    if SUB and 'C' in SUB:
                        return
                    if part == 0:
                        for hp in range(2):
                            b = pm_next()
                            fns = []
                            for hh in range(2):
                                h = hp * 2 + hh
                                fns += [mmf(PS(b, w=256, c0=hh * 256), wv[:, kk, h * 128:(h + 1) * 128], memT[:, kk, :], kk == 0, kk == 7) for kk in range(8)]
                            k.pe(fns, [PS(b)], [memT, wv])
                            k.copy(V(KT.ap[:, hp * 2:hp * 2 + 2, :], KT.cells), V(PS(b).ap.rearrange("p (h m) -> p h m", m=256), PS(b).cells))
                        if SUB and 'D' in SUB:
                            return
                if SUB and 'E' in SUB:
                    return
                for mc in range(2):
                    k.dma("sp", V(mk_o.ap[mc * 128:(mc + 1) * 128, :], []), mko[:, mc, 0:512], is_output=True)
                    k.dma("sp", V(mv_o.ap[mc * 128:(mc + 1) * 128, :], []), mko[:, mc, 512:1024], is_output=True)
                k.barrier()

            def proj_fm(b, wv, wc0, src_slot, nk, c0, n):
                k.pe([mmf(PS(b, w=n), wv[:, kk, wc0:wc0 + 128], SL(src_slot + kk, c0, n), kk == 0, kk == nk - 1) for kk in range(nk)],
                     [PS(b, w=n)], [wv] + [SL(src_slot + kk, c0, n) for kk in range(nk)])

            SQ, SK, SV_ = 8, 12, 16
            SYA, SYD, SYM = 20, 24, 28

            def _ph1(p1):
                ub = k.sb(p1, [128, 2 + TP], F32, "ub")
                ue = k.sb(p1, [128, NS, 6], F32, "ue")
                hb = [k.sb(p1, [128, 512], F32, "hb") for _ in range(2)]
                cvb = [k.sb(p1, [128, 512], F32, "cvb") for _ in range(2)]
                sca_sb = k.sb(p1, [NS * 2, 512], F32, "sca_sb")
                cap_sb = k.sb(p1, [2, 512], F32, "cap_sb")
                cas_sb = k.sb(p1, [NS * 2, 512], F32, "cas_sb")
                tl = k.sb(p1, [128, 32], F32, "tl")
                k.dma("sp", sca_sb, sca)
                k.memset(ub[:, 0:2], 0.0)
                for m in range(4):
                    if m % 2 == 0:
                        wh, wc, wb = w_next()
                    wo_ = (m % 2) * 128
                    k.pe([trf(PS(PT, w=32), sca_sb[:, m * 128:(m + 1) * 128], identf[0:32, 0:32])], [PS(PT)], [sca_sb, identf])
                    k.copy(ue[:, :, 0:2], V(PS(PT, w=32).ap.rearrange("p (s i) -> p s i", i=2), [pcells[PT]]))
                    for bi, (c0, n) in enumerate(BLKS):
                        bh = pm_next()
                        proj_fm(bh, wh, wo_, XN, 8, c0, n)
                        h_ = hb[bi % 2]
                        k.copy(h_[:, 0:n], PS(bh, w=n), e="act")
                        bc_ = pm_next()
                        proj_fm(bc_, wc, wo_, XN, 8, c0, n)
                        if bi < 4:
                            k.tt(ub[:, 2 + c0: 2 + c0 + n], PS(bc_, w=n), h_[:, 0:n], ALU.mult)
                        else:
                            k.tt(ue[:, :, 2:6], V(PS(bc_, w=n).ap.rearrange("p (s t) -> p s t", t=LS), [pcells[bc_]]),
                                 V(h_.ap[:, 0:n].rearrange("p (s t) -> p s t", t=LS), h_.cells), ALU.mult)
                    w0 = vecs[:, V_CAW + 0 * 4 + m: V_CAW + 0 * 4 + m + 1]
                    w1 = vecs[:, V_CAW + 1 * 4 + m: V_CAW + 1 * 4 + m + 1]
                    w2 = vecs[:, V_CAW + 2 * 4 + m: V_CAW + 2 * 4 + m + 1]
                    for bi, (c0, n) in enumerate(BLKS):
                        cv_ = cvb[bi % 2]
                        if bi < 4:
                            k.ts(cv_[:, 0:n], ub[:, c0 + 2: c0 + 2 + n], w2, ALU.mult)
                            k.stt(cv_[:, 0:n], ub[:, c0 + 1: c0 + 1 + n], w1, cv_[:, 0:n], ALU.mult, ALU.add)
                            k.stt(cv_[:, 0:n], ub[:, c0: c0 + n], w0, cv_[:, 0:n], ALU.mult, ALU.add)
                        else:
                            cv3 = V(cv_.ap[:, 0:n].rearrange("p (s t) -> p s t", t=LS), cv_.cells)
                            k.ts(cv3, ue[:, :, 2:6], w2, ALU.mult)
                            k.stt(cv3, ue[:, :, 1:5], w1, cv3, ALU.mult, ALU.add)
                            k.stt(cv3, ue[:, :, 0:4], w0, cv3, ALU.mult, ALU.add)
                        bb = pm_next()
                        proj_fm(bb, wb, wo_, XN, 8, c0, n)
                        k.tt(SL(SYA + m, c0, n), PS(bb, w=n), cv_[:, 0:n], ALU.mult)
                    k.pe([trf(PS(PT, n=2, w=128), ub[:, TP:TP + 2], identf)], [PS(PT)], [ub, identf])
                    k.copy(cap_sb[:, m * 128:(m + 1) * 128], PS(PT, n=2, w=128), e="act")
                    k.copy(V(tl.ap.rearrange("p (s i) -> p s i", i=2), tl.cells), ue[:, :, 4:6])
                    k.pe([trf(PS(PT, n=32, w=128), tl, identf)], [PS(PT)], [tl, identf])
                    k.copy(cas_sb[:, m * 128:(m + 1) * 128], PS(PT, n=32, w=128), e="act")
                k.dma("sp", cap_o, cap_sb, is_output=True)
                k.dma("sp", cas_o, cas_sb, is_output=True)
                k.barrier()

            def _ph2(p1):
                xeb = [k.sb(p1, [128, 3 + TP], BF16, "xe") for _ in range(2)]
                seb = [k.sb(p1, [128, NS, 7], BF16, "se") for _ in range(2)]
                dg = [k.sb(p1, [128, 4, 128], BF16, "dg") for _ in range(2)]
                ktf = k.sb(p1, [128, T], F32, "ktf")
                sqb = [k.sb(p1, [128, 512], BF16, "sqb") for _ in range(2)]
                rr = [k.sb(p1, [128, 512], F32, "rr") for _ in range(2)]
                sdc_sb = k.sb(p1, [NS * 3, 1536], F32, "sdc_sb")
                dcp_t = [k.sb(p1, [3, 128], F32, "dcp_t") for _ in range(2)]
                dcs_t = [k.sb(p1, [NS * 3, 128], F32, "dcs_t") for _ in range(2)]
                x3 = [k.sb(p1, [128, 3], F32, "x3") for _ in range(2)]
                tl = [k.sb(p1, [128, 48], F32, "tl") for _ in range(2)]
                k.dma("sp", sdc_sb, sdc)
                k.memset(xeb[0][:, 0:3], 0.0)
                k.memset(xeb[1][:, 0:3], 0.0)
                PA = [PM[0], PM[1]]
                PB = [PM[2], PD[0]]
                wq_of = {}

                def stageA(j):
                    if j % 4 == 0:
                        (wq_of[j // 4],) = w_next()
                    wq = wq_of[j // 4]
                    jc = (j % 4) * 128
                    xe, se, dg_ = xeb[j % 2], seb[j % 2], dg[j % 2]
                    for i in range(4):
                        k.act(dg_[:, i, :], identf, AF.Copy, scale=vecs[:, V_DCW + i * 12 + j: V_DCW + i * 12 + j + 1])
                    k.pe([trf(PS(PT, w=48), sdc_sb[:, j * 128:(j + 1) * 128], identf[0:48, 0:48])], [PS(PT)], [sdc_sb, identf])
                    k.copy(se[:, :, 0:3], V(PS(PT, w=48).ap.rearrange("p (s i) -> p s i", i=3), [pcells[PT]]))
                    yield
                    for bi, (c0, n) in enumerate(BLKS):
                        b = PA[bi % 2]
                        proj_fm(b, wq, jc, XN, 8, c0, n)
                        if bi < 4:
                            k.copy(xe[:, 3 + c0: 3 + c0 + n], PS(b, w=n), e=("dve" if bi % 2 else "act"))
                            if bi == 3:
                                k.copy(x3[j % 2], PS(b, w=3, c0=n - 3))
                        else:
                            pv = V(PS(b, w=n).ap.rearrange("p (s t) -> p s t", t=LS), [pcells[b]])
                            k.copy(se[:, :, 3:7], pv, e="act")
                            k.copy(V(tl[j % 2].ap.rearrange("p (s i) -> p s i", i=3), tl[j % 2].cells), pv[:, :, 1:4])
                        yield
                    k.pe([trf(PS(PT, n=3, w=128), x3[j % 2], identf)], [PS(PT)], [x3[j % 2], identf])
                    k.copy(dcp_t[j % 2], PS(PT, n=3, w=128), e="act")
                    k.dma("sp", V(dcp_o.ap[:, j * 128:(j + 1) * 128], []), dcp_t[j % 2], is_output=True)
                    k.pe([trf(PS(PT, n=48, w=128), tl[j % 2], identf)], [PS(PT)], [tl[j % 2], identf])
                    k.copy(dcs_t[j % 2], PS(PT, n=48, w=128), e="act")
                    k.dma("sp", V(dcs_o.ap[:, j * 128:(j + 1) * 128], []), dcs_t[j % 2], is_output=True)
                    yield

                def stageB(j):
                    kind = j // 4
                    dst_slot = SQ + j
                    xe, se, dg_ = xeb[j % 2], seb[j % 2], dg[j % 2]
                    for bi, (c0, n) in enumerate(BLKS):
                        b = PB[bi % 2]
                        if bi < 4:
                            k.pe([mmf(PS(b, w=n), dg_[:, i, :], xe[:, c0 + i: c0 + i + n], i == 0, i == 3) for i in range(4)],
                                 [PS(b, w=n)], [dg_, xe])
                        else:
                            k.pe([mmf(V(PS(b, w=n).ap.rearrange("p (s t) -> p s t", t=LS), [pcells[b]]), dg_[:, i, :], se[:, :, i:i + 4], i == 0, i == 3) for i in range(4)],
                                 [PS(b, w=n)], [dg_, se])
                        if kind == 1:
                            k.act(ktf[:, c0:c0 + n], PS(b, w=n), AF.Silu)
                        else:
                            k.act(SL(dst_slot, c0, n), PS(b, w=n), AF.Silu)
                        yield
                    for bi, (c0, n) in enumerate(BLKS):
                        sq_ = sqb[bi % 2]
                        if kind == 0:
                            k.tt(sq_[:, 0:n], SL(dst_slot, c0, n), SL(dst_slot, c0, n), ALU.mult)
                            h = j
                            if bi < 4:
                                fns = [mmf(PS(PD[1], w=1, c0=t4), sq_[:, t4 * 128:(t4 + 1) * 128], onesb[:, 0:1], True, True) for t4 in range(4)]
                                k.pe(fns, [PS(PD[1])], [sq_, onesb])
                                k.copy(rq[:, bi * 4:(bi + 1) * 4, h], PS(PD[1], w=4))
                            else:
                                fns = [mmf(PS(PD[1], n=LS, w=1, c0=s), sq_[:, s * LS:(s + 1) * LS], onesb[:, 0:1], True, True) for s in range(NS)]
                                k.pe(fns, [PS(PD[1])], [sq_, onesb])
                                k.copy(rqs[:, :, h], PS(PD[1], n=LS, w=NS))
                            yield
                        elif kind == 1:
                            k.tt(sq_[:, 0:n], ktf[:, c0:c0 + n], ktf[:, c0:c0 + n], ALU.mult)
                            k.pe([mmf(PS(PD[1], w=n), onesb, sq_[:, 0:n], True, True)], [PS(PD[1])], [sq_, onesb])
                            r_ = rr[bi % 2]
                            k.act(r_[:, 0:n], PS(PD[1], w=n), AF.Ln, bias=EPS)
                            k.act(r_[:, 0:n], r_[:, 0:n], AF.Exp, scale=-0.5)
                            k.tt(SL(dst_slot, c0, n), ktf[:, c0:c0 + n], r_[:, 0:n], ALU.mult)
                            yield

                def run_il(gens):
                    gens = [g for g in gens if g is not None]
                    while gens:
                        for g in list(gens):
                            try:
                                next(g)
                            except StopIteration:
                                gens.remove(g)

                run_il([stageA(0)])
                for j in range(12):
                    run_il([stageA(j + 1) if j + 1 < 12 else None, stageB(j)])
                for r_, in ((V(rq.ap[:, 0:16, :], rq.cells),), (rqs,)):
                    k.act(r_, r_, AF.Ln, bias=EPS)
                    k.act(r_, r_, AF.Exp, scale=-0.5)
                    k.ts(r_, r_, 128.0 ** -0.5, ALU.mult)
                k.barrier()

            def _ph3(p1):
                eT = [k.sb(p1, [128, 2, 512], BF16, "eT") for _ in range(2)]
                rden = [k.sb(p1, [128, 512], F32, "rden") for _ in range(2)]
                kvb = [k.sb(p1, [128, 2, 1024], BF16, "kvb") for _ in range(3)]

                def kvload(s):
                    if s < NS:
                        k.dma("pool", kvb[s % 3], V(ckv.ap[s * 256:(s + 1) * 256, :].rearrange("(mc p) n -> p mc n", p=128), []))

                for s in range(3):
                    kvload(s)
                KTs = [k.sb(p1, [128, 4, 256], BF16, "KTs") for _ in range(3)]
                eTs = [k.sb(p1, [128, 4, 2, LS], BF16, "eTs") for _ in range(3)]
                rds = [k.sb(p1, [128, 4, LS], F32, "rds") for _ in range(3)]
                (wx,) = w_next()
                for h in range(4):
                    for bi, (c0, n) in enumerate(BLKS):
                        b = pm_next()
                        proj_fm(b, wx, h * 128, XN, 8, c0, n)
                        k.copy(SL(SYM + h, c0, n), PS(b, w=n), e="act")
                w_prefetch()
                sc = 128.0 ** -0.5
                it = 0
                for h in range(4):
                    for bi, (c0, n) in enumerate(BLKS[:4]):
                        e_ = eT[it % 2]
                        r_ = rden[it % 2]
                        BK = PD if it % 2 == 0 else [PM[0], PM[1], PM[2], PT]
                        it += 1
                        for mc in range(2):
                            k.pe([mmf(PS(BK[mc]), KT[:, h, mc * 128:(mc + 1) * 128], SL(SYM + h, c0, n), True, True)],
                                 [PS(BK[mc])], [KT, SL(SYM + h, c0, n)])
                            k.act(e_[:, mc, :], PS(BK[mc]), AF.Exp, scale=sc)
                        k.pe([mmf(PS(BK[2]), Vtok[:, mc, h * 128:(h + 1) * 128], e_[:, mc, :], mc == 0, mc == 1) for mc in range(2)],
                             [PS(BK[2])], [Vtok, e_])
                        k.pe([mmf(PS(BK[3]), onesb, e_[:, mc, :], mc == 0, mc == 1) for mc in range(2)],
                             [PS(BK[3])], [onesb, e_])
                        k.act(r_, PS(BK[3]), AF.Ln)
                        k.act(r_, r_, AF.Exp, scale=-1.0)
                        k.tt(SL(SYM + h, c0, n), PS(BK[2]), r_, ALU.mult)
                def satt(s, BK):
                    kv_, kts, es_, rd_ = kvb[s % 3], KTs[s % 3], eTs[s % 3], rds[s % 3]
                    ck_ = V(kv_.ap[:, :, 0:512], kv_.cells)
                    cv_ = V(kv_.ap[:, :, 512:1024], kv_.cells)
                    bT, bS = BK
                    bO, oc = bS, 64
                    for hp in range(2):
                        pt = PSB(bT)
                        fns = []
                        for hh in range(2):
                            for mc in range(2):
                                h = hp * 2 + hh
                                fns.append(trf(pt[:, hh * 256 + mc * 128: hh * 256 + (mc + 1) * 128], ck_[:, mc, h * 128:(h + 1) * 128], identb))
                        k.pe(fns, [pt], [ck_, identb])
                        k.copy(V(kts.ap[:, hp * 2:hp * 2 + 2, :], kts.cells), V(pt.ap[:, 0:512].rearrange("p (h m) -> p h m", m=256), pt.cells),
                               e=("act" if hp else "dve"))
                        yield
                    c0 = TP + s * LS
                    fns = []
                    for h in range(4):
                        for mc in range(2):
                            fns.append(mmf(PS(bS, w=LS, c0=(h * 2 + mc) * LS), kts[:, h, mc * 128:(mc + 1) * 128], SL(SYM + h, c0, LS), True, True))
                    k.pe(fns, [PS(bS)], [kts] + [SL(SYM + h, c0, LS) for h in range(4)])
                    k.act(V(es_.ap.rearrange("p h m t -> p (h m t)"), es_.cells), PS(bS, w=8 * LS), AF.Exp, scale=sc)
                    yield
                    fns = []
                    for h in range(4):
                        for mc in range(2):
                            fns.append(mmf(PS(bO, w=LS, c0=oc + h * LS), cv_[:, mc, h * 128:(h + 1) * 128], es_[:, h, mc, :], mc == 0, mc == 1))
                        for mc in range(2):
                            fns.append(mmf(PS(bO, w=LS, c0=oc + 4 * LS + h * LS), onesb, es_[:, h, mc, :], mc == 0, mc == 1))
                    k.pe(fns, [PS(bO)], [cv_, es_, onesb])
                    k.recip(V(rd_.ap.rearrange("p h t -> p (h t)"), rd_.cells), PS(bO, w=4 * LS, c0=oc + 4 * LS))
                    k.tt(SL(SYM, c0, LS, nslots=4), V(PS(bO, w=4 * LS, c0=oc).ap.rearrange("p (h t) -> p h t", t=LS), [pcells[bO]]), rd_, ALU.mult)
                    kvload(s + 3)
                    yield

                BKS = [(PT, PD[0]), (PM[0], PM[1]), (PM[2], PD[1])]
                pending = [satt(s, BKS[s % 3]) for s in range(NS)]
                active = []
                while pending or active:
                    while pending and len(active) < 3:
                        active.append(pending.pop(0))
                    for g in list(active):
                        try:
                            next(g)
                        except StopIteration:
                            active.remove(g)
                k.barrier()

            def _ph4(p1):
                (wz,) = w_next()

                def f32t(nm):
                    return k.sb(p1, [128, 4, 128], F32, nm)

                def b16t(nm):
                    return k.sb(p1, [128, 4, 128], BF16, nm)

                Gt, Gam, U_ = b16t("Gt"), b16t("Gam"), b16t("U")
                G4 = f32t("G4")
                def b16h(nm):
                    return k.sb(p1, [128, 2, 128], BF16, nm)

                Lb = [[b16h("L"), b16h("L")] for _ in range(2)]
                Fb = [[b16h("F"), b16h("F")] for _ in range(2)]
                nGh = [b16h("nG"), b16h("nG")]
                Eb = [[b16h("E"), b16h("E")] for _ in range(3)]
                QKm = [b16t("QKm") for _ in range(3)]
                kd = [b16t("kd") for _ in range(3)]
                vtok = [b16t("vtok") for _ in range(3)]
                sm = [k.sb(p1, [128, 16], F32, "sm") for _ in range(3)]
                vS, vnew, ydn = b16t("vS"), b16t("vnew"), b16t("ydn")
                o_ = f32t("o")
                S_, Sb = f32t("S"), b16t("Sb")
                S2_, Sb2 = f32t("S2"), b16t("Sb2")
                sm2 = k.sb(p1, [128, 16], F32, "sm2")
                ab = k.sb(p1, [128, 17, 8], F32, "ab")
                abs_ = k.sb(p1, [LS, NS, 8], F32, "abs")
                tmp17 = k.sb(p1, [128, 17, 4], F32, "tmp17")

                for h in range(4):
                    for bi, (c0, n) in enumerate(BLKS):
                        b = pm_next()
                        proj_fm(b, wz, h * 128, XN, 8, c0, n)
                        k.act(SL(SYD + h, c0, n), PS(b, w=n), AF.Silu)
                for tt_ in range(16):
                    k.pe([mmf(PS(PT, w=8, c0=tt_ * 8), SL(XN + kk, tt_ * 128, 128), wz[:, kk, 512:520], kk == 0, kk == 7) for kk in range(8)],
                         [PS(PT)], [wz] + [SL(XN + kk, tt_ * 128, 128) for kk in range(8)])
                k.copy(V(ab.ap[:, 0:16, :], ab.cells), V(PS(PT, w=128).ap.rearrange("p (t c) -> p t c", c=8), [pcells[PT]]))
                for s in range(NS):
                    k.pe([mmf(PS(PT, n=LS, w=8, c0=s * 8), SL(XN + kk, TP + s * LS, LS), wz[:, kk, 512:520], kk == 0, kk == 7) for kk in range(8)],
                         [PS(PT)], [wz] + [SL(XN + kk, TP + s * LS, LS) for kk in range(8)])
                k.copy(abs_, V(PS(PT, n=LS, w=128).ap.rearrange("p (t c) -> p t c", c=8), [pcells[PT]]))
                for (a_, g_, nt, npart) in ((ab, gb, 16, 128), (abs_, gbs, NS, LS)):
                    av = V(a_.ap[0:npart, 0:nt, 0:4], a_.cells)
                    bv = V(a_.ap[0:npart, 0:nt, 4:8], a_.cells)
                    gv = V(g_.ap[0:npart, 0:nt, 0:4], g_.cells)
                    gbv = V(g_.ap[0:npart, 0:nt, 4:8], g_.cells)
                    tv = V(tmp17.ap[0:npart, 0:nt, :], tmp17.cells)
                    k.act(gbv, bv, AF.Sigmoid)
                    k.tt(tv, av, vecs[0:npart, V_DTB:V_DTB + 4].us(1).bc([npart, nt, 4]), ALU.add)
                    k.act(tv, tv, AF.Exp)
                    k.act(tv, tv, AF.Ln, bias=1.0)
                    k.tt(gv, tv, negA[0:npart, :].us(1).bc([npart, nt, 4]), ALU.mult)

                def bc4(v, n, w=128):
                    return v.us(2).bc([n, 4, w])

                def mbc(m, n):
                    return V(m.ap[0:n, 0:n].unsqueeze(1).to_broadcast([n, 4, n]), m.cells)

                def P4(bank, n, w):
                    return V(psap[0:n, bank, 0:4 * w].rearrange("p (h w) -> p h w", w=w), [pcells[bank]])

                F0, F1, SA, SB = PD[0], PD[1], PD[2], PD[3]

                def front(n, c0, gtok, btok, i):
                    E_, Q_, kd_, vt_, s_ = Eb[i % 3], QKm[i % 3], kd[i % 3], vtok[i % 3], sm[i % 3]
                    L_, F_ = Lb[i % 2], Fb[i % 2]
                    kT = [SL(SK + h, c0, n) for h in range(4)]
                    qT = [SL(SQ + h, c0, n) for h in range(4)]
                    vT = [SL(SV_ + h, c0, n) for h in range(4)]
                    k.pe([mmf(PS(F0, n=n, w=4), ML[0:n, 0:n], gtok, True, True),
                          mmf(PS(F0, n=128, w=4, c0=8), onesf[0:n, :], gtok, True, True)], [PS(F0)], [ML, onesf, gtok])
                    k.copy(s_[0:n, 0:4], PS(F0, n=n, w=4), e="act")
                    k.act(s_[:, 12:16], PS(F0, w=4, c0=8), AF.Exp)
                    k.tt(s_[0:n, 8:12], PS(F0, n=n, w=4, c0=8), s_[0:n, 0:4], ALU.subtract)
                    k.act(s_[0:n, 4:8], s_[0:n, 0:4], AF.Exp)
                    k.act(s_[0:n, 8:12], s_[0:n, 8:12], AF.Exp)
                    yield
                    for h in range(4):
                        k.act(G4[0:n, h, 0:n], MG[0:n, 0:n], AF.Copy, scale=gtok[:, h:h + 1])
                    yield
                    k.pe([mmf(P4(F0, n, n)[:, h, :], G4[0:n, h, 0:n], ML[0:n, 0:n], True, True) for h in range(4)],
                         [PS(F0)], [G4, ML])
                    k.pe([mmf(P4(F1, n, n)[:, h, :], kT[h], kT[h], True, True) for h in range(4)], [PS(F1)], kT)
                    k.act(Gam[0:n, :, 0:n], P4(F0, n, n), AF.Exp)
                    yield
                    pt = PSB(PT, n)
                    k.pe([trf(pt[:, h * 128:(h + 1) * 128], kT[h], identb) for h in range(4)] +
                         [trf(pt[:, 512 + h * 128: 512 + (h + 1) * 128], vT[h], identb) for h in range(4)], [pt], kT + vT + [identb])
                    p8 = V(pt.ap.rearrange("p (a h w) -> p a h w", a=2, w=128), pt.cells)
                    k.tt(kd_[0:n], p8[:, 0], bc4(s_[0:n, 8:12], n), ALU.mult)
                    k.copy(vt_[0:n], p8[:, 1], e="act")
                    yield
                    k.pe([mmf(P4(F0, n, n)[:, h, :], kT[h], qT[h], True, True) for h in range(4)], [PS(F0)], kT + qT)
                    k.tt(Gt[0:n, :, 0:n], P4(F0, n, n), Gam[0:n, :, 0:n], ALU.mult)
                    yield
                    k.tt(Q_[0:n, :, 0:n], Gt[0:n, :, 0:n], mbc(ML, n), ALU.mult)
                    k.tt(Gam[0:n, :, 0:n], Gam[0:n, :, 0:n], mbc(MUS, n), ALU.mult)
                    yield
                    for h in range(4):
                        k.stt(U_[0:n, h, 0:n], P4(F1, n, n)[:, h, :], btok[:, h:h + 1], Gam[0:n, h, 0:n], ALU.mult, ALU.mult)
                        if h == 1:
                            yield
                    yield
                    ptl = PSB(PT, n)
                    k.pe([trf(ptl[:, h * 128: h * 128 + n], U_[0:n, h, 0:n], identb[0:n, 0:n]) for h in range(4)], [ptl], [U_, identb])
                    ptl4 = V(ptl.ap[:, 0:512].rearrange("p (h w) -> p h w", w=128)[:, :, 0:n], ptl.cells)
                    idb2 = V(identf.ap[0:n, 0:n].unsqueeze(1).to_broadcast([n, 2, n]), identf.cells)
                    k.tt(Gt[0:n, :, 0:n], U_[0:n, :, 0:n], mbc(LV(0), n), ALU.mult)
                    for hh in range(2):
                        k.copy(L_[hh][0:n, :, 0:n], ptl4[:, 2 * hh:2 * hh + 2, :], e="act")
                        k.tt(E_[hh][0:n, :, 0:n], idb2, Gt[0:n, 2 * hh:2 * hh + 2, 0:n], ALU.add)
                    k.tt(Gam[0:n, :, 0:n], ptl4, mbc(LVT(0), n), ALU.mult)
                    yield
                    for hh in range(2):
                        k.tt(F_[hh][0:n, :, 0:n], idb2, Gam[0:n, 2 * hh:2 * hh + 2, 0:n], ALU.add)
                    yield

                def solve(n, i, hh):
                    E_, L_, F_, nG = Eb[i % 3][hh], Lb[i % 2][hh], Fb[i % 2][hh], nGh[hh]
                    bank = SA if hh == 0 else SB
                    nl = n.bit_length() - 1

                    def H2(c):
                        return V(psap[0:n, bank, c * 256: c * 256 + 2 * n].rearrange("p (h w) -> p h w", w=n), [pcells[bank]])

                    m2 = lambda m: V(m.ap[0:n, 0:n].unsqueeze(1).to_broadcast([n, 2, n]), m.cells)
                    for l in range(1, nl):
                        k.pe([mmf(H2(0)[:, h, :], L_[0:n, h, 0:n], E_[0:n, h, 0:n], True, True) for h in range(2)],
                             [PS(bank)], [L_, E_])
                        k.tt(nG[0:n, :, 0:n], H2(0), m2(LV(l)), ALU.mult)
                        yield
                        fns = []
                        for h in range(2):
                            fns.append(mmf(H2(1)[:, h, :], F_[0:n, h, 0:n], identb[0:n, 0:n], True, False))
                            fns.append(mmf(H2(1)[:, h, :], F_[0:n, h, 0:n], nG[0:n, h, 0:n], False, True))
                        if l < nl - 1:
                            for h in range(2):
                                fns.append(mmf(H2(0)[:, h, :], identb[0:n, 0:n], F_[0:n, h, 0:n], True, False))
                                fns.append(mmf(H2(0)[:, h, :], nG[0:n, h, 0:n], F_[0:n, h, 0:n], False, True))
                        k.pe(fns, [PS(bank)], [F_, nG, identb])
                        k.copy(E_[0:n, :, 0:n], H2(1), e="act")
                        if l < nl - 1:
                            k.copy(F_[0:n, :, 0:n], H2(0), e="act")
                        yield

                def seq(n, c0, btok, rqtok, i, S_=S_, Sb=Sb, need_sb=True):
                    E_, Q_, kd_, vt_, s_ = Eb[i % 3], QKm[i % 3], kd[i % 3], vtok[i % 3], sm[i % 3]
                    kT = [SL(SK + h, c0, n) for h in range(4)]
                    qT = [SL(SQ + h, c0, n) for h in range(4)]
                    b1, b2, b3 = PM[0], PM[1], PM[2]
                    k.pe([mmf(P4(b1, n, 128)[:, h, :], kT[h], Sb[:, h, :], True, True) for h in range(4)], [PS(b1)], kT + [Sb])
                    k.pe([mmf(P4(b2, n, 128)[:, h, :], qT[h], Sb[:, h, :], True, True) for h in range(4)], [PS(b2)], qT + [Sb])
                    k.tt(o_[0:n], P4(b1, n, 128), bc4(s_[0:n, 4:8], n), ALU.mult)
                    yield
                    k.tt(vS[0:n], vt_[0:n], o_[0:n], ALU.subtract)
                    yield
                    k.pe([mmf(P4(b3, n, 128)[:, h, :], E_[h // 2][0:n, h % 2, 0:n], vS[0:n, h, :], True, True) for h in range(4)], [PS(b3)], [E_[0], E_[1], vS])
                    k.tt(vnew[0:n], P4(b3, n, 128), bc4(btok, n), ALU.mult)
                    k.tt(o_[0:n], P4(b2, n, 128), bc4(s_[0:n, 4:8], n), ALU.mult)
                    yield
                    k.pe([mmf(P4(b1, n, 128)[:, h, :], Q_[0:n, h, 0:n], vnew[0:n, h, :], True, True) for h in range(4)], [PS(b1)], [Q_, vnew])
                    k.pe([mmf(P4(b3, 128, 128)[:, h, :], kd_[0:n, h, :], vnew[0:n, h, :], True, True) for h in range(4)], [PS(b3)], [kd_, vnew])
                    k.tt(S_, S_, bc4(s_[:, 12:16], 128), ALU.mult)
                    yield
                    k.tt(S_, S_, P4(b3, 128, 128), ALU.add)
                    if need_sb:
                        k.copy(Sb, S_, e="act")
                    yield
                    k.tt(o_[0:n], o_[0:n], P4(b1, n, 128), ALU.add)
                    for h in range(4):
                        k.act(ydn[0:n, h, :], o_[0:n, h, :], AF.Square, accum=sm2[0:n, h:h + 1])
                    yield
                    k.tt(sm2[0:n, 4:8], rqtok, rqtok, ALU.mult)
                    k.tt(sm2[0:n, 0:4], sm2[0:n, 0:4], sm2[0:n, 4:8], ALU.mult)
                    k.act(sm2[0:n, 0:4], sm2[0:n, 0:4], AF.Ln, scale=1.0 / 128, bias=EPS)
                    k.act(sm2[0:n, 0:4], sm2[0:n, 0:4], AF.Exp, scale=-0.5)
                    k.tt(sm2[0:n, 0:4], sm2[0:n, 0:4], rqtok, ALU.mult)
                    yield
                    k.tt(o_[0:n], o_[0:n], bc4(sm2[0:n, 0:4], n), ALU.mult)
                    k.tt(ydn[0:n], o_[0:n], V(vecs.ap[0:n, V_DNN:V_DNN + 128].unsqueeze(1).to_broadcast([n, 4, 128]), vecs.cells), ALU.mult)
                    pt = PSB(PT)
                    k.pe([trf(pt[:, h * 128: h * 128 + n], ydn[0:n, h, :], identb[0:n, 0:n]) for h in range(4)], [pt], [ydn, identb])
                    k.tt(SL(SYD, c0, n, nslots=4), V(pt.ap[:, 0:512].rearrange("p (h w) -> p h w", w=128)[:, :, 0:n], pt.cells),
                         SL(SYD, c0, n, nslots=4), ALU.mult)
                    yield

                def run_interleaved(gens, strides=None):
                    items = [(g, (strides[i] if strides else 1)) for i, g in enumerate(gens) if g is not None]
                    r = 0
                    while items:
                        for it_ in list(items):
                            g, st = it_
                            if r % st:
                                continue
                            try:
                                next(g)
                            except StopIteration:
                                items.remove(it_)
                        r += 1

                k.memset(S_, 0.0)
                k.memset(Sb, 0.0)
                NCH = TP // 128

                def pfront(c):
                    return front(128, c * 128, gb[:, c, 0:4], gb[:, c, 4:8], c) if c < NCH else None

                def psolve(c, hh):
                    return solve(128, c, hh) if c < NCH else None

                run_interleaved([pfront(0)])
                run_interleaved([pfront(1), psolve(0, 0), psolve(0, 1)])
                for c in range(NCH):
                    run_interleaved([pfront(c + 2), psolve(c + 1, 0), psolve(c + 1, 1), seq(128, c * 128, gb[:, c, 4:8], rq[:, c, :], c)], [2, 1, 1, 1])
                k.dma("sp", V(sp_o.ap.rearrange("(h k) v -> k h v", k=128), []), S_, is_output=True)
                def sfront(s_i):
                    return front(LS, TP + s_i * LS, gbs[:, s_i, 0:4], gbs[:, s_i, 4:8], s_i) if s_i < NS else None

                def ssolve(s_i, hh):
                    return solve(LS, s_i, hh) if s_i < NS else None

                Sd, Sbd = [S_, S2_], [Sb, Sb2]

                def sload(s_i):
                    if s_i < NS:
                        k.dma("sp", Sd[s_i % 2], V(sdn.ap[s_i * 512:(s_i + 1) * 512, :].rearrange("(h k) v -> k h v", k=128), []))

                def sseq(s_i):
                    Sa, Sba = Sd[s_i % 2], Sbd[s_i % 2]
                    sload(s_i + 1)
                    k.copy(Sba, Sa, e="act")
                    yield
                    yield from seq(LS, TP + s_i * LS, gbs[:, s_i, 4:8], rqs[:, s_i, :], s_i, Sa, Sba, need_sb=False)
                    k.dma("sp", V(ss_o.ap[s_i * 512:(s_i + 1) * 512, :].rearrange("(h k) v -> k h v", k=128), []), Sa, is_output=True)
                    yield

                sload(0)

                run_interleaved([sfront(0)])
                run_interleaved([sfront(1), ssolve(0, 0), ssolve(0, 1)])
                for s_i in range(NS):
                    run_interleaved([sfront(s_i + 2), ssolve(s_i + 1, 0), ssolve(s_i + 1, 1), sseq(s_i)], [2, 1, 1, 1])
                k.barrier()

            SMG = 8

            def _ph5(p1):
                sg = [k.sb(p1, [128, 512], F32, "sg") for _ in range(2)]
                tm = [k.sb(p1, [128, 512], F32, "tm") for _ in range(2)]
                accf = [k.sb(p1, [128, T], F32, "accf") for _ in range(2)]
                it = 0
                ysl = [SYA, SYD, SYM]
                for mp in range(4):
                    for b3 in range(3):
                        wg, wb = w_next()
                        for mm in range(2):
                            m = mp * 2 + mm
                            wo_ = mm * 128
                            for bi, (c0, n) in enumerate(BLKS):
                                a_ = accf[mm][:, c0:c0 + n]
                                s_ = sg[it % 2]
                                t_ = tm[it % 2]
                                it += 1
                                bg = pm_next()
                                proj_fm(bg, wg, wo_, XN, 8, c0, n)
                                k.act(s_[:, 0:n], PS(bg, w=n), AF.Sigmoid)
                                bp = pm_next()
                                k.pe([mmf(PS(bp, w=n), wb[:, kk, wo_:wo_ + 128], SL(ysl[b3] + kk, c0, n), kk == 0, kk == 3) for kk in range(4)],
                                     [PS(bp, w=n)], [wb] + [SL(ysl[b3] + kk, c0, n) for kk in range(4)])
                                if b3 == 0:
                                    k.tt(a_, s_[:, 0:n], PS(bp, w=n), ALU.mult)
                                elif b3 == 1:
                                    k.tt(t_[:, 0:n], s_[:, 0:n], PS(bp, w=n), ALU.mult)
                                    k.tt(a_, a_, t_[:, 0:n], ALU.add)
                                else:
                                    k.tt(t_[:, 0:n], s_[:, 0:n], PS(bp, w=n), ALU.mult)
                                    k.tt(SL(SMG + m, c0, n), a_, t_[:, 0:n], ALU.add)
                for nm_, sl_ in (("ya", SYA), ("yd", SYD), ("ym", SYM), ("mg", SMG)):
                    tap(nm_ + "0", SL(sl_, 0, T))
                    tap(nm_ + "3", SL(sl_ + 3, 0, T))
                k.barrier()

            SX = 16
            def _ph6(p2):
                x1s = k.sb(p2, [128, D], F32, "x1s")
                xs = [k.sb(p2, [128, D], BF16, "xs2") for _ in range(2)]
                junk = k.sb(p2, [128, D], BF16, "junk2")
                ssq = [k.sb(p2, [128, 1], F32, "ssq2") for _ in range(2)]
                sgl = [k.sb(p2, [128, 512], F32, "sgl") for _ in range(2)]
                yt = [k.sb(p2, [128, D], F32, "yt") for _ in range(2)]
                gfin = k.sb(p2, [128, D], F32, "gfin")
                k.dma("sp", gfin, gfin_d)
                X1BASE = SX * (SLOTW // 2)

                def X1(tt_, n, c0=0, w=D):
                    if tt_ < 16:
                        a0 = X1BASE + tt_ * D + c0
                        return V(arena32[0:n, a0:a0 + w], cells_bytes(SX, (tt_ * D + c0) * 4, (tt_ * D + c0 + w) * 4))
                    return x1s[0:n, c0:c0 + w]

                (woA,) = w_next()
                (woB,) = w_next(prefetch=False)
                wo2 = [woA, woB]
                def p2tile(tt_):
                    n = 128 if tt_ < 16 else 64
                    t0 = tt_ * 128
                    for half in range(2):
                        b = (PM[half] if tt_ % 2 == 0 else PD[half])
                        k.pe([mmf(PS(b, n=n), SL(SMG + kk, t0, n), wo2[half][:, kk, :], kk == 0, kk == 7) for kk in range(8)],
                             [PS(b)], [wo2[half]] + [SL(SMG + kk, t0, n) for kk in range(8)])
                        xv = X1(tt_, n, half * 512, 512)
                        k.tt(xv, xv, PS(b, n=n), ALU.add)
                        yield
                    q_, s_ = ssq[tt_ % 2], xs[tt_ % 2]
                    k.act(junk[0:n, :], X1(tt_, n), AF.Square, accum=q_[0:n, :])
                    k.act(q_[0:n, :], q_[0:n, :], AF.Sqrt, scale=1.0 / D, bias=EPS)
                    yield
                    k.recip(q_[0:n, :], q_[0:n, :])
                    k.ts(s_[0:n, :], X1(tt_, n), q_[0:n, 0:1], ALU.mult)
                    yield
                    pt = PSB(PT if tt_ % 2 == 0 else PD[2])
                    k.pe([trf(pt[:, kk * 128: kk * 128 + n], s_[0:n, kk * 128:(kk + 1) * 128], identb[0:n, 0:n]) for kk in range(8)],
                         [pt], [s_, identb])
                    ptv = V(pt.ap.rearrange("p (k t) -> p k t", t=128)[:, :, 0:n], pt.cells)
                    k.tt(SL(XN, t0, n, nslots=8), ptv, vecs[:, V_GFFN:V_GFFN + 8].us(2).bc([128, 8, n]), ALU.mult)
                    yield

                for tt_ in range(17):
                    n_ = 128 if tt_ < 16 else 64
                    k.dma("sp", X1(tt_, n_), V(xin.ap[tt_ * 128: tt_ * 128 + n_, :], []))
                run_window([p2tile(tt_) for tt_ in range(17)], 2)
                w_issue(wstate["used"] + 1)
                SACT = 8
                it = 0
                for (j0, nj) in FFN_PARTS:
                    for jj in range(nj):
                        if jj % 2 == 0:
                            wg, wu = w_next()
                        wo_ = (jj % 2) * 128
                        for bi, (c0, n) in enumerate(BLKS):
                            bg = pm_next()
                            proj_fm(bg, wg, wo_, XN, 8, c0, n)
                            s_ = sgl[it % 2]
                            it += 1
                            k.act(s_[:, 0:n], PS(bg, w=n), AF.Silu)
                            bu = pm_next()
                            proj_fm(bu, wu, wo_, XN, 8, c0, n)
                            k.tt(SL(SACT + jj, c0, n), s_[:, 0:n], PS(bu, w=n), ALU.mult)
                    for half in range(2):
                        (wd,) = w_next()
                        for tt_ in range(17):
                            n = 128 if tt_ < 16 else 64
                            t0 = tt_ * 128
                            b = pm_next()
                            k.pe([mmf(PS(b, n=n), SL(SACT + kk, t0, n), wd[:, kk, :], kk == 0, kk == nj - 1) for kk in range(nj)],
                                 [PS(b)], [wd] + [SL(SACT + kk, t0, n) for kk in range(nj)])
                            xv = X1(tt_, n, half * 512, 512)
                            k.tt(xv, xv, PS(b, n=n), ALU.add)
                def p4tile(tt_):
                    n = 128 if tt_ < 16 else 64
                    q_, y_ = ssq[tt_ % 2], yt[tt_ % 2]
                    k.act(junk[0:n, :], X1(tt_, n), AF.Square, accum=q_[0:n, :])
                    k.act(q_[0:n, :], q_[0:n, :], AF.Sqrt, scale=1.0 / D, bias=EPS)
                    yield
                    k.recip(q_[0:n, :], q_[0:n, :])
                    k.stt(y_[0:n, :], X1(tt_, n), q_[0:n, 0:1], gfin[0:n, :], ALU.mult, ALU.mult)
                    yield
                    k.dma("sp", V(y_o.ap[tt_ * 128: tt_ * 128 + n, :], []), y_[0:n, :], is_output=True)
                    yield

                run_window([p4tile(tt_) for tt_ in range(17)], 2)

            for _i, _ph in enumerate([_ph0, _ph1, _ph2, _ph3, _ph4, _ph5, _ph6]):
                with contextlib.ExitStack() as _pes:
                    _ph(_pes)
                PHASE_MARKS.append((_i, k.npe, dict(k.cnt)))
                if STOP == _i + 1:
                    break
            else:
                assert wstate["used"] == len(jobs), (wstate, len(jobs))
        except _Stop:
            pass
        for ev in k.out_events:
            k.need("sp", ev)
        k.barrier(engines=("sp",))
    return nc


_NC = None


def _consts():
    c = np.zeros((128, NCST), np.float32)
    i = np.arange(128)
    c[:, K_ID:K_ID + 128] = np.eye(128)
    c[:, K_ONE:K_ONE + 128] = 1.0
    c[:, K_ML:K_ML + 128] = (i[:, None] <= i[None, :])
    c[:, K_MG:K_MG + 128] = (i[:, None] > i[None, :])
    c[:, K_MUS:K_MUS + 128] = (i[:, None] < i[None, :])
    for l in range(7):
        m = ((i[:, None] >> (l + 1)) == (i[None, :] >> (l + 1))) & (((i[:, None] >> l) & 1) == 0) & (((i[None, :] >> l) & 1) == 1)
        c[:, K_LV + l * 128: K_LV + (l + 1) * 128] = -m.astype(np.float32)
        c[:, K_LV + (7 + l) * 128: K_LV + (8 + l) * 128] = -m.T.astype(np.float32)
    return c


def kernel(x_prompt, x_sample, mem_prompt, state_conv_a, state_dn_conv, state_dn, cache_mem_k,
           cache_mem_v, norm_mix, w_in, conv_a_w, dn_conv_w, dn_a_log, dn_dt_bias, dn_norm,
           norm_mem, w_mem_kv, w_branch, w_o, norm_ffn, w_ffn_up, w_ffn_down, norm_final):
    global _NC
    f = lambda a: np.ascontiguousarray(np.asarray(a, dtype=np.float32))
    x_prompt, x_sample, mem_prompt = f(x_prompt), f(x_sample), f(mem_prompt)
    if _NC is None:
        _NC = build_nc()
    vec = np.zeros((128, NVEC), np.float32)
    vec[:, V_GMIX:V_GMIX + 8] = f(norm_mix)[0].reshape(8, 128).T
    vec[:, V_GMEM:V_GMEM + 8] = f(norm_mem)[0].reshape(8, 128).T
    vec[:, V_GFFN:V_GFFN + 8] = f(norm_ffn)[0].reshape(8, 128).T
    vec[:, V_CAW:V_CAW + 12] = f(conv_a_w)[0].reshape(3, 4, 128).transpose(2, 0, 1).reshape(128, 12)
    vec[:, V_DCW:V_DCW + 48] = f(dn_conv_w)[0].reshape(4, 12, 128).transpose(2, 0, 1).reshape(128, 48)
    vec[:, V_ALOG:V_ALOG + 4] = f(dn_a_log)[0][None, :]
    vec[:, V_DTB:V_DTB + 4] = f(dn_dt_bias)[0][None, :]
    vec[:, V_DNN:V_DNN + 128] = f(dn_norm)[0][None, :]
    cst = _consts()
    shared = dict(w_in=f(w_in)[0], w_kv=f(w_mem_kv)[0], w_br=f(w_branch)[0], w_o=f(w_o)[0],
                  w_up=f(w_ffn_up)[0], w_dn=f(w_ffn_down)[0], vecs=vec, cst=cst,
                  gfin=np.ascontiguousarray(np.broadcast_to(f(norm_final)[None, :], (128, D))))
    sca, sdc, sdn = f(state_conv_a)[0], f(state_dn_conv)[0], f(state_dn)[0]
    ck, cv = f(cache_mem_k)[0], f(cache_mem_v)[0]
    in_maps = []
    for c in range(NCORES):
        s0, s1 = c * NS, (c + 1) * NS
        m = dict(shared)
        m["xin"] = np.ascontiguousarray(np.concatenate([x_prompt[c], x_sample[s0:s1].reshape(NS * LS, D)], axis=0))
        m["mem"] = mem_prompt[c]
        m["sca"] = np.ascontiguousarray(sca[s0:s1].reshape(NS * 2, 512))
        m["sdc"] = np.ascontiguousarray(sdc[s0:s1].reshape(NS * 3, 1536))
        m["sdn"] = np.ascontiguousarray(sdn[s0:s1].reshape(NS * 512, 128))
        m["ckv"] = np.ascontiguousarray(np.concatenate([ck[s0:s1].reshape(NS * 256, 512), cv[s0:s1].reshape(NS * 256, 512)], axis=1))
        in_maps.append(m)
    if DBG_CORES:
        res = run_bass_kernel_spmd(_NC, in_maps[:DBG_CORES], core_ids=list(range(DBG_CORES)))
        R = list(res.results) + [res.results[0]] * (NCORES - DBG_CORES)
        global LAST_R
        LAST_R = res.results[0]
    else:
        res = run_bass_kernel_spmd(_NC, in_maps, core_ids=list(range(NCORES)))
        R = res.results
    y_prompt = np.stack([R[c]["y"][:TP] for c in range(NCORES)])
    y_sample = np.concatenate([R[c]["y"][TP:].reshape(NS, LS, D) for c in range(NCORES)], axis=0)
    ca_p = np.stack([R[c]["ca_p"] for c in range(NCORES)])[None]
    dc_p = np.stack([R[c]["dc_p"] for c in range(NCORES)])[None]
    s_p = np.stack([R[c]["s_p"].reshape(4, 128, 128) for c in range(NCORES)])[None]
    mk_p = np.stack([R[c]["mk"].reshape(256, 4, 128) for c in range(NCORES)])[None]
    mv_p = np.stack([R[c]["mv"].reshape(256, 4, 128) for c in range(NCORES)])[None]
    ca_s = np.concatenate([R[c]["ca_s"].reshape(NS, 2, 512) for c in range(NCORES)], axis=0)[None]
    dc_s = np.concatenate([R[c]["dc_s"].reshape(NS, 3, 1536) for c in range(NCORES)], axis=0)[None]
    s_s = np.concatenate([R[c]["s_s"].reshape(NS, 4, 128, 128) for c in range(NCORES)], axis=0)[None]
    return tuple(np.ascontiguousarray(a, dtype=np.float32) for a in
                 (y_prompt, y_sample, ca_p, dc_p, s_p, mk_p, mv_p, ca_s, dc_s, s_s))
```

```python
import contextlib
import numpy as np
import concourse.bass as bass
import concourse.mybir as mybir
from concourse.bass_utils import run_bass_kernel_spmd

F32 = mybir.dt.float32
BF16 = mybir.dt.bfloat16
F32R = mybir.dt.float32r
AF = mybir.ActivationFunctionType
ALU = mybir.AluOpType
AX = mybir.AxisListType

NCORES = 8
D = 1024
TP = 2048
NS = 16
LS = 4
T = TP + NS * LS
SLOTW = T
NSLOT = 32
BLKS = [(0, 512), (512, 512), (1024, 512), (1536, 512), (2048, 64)]
EPS = 1e-6
NRING = 32
WBUF = 6144
INW = 7176
C_B, C_C, C_H, C_Q, C_K, C_V, C_Z, C_AB, C_XQ, C_GA = 0, 512, 1024, 1536, 2048, 2560, 3072, 3584, 3592, 4104
V_GMIX, V_GMEM, V_GFFN, V_CAW, V_DCW, V_ALOG, V_DTB, V_DNN = 0, 8, 16, 24, 36, 84, 88, 92
NVEC = 220
K_ID, K_ONE, K_ML, K_MG, K_MUS, K_LV = 0, 128, 256, 384, 512, 640
NCST = 640 + 14 * 128


class Cell:
    __slots__ = ("w", "r", "x")

    def __init__(self, x=False):
        self.w = None
        self.r = {}
        self.x = x


class V:
    def __init__(self, ap, cells):
        self.ap = ap
        self.cells = cells

    def __getitem__(self, idx):
        return V(self.ap[idx], self.cells)

    def bc(self, shape):
        return V(self.ap.to_broadcast(shape), self.cells)

    def us(self, ax):
        return V(self.ap.unsqueeze(ax), self.cells)


class KB:
    def __init__(self, nc, es):
        self.nc = nc
        self.es = es
        self.eng = dict(pe=nc.tensor, act=nc.scalar, dve=nc.vector, pool=nc.gpsimd, sp=nc.sync)
        self.sem = {e: es.enter_context(nc.semaphore("s_" + e)) for e in ("pe", "act", "dve", "pool")}
        self.cnt = dict.fromkeys(self.sem, 0)
        self.known = {e: {} for e in self.eng}
        self.ring = {q: [es.enter_context(nc.semaphore("r_%s%d" % (q, i))) for i in range(NRING)] for q in ("sp", "pool")}
        self.ring_val = {q: [0] * NRING for q in self.ring}
        self.ring_pos = {q: 0 for q in self.ring}
        self.out_events = []
        self.npe = 0
        self.nalloc = 0

    def sb(self, es, shape, dt, name=None):
        self.nalloc += 1
        t = es.enter_context(self.nc.sbuf_tensor("%s_%d" % (name or "t", self.nalloc), list(shape), dt))
        return V(t[:], [Cell()])

    def need(self, e, ev):
        if ev is None:
            return
        key, sem, val = ev
        k = self.known[e]
        if k.get(key, 0) >= val:
            return
        self.eng[e].wait_ge(sem, val)
        k[key] = val

    def deps(self, e, outs, ins):
        for v in ins:
            for c in v.cells:
                self.need(e, c.w)
                if c.x:
                    for ev in c.r.values():
                        if ev[0] != e:
                            self.need(e, ev)
        for v in outs:
            for c in v.cells:
                if c.w is not None and not (c.w[0] == e and e in ("pe", "act", "dve")):
                    self.need(e, c.w)
                for ev in c.r.values():
                    if not (ev[0] == e and e in ("pe", "act", "dve")):
                        self.need(e, ev)

    def commit(self, ev, outs, ins):
        for v in ins:
            for c in v.cells:
                c.r[ev[0]] = ev
        for v in outs:
            for c in v.cells:
                c.w = ev
                c.r = {}

    def op(self, e, fn, outs, ins):
        self.deps(e, outs, ins)
        i = fn(self.eng[e])
        self.cnt[e] += 1
        i.then_inc(self.sem[e], 1)
        self.commit((e, self.sem[e], self.cnt[e]), outs, ins)

    def pe(self, fns, outs, ins):
        self.deps("pe", outs, ins)
        i = None
        for fn in fns:
            i = fn(self.nc.tensor)
            self.npe += 1
        self.cnt["pe"] += 1
        i.then_inc(self.sem["pe"], 1)
        self.commit(("pe", self.sem["pe"], self.cnt["pe"]), outs, ins)

    def dma(self, q, out, in_, is_output=False, **kw):
        self.deps(q, [out], [in_])
        i = self.ring_pos[q]
        self.ring_pos[q] = (i + 1) % NRING
        sem = self.ring[q][i]
        key = "r_%s%d" % (q, i)
        prev = self.ring_val[q][i]
        if prev:
            self.need(q, (key, sem, prev))
        val = prev + 16
        self.eng[q].dma_start(out=out.ap, in_=in_.ap, **kw).then_inc(sem, 16)
        self.ring_val[q][i] = val
        ev = (key, sem, val)
        self.commit(ev, [out], [in_])
        if is_output:
            self.out_events.append(ev)
        return ev

    def barrier(self, engines=("pe", "act", "dve", "sp", "pool")):
        evs = [(e, self.sem[e], self.cnt[e]) for e in ("pe", "act", "dve") if self.cnt[e]]
        for q in ("sp", "pool"):
            for i in range(NRING):
                if self.ring_val[q][i]:
                    evs.append(("r_%s%d" % (q, i), self.ring[q][i], self.ring_val[q][i]))
        for e in engines:
            for ev in evs:
                if ev[0] != e:
                    self.need(e, ev)

    def act(self, out, in_, func, scale=1.0, bias=0.0, accum=None, extra_in=()):
        kw = {}
        ins = [in_] + list(extra_in)
        if isinstance(scale, V):
            ins.append(scale)
            scale = scale.ap
        if isinstance(bias, V):
            ins.append(bias)
            bias = bias.ap
        outs = [out]
        if accum is not None:
            outs.append(accum)
            kw["accum_out"] = accum.ap
        self.op("act", lambda e: e.activation(out=out.ap, in_=in_.ap, func=func, scale=scale, bias=bias, **kw), outs, ins)

    def tt(self, out, a, b, op, e="dve"):
        self.op(e, lambda en: en.tensor_tensor(out=out.ap, in0=a.ap, in1=b.ap, op=op), [out], [a, b])

    def ts(self, out, a, s1, op0, s2=None, op1=None, e="dve"):
        ins = [a]
        if isinstance(s1, V):
            ins.append(s1)
            s1 = s1.ap
        if isinstance(s2, V):
            ins.append(s2)
            s2 = s2.ap
        kw = {}
        if op1 is not None:
            kw["op1"] = op1
        self.op(e, lambda en: en.tensor_scalar(out=out.ap, in0=a.ap, scalar1=s1, scalar2=s2, op0=op0, **kw), [out], ins)

    def stt(self, out, a, s, b, op0, op1):
        ins = [a, b]
        if isinstance(s, V):
            ins.append(s)
            s = s.ap
        self.op("dve", lambda en: en.scalar_tensor_tensor(out=out.ap, in0=a.ap, scalar=s, in1=b.ap, op0=op0, op1=op1), [out], ins)

    def copy(self, out, in_, e="dve"):
        if e == "act":
            self.op("act", lambda en: en.copy(out=out.ap, in_=in_.ap), [out], [in_])
        else:
            self.op(e, lambda en: en.tensor_copy(out=out.ap, in_=in_.ap), [out], [in_])

    def recip(self, out, in_):
        self.op("dve", lambda en: en.reciprocal(out=out.ap, in_=in_.ap), [out], [in_])

    def memset(self, out, val, e="dve"):
        self.op(e, lambda en: en.memset(out.ap, val), [out], [])

    def rsum(self, out, in_):
        self.op("dve", lambda en: en.reduce_sum(out=out.ap, in_=in_.ap, axis=AX.X), [out], [in_])


def run_window(gens, width):
    pending = list(gens)
    active = []
    while pending or active:
        while pending and len(active) < width:
            active.append(pending.pop(0))
        for g in list(active):
            try:
                next(g)
            except StopIteration:
                active.remove(g)


def mmf(out, lhsT, rhs, start, stop):
    return lambda pe: pe.matmul(out.ap, lhsT=lhsT.ap, rhs=rhs.ap, start=start, stop=stop)


def trf(out, in_, ident):
    return lambda pe: pe.transpose(out=out.ap, in_=in_.ap, identity=ident.ap)


class _Stop(Exception):
    pass


STOP = None
SUB = None
DBG_CORES = None
DBG_TAPS = False
LAST_R = None
PHASE_MARKS = []


def build_nc():
    nc = bass.Bass("TRN2", target_bir_lowering=False)

    def din(name, shape):
        return V(nc.dram_tensor(name, list(shape), F32, kind="ExternalInput").ap(), [])

    def dout(name, shape):
        return V(nc.dram_tensor(name, list(shape), F32, kind="ExternalOutput").ap(), [])

    xin = din("xin", [T, D])
    mem = din("mem", [256, D])
    sca = din("sca", [NS * 2, 512])
    sdc = din("sdc", [NS * 3, 1536])
    sdn = din("sdn", [NS * 4 * 128, 128])
    ckv = din("ckv", [NS * 256, 1024])
    w_in = din("w_in", [D, INW])
    w_kv = din("w_kv", [D, 1024])
    w_br = din("w_br", [1536, D])
    w_o = din("w_o", [D, D])
    w_up = din("w_up", [D, 5632])
    w_dn = din("w_dn", [2816, D])
    vecs_d = din("vecs", [128, NVEC])
    cst_d = din("cst", [128, NCST])
    gfin_d = din("gfin", [128, D])
    y_o = dout("y", [T, D])
    cap_o = dout("ca_p", [2, 512])
    dcp_o = dout("dc_p", [3, 1536])
    sp_o = dout("s_p", [512, 128])
    mk_o = dout("mk", [256, 512])
    mv_o = dout("mv", [256, 512])
    cas_o = dout("ca_s", [NS * 2, 512])
    dcs_o = dout("dc_s", [NS * 3, 1536])
    ss_o = dout("s_s", [NS * 512, 128])

    with contextlib.ExitStack() as es:
        k = KB(nc, es)
        arena_t = es.enter_context(nc.sbuf_tensor("arena", [128, NSLOT * SLOTW], BF16))
        arena = arena_t[:]
        arena3 = arena.rearrange("p (s t) -> p s t", t=SLOTW)
        arena32 = arena.bitcast(F32)
        acells = [[Cell() for _ in range(5)] for _ in range(NSLOT)]

        def cells_bytes(slot, b0, b1):
            out = []
            while b0 < b1:
                s = slot + b0 // (SLOTW * 2)
                bb = b0 % (SLOTW * 2)
                ci = min(bb // 1024, 4)
                out.append(acells[s][ci])
                nxt = (bb // 1024 + 1) * 1024 if ci < 4 else SLOTW * 2
                b0 += nxt - bb
            return out

        def SL(slot, c0, n, nslots=1):
            cs = []
            for s in range(slot, slot + nslots):
                cs += cells_bytes(s, c0 * 2, (c0 + n) * 2)
            if nslots == 1:
                return V(arena3[:, slot, c0:c0 + n], cs)
            return V(arena3[:, slot:slot + nslots, c0:c0 + n], cs)

        def SL32(slot, c0, n):
            base = slot * (SLOTW // 2)
            return V(arena32[:, base + c0: base + c0 + n], cells_bytes(slot, c0 * 4, (c0 + n) * 4))

        ps_t = es.enter_context(nc.psum_tensor("ps", [128, 8, 512], F32))
        psap = ps_t[:]
        pcells = [Cell(True) for _ in range(8)]

        def PS(bank, n=128, w=512, c0=0):
            return V(psap[0:n, bank, c0:c0 + w], [pcells[bank]])

        def PSB(bank, n=128):
            return V(psap[0:n, bank, :].bitcast(BF16), [pcells[bank]])

        def PS2(bank, n=128):
            return V(psap[0:n, bank:bank + 2, :], [pcells[bank], pcells[bank + 1]])

        PM = [0, 1, 2]
        PT = 3
        PD = [4, 5, 6, 7]

        ring = [k.sb(es, [128, WBUF], BF16, "wring") for _ in range(2)]
        vecs = k.sb(es, [128, NVEC], F32, "vecs")
        cst = k.sb(es, [128, NCST], F32, "cst")
        cstb = k.sb(es, [128, 256], BF16, "cstb")
        gb = k.sb(es, [128, 17, 8], F32, "gb")
        gbs = k.sb(es, [LS, NS, 8], F32, "gbs")
        rq = k.sb(es, [128, 17, 4], F32, "rq")
        rqs = k.sb(es, [LS, NS, 4], F32, "rqs")
        negA = k.sb(es, [128, 4], F32, "negA")
        KT = k.sb(es, [128, 4, 256], BF16, "KT")
        Vtok = k.sb(es, [128, 2, 512], BF16, "Vtok")

        k.dma("sp", vecs, vecs_d)
        k.dma("sp", cst, cst_d)
        k.copy(cstb, cst[:, 0:256])
        identf = cst[:, K_ID:K_ID + 128]
        onesf = cst[:, K_ONE:K_ONE + 128]
        ML = cst[:, K_ML:K_ML + 128]
        MG = cst[:, K_MG:K_MG + 128]
        MUS = cst[:, K_MUS:K_MUS + 128]
        identb = cstb[:, 0:128]
        onesb = cstb[:, 128:256]

        def LV(l):
            return cst[:, K_LV + l * 128: K_LV + (l + 1) * 128]

        def LVT(l):
            return cst[:, K_LV + (7 + l) * 128: K_LV + (8 + l) * 128]

        k.act(negA, vecs[:, V_ALOG:V_ALOG + 4], AF.Exp)
        k.ts(negA, negA, -1.0, ALU.mult)

        jobs = []

        def wsrc(w, r0, nk, c0, nc_):
            return V(w.ap[r0:r0 + nk * 128, c0:c0 + nc_].rearrange("(kc p) n -> p kc n", p=128), [])

        jobs.append([(wsrc(w_kv, 0, 8, 0, 512), 8, 512)])
        jobs.append([(wsrc(w_kv, 0, 8, 512, 512), 8, 512)])
        for mp in range(2):
            jobs.append([(wsrc(w_in, 0, 8, C_H + mp * 256, 256), 8, 256),
                         (wsrc(w_in, 0, 8, C_C + mp * 256, 256), 8, 256),
                         (wsrc(w_in, 0, 8, C_B + mp * 256, 256), 8, 256)])
        for j in range(3):
            jobs.append([(wsrc(w_in, 0, 8, C_Q + j * 512, 512), 8, 512)])
        jobs.append([(wsrc(w_in, 0, 8, C_XQ, 512), 8, 512)])
        jobs.append([(wsrc(w_in, 0, 8, C_Z, 520), 8, 520)])
        for mp in range(4):
            for b in range(3):
                jobs.append([(wsrc(w_in, 0, 8, C_GA + b * 1024 + mp * 256, 256), 8, 256),
                             (wsrc(w_br, b * 512, 4, mp * 256, 256), 4, 256)])
        for j in range(2):
            jobs.append([(wsrc(w_o, 0, 8, j * 512, 512), 8, 512)])
        FFN_PARTS = [(0, 8), (8, 8), (16, 6)]
        for (j0, nj) in FFN_PARTS:
            for j in range(j0, j0 + nj, 2):
                jobs.append([(wsrc(w_up, 0, 8, j * 128, 256), 8, 256),
                             (wsrc(w_up, 0, 8, 2816 + j * 128, 256), 8, 256)])
            for half in range(2):
                jobs.append([(wsrc(w_dn, j0 * 128, nj, half * 512, 512), nj, 512)])
        wstate = dict(issued=0, used=0)

        def w_issue(upto):
            while wstate["issued"] < min(upto, len(jobs)):
                ji = wstate["issued"]
                buf = ring[ji % 2]
                off = 0
                for (src, nk, ncol) in jobs[ji]:
                    dst = V(buf.ap[:, off:off + nk * ncol].rearrange("p (k n) -> p k n", n=ncol), buf.cells)
                    k.dma("pool", dst, src)
                    off += nk * ncol
                wstate["issued"] += 1

        def w_next(prefetch=True):
            ji = wstate["used"]
            w_issue(ji + (2 if prefetch else 1))
            buf = ring[ji % 2]
            views = []
            off = 0
            for (src, nk, ncol) in jobs[ji]:
                views.append(V(buf.ap[:, off:off + nk * ncol].rearrange("p (k n) -> p k n", n=ncol), buf.cells))
                off += nk * ncol
            wstate["used"] += 1
            return views

        def w_prefetch():
            w_issue(wstate["used"] + 1)

        w_issue(2)

        def tap(name, v):
            if not DBG_TAPS:
                return
            shp = list(v.ap.shape)
            d_ = V(nc.dram_tensor("tap_" + name, shp, v.ap.dtype, kind="ExternalOutput").ap(), [])
            k.dma("sp", d_, v, is_output=True)

        pm_rr = [0]

        def pm_next():
            b = PM[pm_rr[0] % 3]
            pm_rr[0] += 1
            return b

        try:
            XN = 0
            def _ph0(p0):
                xt = [k.sb(p0, [128, D], F32, "xt") for _ in range(4)]
                xs = [k.sb(p0, [128, D], BF16, "xs") for _ in range(2)]
                junk = k.sb(p0, [128, D], BF16, "junk")
                ssq = [k.sb(p0, [128, 1], F32, "ssq") for _ in range(2)]
                rst = [k.sb(p0, [128, 1], F32, "rst") for _ in range(2)]
                memT = k.sb(p0, [128, 8, 256], BF16, "memT")
                mko = k.sb(p0, [128, 2, 1024], F32, "mko")

                def norm_T(src_d, r0, n, gcol, dst, i):
                    x_, s_, q_, r_ = xt[i % 4], xs[i % 2], ssq[i % 2], rst[i % 2]
                    k.act(junk[0:n, :], x_[0:n, :], AF.Square, accum=q_[0:n, :])
                    k.act(r_[0:n, :], q_[0:n, :], AF.Sqrt, scale=1.0 / D, bias=EPS)
                    yield
                    k.recip(r_[0:n, :], r_[0:n, :])
                    k.ts(s_[0:n, :], x_[0:n, :], r_[0:n, 0:1], ALU.mult)
                    yield
                    pt = PSB(PT if i % 2 == 0 else PD[0])
                    k.pe([trf(pt[:, kk * 128: kk * 128 + n], s_[0:n, kk * 128:(kk + 1) * 128], identb[0:n, 0:n]) for kk in range(8)],
                         [pt], [s_, identb])
                    ptv = V(pt.ap.rearrange("p (k t) -> p k t", t=128)[:, :, 0:n], pt.cells)
                    k.tt(dst, ptv, vecs[:, gcol:gcol + 8].us(2).bc([128, 8, n]), ALU.mult)
                    xload(i + 4)
                    yield

                srcs0 = [(xin, tt_ * 128, 128 if tt_ < 16 else 64) for tt_ in range(17)] + [(mem, mc * 128, 128) for mc in range(2)]

                def xload(i):
                    if i < len(srcs0):
                        sd, r0, n = srcs0[i]
                        k.dma("sp", xt[i % 4][0:n, :], V(sd.ap[r0:r0 + n, :], []))

                for i_ in range(4):
                    xload(i_)
                if SUB and '0' in SUB:
                    return
                gens0 = []
                for tt_ in range(17):
                    n = 128 if tt_ < 16 else 64
                    gens0.append(norm_T(xin, tt_ * 128, n, V_GMIX, SL(XN, tt_ * 128, n, nslots=8), tt_))
                if SUB and 'A' in SUB:
                    return
                for mc in range(2):
                    gens0.append(norm_T(mem, mc * 128, 128, V_GMEM, memT[:, :, mc * 128:(mc + 1) * 128], 17 + mc))
                run_window(gens0, 2)
                if SUB and 'B' in SUB:
                    return

                for part in range(2):
                    (wv,) = w_next()
                    if SUB and 'F' in SUB and part == 1:
                        return
                    for mc in range(2):
                        b = pm_next()
                        k.pe([mmf(PS(b), memT[:, kk, mc * 128:(mc + 1) * 128], wv[:, kk, :], kk == 0, kk == 7) for kk in range(8)],
                             [PS(b)], [memT, wv])
                        if not (SUB and 'H' in SUB and part == 1):
                            k.copy(mko[:, mc, part * 512:(part + 1) * 512], PS(b), e=("dve" if (SUB and 'I' in SUB) else "act"))
                        if part == 1 and not (SUB and 'G' in SUB):
                            k.copy(Vtok[:, mc, :], mko[:, mc, 512:1024])
                    if SUB and 'C' in SUB:
                        return
                    if part == 0:
                        for hp in range(2):
                            b = pm_next()
                            fns = []
                            for hh in range(2):
                                h = hp * 2 + hh
                                fns += [mmf(PS(b, w=256, c0=hh * 256), wv[:, kk, h * 128:(h + 1) * 128], memT[:, kk, :], kk == 0, kk == 7) for kk in range(8)]
                            k.pe(fns, [PS(b)], [memT, wv])
                            k.copy(V(KT.ap[:, hp * 2:hp * 2 + 2, :], KT.cells), V(PS(b).ap.rearrange("p (h m) -> p h m", m=256), PS(b).cells))
                        if SUB and 'D' in SUB:
                            return
                if SUB and 'E' in SUB:
                    return
                for mc in range(2):
                    k.dma("sp", V(mk_o.ap[mc * 128:(mc + 1) * 128, :], []), mko[:, mc, 0:512], is_output=True)
                    k.dma("sp", V(mv_o.ap[mc * 128:(mc + 1) * 128, :], []), mko[:, mc, 512:1024], is_output=True)
                k.barrier()

            def proj_fm(b, wv, wc0, src_slot, nk, c0, n):
                k.pe([mmf(PS(b, w=n), wv[:, kk, wc0:wc0 + 128], SL(src_slot + kk, c0, n), kk == 0, kk == nk - 1) for kk in range(nk)],
                     [PS(b, w=n)], [wv] + [SL(src_slot + kk, c0, n) for kk in range(nk)])

            SQ, SK, SV_ = 8, 12, 16
            SYA, SYD, SYM = 20, 24, 28

            def _ph1(p1):
                ub = k.sb(p1, [128, 2 + TP], F32, "ub")
                ue = k.sb(p1, [128, NS, 6], F32, "ue")
                hb = [k.sb(p1, [128, 512], F32, "hb") for _ in range(2)]
                cvb = [k.sb(p1, [128, 512], F32, "cvb") for _ in range(2)]
                sca_sb = k.sb(p1, [NS * 2, 512], F32, "sca_sb")
                cap_sb = k.sb(p1, [2, 512], F32, "cap_sb")
                cas_sb = k.sb(p1, [NS * 2, 512], F32, "cas_sb")
                tl = k.sb(p1, [128, 32], F32, "tl")
                k.dma("sp", sca_sb, sca)
                k.memset(ub[:, 0:2], 0.0)
                for m in range(4):
                    if m % 2 == 0:
                        wh, wc, wb = w_next()
                    wo_ = (m % 2) * 128
                    k.pe([trf(PS(PT, w=32), sca_sb[:, m * 128:(m + 1) * 128], identf[0:32, 0:32])], [PS(PT)], [sca_sb, identf])
                    k.copy(ue[:, :, 0:2], V(PS(PT, w=32).ap.rearrange("p (s i) -> p s i", i=2), [pcells[PT]]))
                    for bi, (c0, n) in enumerate(BLKS):
                        bh = pm_next()
                        proj_fm(bh, wh, wo_, XN, 8, c0, n)
                        h_ = hb[bi % 2]
                        k.copy(h_[:, 0:n], PS(bh, w=n), e="act")
                        bc_ = pm_next()
                        proj_fm(bc_, wc, wo_, XN, 8, c0, n)
                        if bi < 4:
                            k.tt(ub[:, 2 + c0: 2 + c0 + n], PS(bc_, w=n), h_[:, 0:n], ALU.mult)
                        else:
                            k.tt(ue[:, :, 2:6], V(PS(bc_, w=n).ap.rearrange("p (s t) -> p s t", t=LS), [pcells[bc_]]),
                                 V(h_.ap[:, 0:n].rearrange("p (s t) -> p s t", t=LS), h_.cells), ALU.mult)
                    w0 = vecs[:, V_CAW + 0 * 4 + m: V_CAW + 0 * 4 + m + 1]
                    w1 = vecs[:, V_CAW + 1 * 4 + m: V_CAW + 1 * 4 + m + 1]
                    w2 = vecs[:, V_CAW + 2 * 4 + m: V_CAW + 2 * 4 + m + 1]
                    for bi, (c0, n) in enumerate(BLKS):
                        cv_ = cvb[bi % 2]
                        if bi < 4:
                            k.ts(cv_[:, 0:n], ub[:, c0 + 2: c0 + 2 + n], w2, ALU.mult)
                            k.stt(cv_[:, 0:n], ub[:, c0 + 1: c0 + 1 + n], w1, cv_[:, 0:n], ALU.mult, ALU.add)
                            k.stt(cv_[:, 0:n], ub[:, c0: c0 + n], w0, cv_[:, 0:n], ALU.mult, ALU.add)
                        else:
                            cv3 = V(cv_.ap[:, 0:n].rearrange("p (s t) -> p s t", t=LS), cv_.cells)
                            k.ts(cv3, ue[:, :, 2:6], w2, ALU.mult)
                            k.stt(cv3, ue[:, :, 1:5], w1, cv3, ALU.mult, ALU.add)
                            k.stt(cv3, ue[:, :, 0:4], w0, cv3, ALU.mult, ALU.add)
                        bb = pm_next()
                        proj_fm(bb, wb, wo_, XN, 8, c0, n)
                        k.tt(SL(SYA + m, c0, n), PS(bb, w=n), cv_[:, 0:n], ALU.mult)
                    k.pe([trf(PS(PT, n=2, w=128), ub[:, TP:TP + 2], identf)], [PS(PT)], [ub, identf])
                    k.copy(cap_sb[:, m * 128:(m + 1) * 128], PS(PT, n=2, w=128), e="act")
                    k.copy(V(tl.ap.rearrange("p (s i) -> p s i", i=2), tl.cells), ue[:, :, 4:6])
                    k.pe([trf(PS(PT, n=32, w=128), tl, identf)], [PS(PT)], [tl, identf])
                    k.copy(cas_sb[:, m * 128:(m + 1) * 128], PS(PT, n=32, w=128), e="act")
                k.dma("sp", cap_o, cap_sb, is_output=True)
                k.dma("sp", cas_o, cas_sb, is_output=True)
                k.barrier()

            def _ph2(p1):
                xeb = [k.sb(p1, [128, 3 + TP], BF16, "xe") for _ in range(2)]
                seb = [k.sb(p1, [128, NS, 7], BF16, "se") for _ in range(2)]
                dg = [k.sb(p1, [128, 4, 128], BF16, "dg") for _ in range(2)]
                ktf = k.sb(p1, [128, T], F32, "ktf")
                sqb = [k.sb(p1, [128, 512], BF16, "sqb") for _ in range(2)]
                rr = [k.sb(p1, [128, 512], F32, "rr") for _ in range(2)]
                sdc_sb = k.sb(p1, [NS * 3, 1536], F32, "sdc_sb")
                dcp_t = [k.sb(p1, [3, 128], F32, "dcp_t") for _ in range(2)]
                dcs_t = [k.sb(p1, [NS * 3, 128], F32, "dcs_t") for _ in range(2)]
                x3 = [k.sb(p1, [128, 3], F32, "x3") for _ in range(2)]
                tl = [k.sb(p1, [128, 48], F32, "tl") for _ in range(2)]
                k.dma("sp", sdc_sb, sdc)
                k.memset(xeb[0][:, 0:3], 0.0)
                k.memset(xeb[1][:, 0:3], 0.0)
                PA = [PM[0], PM[1]]
                PB = [PM[2], PD[0]]
                wq_of = {}

                def stageA(j):
                    if j % 4 == 0:
                        (wq_of[j // 4],) = w_next()
                    wq = wq_of[j // 4]
                    jc = (j % 4) * 128
                    xe, se, dg_ = xeb[j % 2], seb[j % 2], dg[j % 2]
                    for i in range(4):
                        k.act(dg_[:, i, :], identf, AF.Copy, scale=vecs[:, V_DCW + i * 12 + j: V_DCW + i * 12 + j + 1])
                    k.pe([trf(PS(PT, w=48), sdc_sb[:, j * 128:(j + 1) * 128], identf[0:48, 0:48])], [PS(PT)], [sdc_sb, identf])
                    k.copy(se[:, :, 0:3], V(PS(PT, w=48).ap.rearrange("p (s i) -> p s i", i=3), [pcells[PT]]))
                    yield
                    for bi, (c0, n) in enumerate(BLKS):
                        b = PA[bi % 2]
                        proj_fm(b, wq, jc, XN, 8, c0, n)
                        if bi < 4:
                            k.copy(xe[:, 3 + c0: 3 + c0 + n], PS(b, w=n), e=("dve" if bi % 2 else "act"))
                            if bi == 3:
                                k.copy(x3[j % 2], PS(b, w=3, c0=n - 3))
                        else:
                            pv = V(PS(b, w=n).ap.rearrange("p (s t) -> p s t", t=LS), [pcells[b]])
                            k.copy(se[:, :, 3:7], pv, e="act")
                            k.copy(V(tl[j % 2].ap.rearrange("p (s i) -> p s i", i=3), tl[j % 2].cells), pv[:, :, 1:4])
                        yield
                    k.pe([trf(PS(PT, n=3, w=128), x3[j % 2], identf)], [PS(PT)], [x3[j % 2], identf])
                    k.copy(dcp_t[j % 2], PS(PT, n=3, w=128), e="act")
                    k.dma("sp", V(dcp_o.ap[:, j * 128:(j + 1) * 128], []), dcp_t[j % 2], is_output=True)
                    k.pe([trf(PS(PT, n=48, w=128), tl[j % 2], identf)], [PS(PT)], [tl[j % 2], identf])
                    k.copy(dcs_t[j % 2], PS(PT, n=48, w=128), e="act")
                    k.dma("sp", V(dcs_o.ap[:, j * 128:(j + 1) * 128], []), dcs_t[j % 2], is_output=True)
                    yield

                def stageB(j):
                    kind = j // 4
                    dst_slot = SQ + j
                    xe, se, dg_ = xeb[j % 2], seb[j % 2], dg[j % 2]
                    for bi, (c0, n) in enumerate(BLKS):
                        b = PB[bi % 2]
                        if bi < 4:
                            k.pe([mmf(PS(b, w=n), dg_[:, i, :], xe[:, c0 + i: c0 + i + n], i == 0, i == 3) for i in range(4)],
                                 [PS(b, w=n)], [dg_, xe])
                        else:
                            k.pe([mmf(V(PS(b, w=n).ap.rearrange("p (s t) -> p s t", t=LS), [pcells[b]]), dg_[:, i, :], se[:, :, i:i + 4], i == 0, i == 3) for i in range(4)],
                                 [PS(b, w=n)], [dg_, se])
                        if kind == 1:
                            k.act(ktf[:, c0:c0 + n], PS(b, w=n), AF.Silu)
                        else:
                            k.act(SL(dst_slot, c0, n), PS(b, w=n), AF.Silu)
                        yield
                    for bi, (c0, n) in enumerate(BLKS):
                        sq_ = sqb[bi % 2]
                        if kind == 0:
                            k.tt(sq_[:, 0:n], SL(dst_slot, c0, n), SL(dst_slot, c0, n), ALU.mult)
                            h = j
                            if bi < 4:
                                fns = [mmf(PS(PD[1], w=1, c0=t4), sq_[:, t4 * 128:(t4 + 1) * 128], onesb[:, 0:1], True, True) for t4 in range(4)]
                                k.pe(fns, [PS(PD[1])], [sq_, onesb])
                                k.copy(rq[:, bi * 4:(bi + 1) * 4, h], PS(PD[1], w=4))
                            else:
                                fns = [mmf(PS(PD[1], n=LS, w=1, c0=s), sq_[:, s * LS:(s + 1) * LS], onesb[:, 0:1], True, True) for s in range(NS)]
                                k.pe(fns, [PS(PD[1])], [sq_, onesb])
                                k.copy(rqs[:, :, h], PS(PD[1], n=LS, w=NS))
                            yield
                        elif kind == 1:
                            k.tt(sq_[:, 0:n], ktf[:, c0:c0 + n], ktf[:, c0:c0 + n], ALU.mult)
                            k.pe([mmf(PS(PD[1], w=n), onesb, sq_[:, 0:n], True, True)], [PS(PD[1])], [sq_, onesb])
                            r_ = rr[bi % 2]
                            k.act(r_[:, 0:n], PS(PD[1], w=n), AF.Ln, bias=EPS)
                            k.act(r_[:, 0:n], r_[:, 0:n], AF.Exp, scale=-0.5)
                            k.tt(SL(dst_slot, c0, n), ktf[:, c0:c0 + n], r_[:, 0:n], ALU.mult)
                            yield

                def run_il(gens):
                    gens = [g for g in gens if g is not None]
                    while gens:
                        for g in list(gens):
                            try:
                                next(g)
                            except StopIteration:
                                gens.remove(g)

                run_il([stageA(0)])
                for j in range(12):
                    run_il([stageA(j + 1) if j + 1 < 12 else None, stageB(j)])
                for r_, in ((V(rq.ap[:, 0:16, :], rq.cells),), (rqs,)):
                    k.act(r_, r_, AF.Ln, bias=EPS)
                    k.act(r_, r_, AF.Exp, scale=-0.5)
                    k.ts(r_, r_, 128.0 ** -0.5, ALU.mult)
                k.barrier()

            def _ph3(p1):
                eT = [k.sb(p1, [128, 2, 512], BF16, "eT") for _ in range(2)]
                rden = [k.sb(p1, [128, 512], F32, "rden") for _ in range(2)]
                kvb = [k.sb(p1, [128, 2, 1024], BF16, "kvb") for _ in range(3)]

                def kvload(s):
                    if s < NS:
                        k.dma("pool", kvb[s % 3], V(ckv.ap[s * 256:(s + 1) * 256, :].rearrange("(mc p) n -> p mc n", p=128), []))

                for s in range(3):
                    kvload(s)
                KTs = [k.sb(p1, [128, 4, 256], BF16, "KTs") for _ in range(3)]
                eTs = [k.sb(p1, [128, 4, 2, LS], BF16, "eTs") for _ in range(3)]
                rds = [k.sb(p1, [128, 4, LS], F32, "rds") for _ in range(3)]
                (wx,) = w_next()
                for h in range(4):
                    for bi, (c0, n) in enumerate(BLKS):
                        b = pm_next()
                        proj_fm(b, wx, h * 128, XN, 8, c0, n)
                        k.copy(SL(SYM + h, c0, n), PS(b, w=n), e="act")
                w_prefetch()
                sc = 128.0 ** -0.5
                it = 0
                for h in range(4):
                    for bi, (c0, n) in enumerate(BLKS[:4]):
                        e_ = eT[it % 2]
                        r_ = rden[it % 2]
                        BK = PD if it % 2 == 0 else [PM[0], PM[1], PM[2], PT]
                        it += 1
                        for mc in range(2):
                            k.pe([mmf(PS(BK[mc]), KT[:, h, mc * 128:(mc + 1) * 128], SL(SYM + h, c0, n), True, True)],
                                 [PS(BK[mc])], [KT, SL(SYM + h, c0, n)])
                            k.act(e_[:, mc, :], PS(BK[mc]), AF.Exp, scale=sc)
                        k.pe([mmf(PS(BK[2]), Vtok[:, mc, h * 128:(h + 1) * 128], e_[:, mc, :], mc == 0, mc == 1) for mc in range(2)],
                             [PS(BK[2])], [Vtok, e_])
                        k.pe([mmf(PS(BK[3]), onesb, e_[:, mc, :], mc == 0, mc == 1) for mc in range(2)],
                             [PS(BK[3])], [onesb, e_])
                        k.act(r_, PS(BK[3]), AF.Ln)
                        k.act(r_, r_, AF.Exp, scale=-1.0)
                        k.tt(SL(SYM + h, c0, n), PS(BK[2]), r_, ALU.mult)
                def satt(s, BK):
                    kv_, kts, es_, rd_ = kvb[s % 3], KTs[s % 3], eTs[s % 3], rds[s % 3]
                    ck_ = V(kv_.ap[:, :, 0:512], kv_.cells)
                    cv_ = V(kv_.ap[:, :, 512:1024], kv_.cells)
                    bT, bS = BK
                    bO, oc = bS, 64
                    for hp in range(2):
                        pt = PSB(bT)
                        fns = []
                        for hh in range(2):
                            for mc in range(2):
                                h = hp * 2 + hh
                                fns.append(trf(pt[:, hh * 256 + mc * 128: hh * 256 + (mc + 1) * 128], ck_[:, mc, h * 128:(h + 1) * 128], identb))
                        k.pe(fns, [pt], [ck_, identb])
                        k.copy(V(kts.ap[:, hp * 2:hp * 2 + 2, :], kts.cells), V(pt.ap[:, 0:512].rearrange("p (h m) -> p h m", m=256), pt.cells),
                               e=("act" if hp else "dve"))
                        yield
                    c0 = TP + s * LS
                    fns = []
                    for h in range(4):
                        for mc in range(2):
                            fns.append(mmf(PS(bS, w=LS, c0=(h * 2 + mc) * LS), kts[:, h, mc * 128:(mc + 1) * 128], SL(SYM + h, c0, LS), True, True))
                    k.pe(fns, [PS(bS)], [kts] + [SL(SYM + h, c0, LS) for h in range(4)])
                    k.act(V(es_.ap.rearrange("p h m t -> p (h m t)"), es_.cells), PS(bS, w=8 * LS), AF.Exp, scale=sc)
                    yield
                    fns = []
                    for h in range(4):
                        for mc in range(2):
                            fns.append(mmf(PS(bO, w=LS, c0=oc + h * LS), cv_[:, mc, h * 128:(h + 1) * 128], es_[:, h, mc, :], mc == 0, mc == 1))
                        for mc in range(2):
                            fns.append(mmf(PS(bO, w=LS, c0=oc + 4 * LS + h * LS), onesb, es_[:, h, mc, :], mc == 0, mc == 1))
                    k.pe(fns, [PS(bO)], [cv_, es_, onesb])
                    k.recip(V(rd_.ap.rearrange("p h t -> p (h t)"), rd_.cells), PS(bO, w=4 * LS, c0=oc + 4 * LS))
                    k.tt(SL(SYM, c0, LS, nslots=4), V(PS(bO, w=4 * LS, c0=oc).ap.rearrange("p (h t) -> p h t", t=LS), [pcells[bO]]), rd_, ALU.mult)
                    kvload(s + 3)
                    yield

                BKS = [(PT, PD[0]), (PM[0], PM[1]), (PM[2], PD[1])]
                pending = [satt(s, BKS[s % 3]) for s in range(NS)]
                active = []
                while pending or active:
                    while pending and len(active) < 3:
                        active.append(pending.pop(0))
                    for g in list(active):
                        try:
                            next(g)
                        except StopIteration:
                            active.remove(g)
                k.barrier()

            def _ph4(p1):
                (wz,) = w_next()

                def f32t(nm):
                    return k.sb(p1, [128, 4, 128], F32, nm)

                def b16t(nm):
                    return k.sb(p1, [128, 4, 128], BF16, nm)

                Gt, Gam, U_ = b16t("Gt"), b16t("Gam"), b16t("U")
                G4 = f32t("G4")
                def b16h(nm):
                    return k.sb(p1, [128, 2, 128], BF16, nm)

                Lb = [[b16h("L"), b16h("L")] for _ in range(2)]
                Fb = [[b16h("F"), b16h("F")] for _ in range(2)]
                nGh = [b16h("nG"), b16h("nG")]
                Eb = [[b16h("E"), b16h("E")] for _ in range(3)]
                QKm = [b16t("QKm") for _ in range(3)]
                kd = [b16t("kd") for _ in range(3)]
                vtok = [b16t("vtok") for _ in range(3)]
                sm = [k.sb(p1, [128, 16], F32, "sm") for _ in range(3)]
                vS, vnew, ydn = b16t("vS"), b16t("vnew"), b16t("ydn")
                o_ = f32t("o")
                S_, Sb = f32t("S"), b16t("Sb")
                S2_, Sb2 = f32t("S2"), b16t("Sb2")
                sm2 = k.sb(p1, [128, 16], F32, "sm2")
                ab = k.sb(p1, [128, 17, 8], F32, "ab")
                abs_ = k.sb(p1, [LS, NS, 8], F32, "abs")
                tmp17 = k.sb(p1, [128, 17, 4], F32, "tmp17")

                for h in range(4):
                    for bi, (c0, n) in enumerate(BLKS):
                        b = pm_next()
                        proj_fm(b, wz, h * 128, XN, 8, c0, n)
                        k.act(SL(SYD + h, c0, n), PS(b, w=n), AF.Silu)
                for tt_ in range(16):
                    k.pe([mmf(PS(PT, w=8, c0=tt_ * 8), SL(XN + kk, tt_ * 128, 128), wz[:, kk, 512:520], kk == 0, kk == 7) for kk in range(8)],
                         [PS(PT)], [wz] + [SL(XN + kk, tt_ * 128, 128) for kk in range(8)])
                k.copy(V(ab.ap[:, 0:16, :], ab.cells), V(PS(PT, w=128).ap.rearrange("p (t c) -> p t c", c=8), [pcells[PT]]))
                for s in range(NS):
                    k.pe([mmf(PS(PT, n=LS, w=8, c0=s * 8), SL(XN + kk, TP + s * LS, LS), wz[:, kk, 512:520], kk == 0, kk == 7) for kk in range(8)],
                         [PS(PT)], [wz] + [SL(XN + kk, TP + s * LS, LS) for kk in range(8)])
                k.copy(abs_, V(PS(PT, n=LS, w=128).ap.rearrange("p (t c) -> p t c", c=8), [pcells[PT]]))
                for (a_, g_, nt, npart) in ((ab, gb, 16, 128), (abs_, gbs, NS, LS)):
                    av = V(a_.ap[0:npart, 0:nt, 0:4], a_.cells)
                    bv = V(a_.ap[0:npart, 0:nt, 4:8], a_.cells)
                    gv = V(g_.ap[0:npart, 0:nt, 0:4], g_.cells)
                    gbv = V(g_.ap[0:npart, 0:nt, 4:8], g_.cells)
                    tv = V(tmp17.ap[0:npart, 0:nt, :], tmp17.cells)
                    k.act(gbv, bv, AF.Sigmoid)
                    k.tt(tv, av, vecs[0:npart, V_DTB:V_DTB + 4].us(1).bc([npart, nt, 4]), ALU.add)
                    k.act(tv, tv, AF.Exp)
                    k.act(tv, tv, AF.Ln, bias=1.0)
                    k.tt(gv, tv, negA[0:npart, :].us(1).bc([npart, nt, 4]), ALU.mult)

                def bc4(v, n, w=128):
                    return v.us(2).bc([n, 4, w])

                def mbc(m, n):
                    return V(m.ap[0:n, 0:n].unsqueeze(1).to_broadcast([n, 4, n]), m.cells)

                def P4(bank, n, w):
                    return V(psap[0:n, bank, 0:4 * w].rearrange("p (h w) -> p h w", w=w), [pcells[bank]])

                F0, F1, SA, SB = PD[0], PD[1], PD[2], PD[3]

                def front(n, c0, gtok, btok, i):
                    E_, Q_, kd_, vt_, s_ = Eb[i % 3], QKm[i % 3], kd[i % 3], vtok[i % 3], sm[i % 3]
                    L_, F_ = Lb[i % 2], Fb[i % 2]
                    kT = [SL(SK + h, c0, n) for h in range(4)]
                    qT = [SL(SQ + h, c0, n) for h in range(4)]
                    vT = [SL(SV_ + h, c0, n) for h in range(4)]
                    k.pe([mmf(PS(F0, n=n, w=4), ML[0:n, 0:n], gtok, True, True),
                          mmf(PS(F0, n=128, w=4, c0=8), onesf[0:n, :], gtok, True, True)], [PS(F0)], [ML, onesf, gtok])
                    k.copy(s_[0:n, 0:4], PS(F0, n=n, w=4), e="act")
                    k.act(s_[:, 12:16], PS(F0, w=4, c0=8), AF.Exp)
                    k.tt(s_[0:n, 8:12], PS(F0, n=n, w=4, c0=8), s_[0:n, 0:4], ALU.subtract)
                    k.act(s_[0:n, 4:8], s_[0:n, 0:4], AF.Exp)
                    k.act(s_[0:n, 8:12], s_[0:n, 8:12], AF.Exp)
                    yield
                    for h in range(4):
                        k.act(G4[0:n, h, 0:n], MG[0:n, 0:n], AF.Copy, scale=gtok[:, h:h + 1])
                    yield
                    k.pe([mmf(P4(F0, n, n)[:, h, :], G4[0:n, h, 0:n], ML[0:n, 0:n], True, True) for h in range(4)],
                         [PS(F0)], [G4, ML])
                    k.pe([mmf(P4(F1, n, n)[:, h, :], kT[h], kT[h], True, True) for h in range(4)], [PS(F1)], kT)
                    k.act(Gam[0:n, :, 0:n], P4(F0, n, n), AF.Exp)
                    yield
                    pt = PSB(PT, n)
                    k.pe([trf(pt[:, h * 128:(h + 1) * 128], kT[h], identb) for h in range(4)] +
                         [trf(pt[:, 512 + h * 128: 512 + (h + 1) * 128], vT[h], identb) for h in range(4)], [pt], kT + vT + [identb])
                    p8 = V(pt.ap.rearrange("p (a h w) -> p a h w", a=2, w=128), pt.cells)
                    k.tt(kd_[0:n], p8[:, 0], bc4(s_[0:n, 8:12], n), ALU.mult)
                    k.copy(vt_[0:n], p8[:, 1], e="act")
                    yield
                    k.pe([mmf(P4(F0, n, n)[:, h, :], kT[h], qT[h], True, True) for h in range(4)], [PS(F0)], kT + qT)
                    k.tt(Gt[0:n, :, 0:n], P4(F0, n, n), Gam[0:n, :, 0:n], ALU.mult)
                    yield
                    k.tt(Q_[0:n, :, 0:n], Gt[0:n, :, 0:n], mbc(ML, n), ALU.mult)
                    k.tt(Gam[0:n, :, 0:n], Gam[0:n, :, 0:n], mbc(MUS, n), ALU.mult)
                    yield
                    for h in range(4):
                        k.stt(U_[0:n, h, 0:n], P4(F1, n, n)[:, h, :], btok[:, h:h + 1], Gam[0:n, h, 0:n], ALU.mult, ALU.mult)
                        if h == 1:
                            yield
                    yield
                    ptl = PSB(PT, n)
                    k.pe([trf(ptl[:, h * 128: h * 128 + n], U_[0:n, h, 0:n], identb[0:n, 0:n]) for h in range(4)], [ptl], [U_, identb])
                    ptl4 = V(ptl.ap[:, 0:512].rearrange("p (h w) -> p h w", w=128)[:, :, 0:n], ptl.cells)
                    idb2 = V(identf.ap[0:n, 0:n].unsqueeze(1).to_broadcast([n, 2, n]), identf.cells)
                    k.tt(Gt[0:n, :, 0:n], U_[0:n, :, 0:n], mbc(LV(0), n), ALU.mult)
                    for hh in range(2):
                        k.copy(L_[hh][0:n, :, 0:n], ptl4[:, 2 * hh:2 * hh + 2, :], e="act")
                        k.tt(E_[hh][0:n, :, 0:n], idb2, Gt[0:n, 2 * hh:2 * hh + 2, 0:n], ALU.add)
                    k.tt(Gam[0:n, :, 0:n], ptl4, mbc(LVT(0), n), ALU.mult)
                    yield
                    for hh in range(2):
                        k.tt(F_[hh][0:n, :, 0:n], idb2, Gam[0:n, 2 * hh:2 * hh + 2, 0:n], ALU.add)
                    yield

                def solve(n, i, hh):
                    E_, L_, F_, nG = Eb[i % 3][hh], Lb[i % 2][hh], Fb[i % 2][hh], nGh[hh]
                    bank = SA if hh == 0 else SB
                    nl = n.bit_length() - 1

                    def H2(c):
                        return V(psap[0:n, bank, c * 256: c * 256 + 2 * n].rearrange("p (h w) -> p h w", w=n), [pcells[bank]])

                    m2 = lambda m: V(m.ap[0:n, 0:n].unsqueeze(1).to_broadcast([n, 2, n]), m.cells)
                    for l in range(1, nl):
                        k.pe([mmf(H2(0)[:, h, :], L_[0:n, h, 0:n], E_[0:n, h, 0:n], True, True) for h in range(2)],
                             [PS(bank)], [L_, E_])
                        k.tt(nG[0:n, :, 0:n], H2(0), m2(LV(l)), ALU.mult)
                        yield
                        fns = []
                        for h in range(2):
                            fns.append(mmf(H2(1)[:, h, :], F_[0:n, h, 0:n], identb[0:n, 0:n], True, False))
                            fns.append(mmf(H2(1)[:, h, :], F_[0:n, h, 0:n], nG[0:n, h, 0:n], False, True))
                        if l < nl - 1:
                            for h in range(2):
                                fns.append(mmf(H2(0)[:, h, :], identb[0:n, 0:n], F_[0:n, h, 0:n], True, False))
                                fns.append(mmf(H2(0)[:, h, :], nG[0:n, h, 0:n], F_[0:n, h, 0:n], False, True))
                        k.pe(fns, [PS(bank)], [F_, nG, identb])
                        k.copy(E_[0:n, :, 0:n], H2(1), e="act")
                        if l < nl - 1:
                            k.copy(F_[0:n, :, 0:n], H2(0), e="act")
                        yield

                def seq(n, c0, btok, rqtok, i, S_=S_, Sb=Sb, need_sb=True):
                    E_, Q_, kd_, vt_, s_ = Eb[i % 3], QKm[i % 3], kd[i % 3], vtok[i % 3], sm[i % 3]
                    kT = [SL(SK + h, c0, n) for h in range(4)]
                    qT = [SL(SQ + h, c0, n) for h in range(4)]
                    b1, b2, b3 = PM[0], PM[1], PM[2]
                    k.pe([mmf(P4(b1, n, 128)[:, h, :], kT[h], Sb[:, h, :], True, True) for h in range(4)], [PS(b1)], kT + [Sb])
                    k.pe([mmf(P4(b2, n, 128)[:, h, :], qT[h], Sb[:, h, :], True, True) for h in range(4)], [PS(b2)], qT + [Sb])
                    k.tt(o_[0:n], P4(b1, n, 128), bc4(s_[0:n, 4:8], n), ALU.mult)
                    yield
                    k.tt(vS[0:n], vt_[0:n], o_[0:n], ALU.subtract)
                    yield
                    k.pe([mmf(P4(b3, n, 128)[:, h, :], E_[h // 2][0:n, h % 2, 0:n], vS[0:n, h, :], True, True) for h in range(4)], [PS(b3)], [E_[0], E_[1], vS])
                    k.tt(vnew[0:n], P4(b3, n, 128), bc4(btok, n), ALU.mult)
                    k.tt(o_[0:n], P4(b2, n, 128), bc4(s_[0:n, 4:8], n), ALU.mult)
                    yield
                    k.pe([mmf(P4(b1, n, 128)[:, h, :], Q_[0:n, h, 0:n], vnew[0:n, h, :], True, True) for h in range(4)], [PS(b1)], [Q_, vnew])
                    k.pe([mmf(P4(b3, 128, 128)[:, h, :], kd_[0:n, h, :], vnew[0:n, h, :], True, True) for h in range(4)], [PS(b3)], [kd_, vnew])
                    k.tt(S_, S_, bc4(s_[:, 12:16], 128), ALU.mult)
                    yield
                    k.tt(S_, S_, P4(b3, 128, 128), ALU.add)
                    if need_sb:
                        k.copy(Sb, S_, e="act")
                    yield
                    k.tt(o_[0:n], o_[0:n], P4(b1, n, 128), ALU.add)
                    for h in range(4):
                        k.act(ydn[0:n, h, :], o_[0:n, h, :], AF.Square, accum=sm2[0:n, h:h + 1])
                    yield
                    k.tt(sm2[0:n, 4:8], rqtok, rqtok, ALU.mult)
                    k.tt(sm2[0:n, 0:4], sm2[0:n, 0:4], sm2[0:n, 4:8], ALU.mult)
                    k.act(sm2[0:n, 0:4], sm2[0:n, 0:4], AF.Ln, scale=1.0 / 128, bias=EPS)
                    k.act(sm2[0:n, 0:4], sm2[0:n, 0:4], AF.Exp, scale=-0.5)
                    k.tt(sm2[0:n, 0:4], sm2[0:n, 0:4], rqtok, ALU.mult)
                    yield
                    k.tt(o_[0:n], o_[0:n], bc4(sm2[0:n, 0:4], n), ALU.mult)
                    k.tt(ydn[0:n], o_[0:n], V(vecs.ap[0:n, V_DNN:V_DNN + 128].unsqueeze(1).to_broadcast([n, 4, 128]), vecs.cells), ALU.mult)
                    pt = PSB(PT)
                    k.pe([trf(pt[:, h * 128: h * 128 + n], ydn[0:n, h, :], identb[0:n, 0:n]) for h in range(4)], [pt], [ydn, identb])
                    k.tt(SL(SYD, c0, n, nslots=4), V(pt.ap[:, 0:512].rearrange("p (h w) -> p h w", w=128)[:, :, 0:n], pt.cells),
                         SL(SYD, c0, n, nslots=4), ALU.mult)
                    yield

                def run_interleaved(gens, strides=None):
                    items = [(g, (strides[i] if strides else 1)) for i, g in enumerate(gens) if g is not None]
                    r = 0
                    while items:
                        for it_ in list(items):
                            g, st = it_
                            if r % st:
                                continue
                            try:
                                next(g)
                            except StopIteration:
                                items.remove(it_)
                        r += 1

                k.memset(S_, 0.0)
                k.memset(Sb, 0.0)
                NCH = TP // 128

                def pfront(c):
                    return front(128, c * 128, gb[:, c, 0:4], gb[:, c, 4:8], c) if c < NCH else None

                def psolve(c, hh):
                    return solve(128, c, hh) if c < NCH else None

                run_interleaved([pfront(0)])
                run_interleaved([pfront(1), psolve(0, 0), psolve(0, 1)])
                for c in range(NCH):
                    run_interleaved([pfront(c + 2), psolve(c + 1, 0), psolve(c + 1, 1), seq(128, c * 128, gb[:, c, 4:8], rq[:, c, :], c)], [2, 1, 1, 1])
                k.dma("sp", V(sp_o.ap.rearrange("(h k) v -> k h v", k=128), []), S_, is_output=True)
                def sfront(s_i):
                    return front(LS, TP + s_i * LS, gbs[:, s_i, 0:4], gbs[:, s_i, 4:8], s_i) if s_i < NS else None

                def ssolve(s_i, hh):
                    return solve(LS, s_i, hh) if s_i < NS else None

                Sd, Sbd = [S_, S2_], [Sb, Sb2]

                def sload(s_i):
                    if s_i < NS:
                        k.dma("sp", Sd[s_i % 2], V(sdn.ap[s_i * 512:(s_i + 1) * 512, :].rearrange("(h k) v -> k h v", k=128), []))

                def sseq(s_i):
                    Sa, Sba = Sd[s_i % 2], Sbd[s_i % 2]
                    sload(s_i + 1)
                    k.copy(Sba, Sa, e="act")
                    yield
                    yield from seq(LS, TP + s_i * LS, gbs[:, s_i, 4:8], rqs[:, s_i, :], s_i, Sa, Sba, need_sb=False)
                    k.dma("sp", V(ss_o.ap[s_i * 512:(s_i + 1) * 512, :].rearrange("(h k) v -> k h v", k=128), []), Sa, is_output=True)
                    yield

                sload(0)

                run_interleaved([sfront(0)])
                run_interleaved([sfront(1), ssolve(0, 0), ssolve(0, 1)])
                for s_i in range(NS):
                    run_interleaved([sfront(s_i + 2), ssolve(s_i + 1, 0), ssolve(s_i + 1, 1), sseq(s_i)])
                k.barrier()

            SMG = 8

            def _ph5(p1):
                sg = [k.sb(p1, [128, 512], F32, "sg") for _ in range(2)]
                tm = [k.sb(p1, [128, 512], F32, "tm") for _ in range(2)]
                accf = [k.sb(p1, [128, T], F32, "accf") for _ in range(2)]
                it = 0
                ysl = [SYA, SYD, SYM]
                for mp in range(4):
                    for b3 in range(3):
                        wg, wb = w_next()
                        for mm in range(2):
                            m = mp * 2 + mm
                            wo_ = mm * 128
                            for bi, (c0, n) in enumerate(BLKS):
                                a_ = accf[mm][:, c0:c0 + n]
                                s_ = sg[it % 2]
                                t_ = tm[it % 2]
                                it += 1
                                bg = pm_next()
                                proj_fm(bg, wg, wo_, XN, 8, c0, n)
                                k.act(s_[:, 0:n], PS(bg, w=n), AF.Sigmoid)
                                bp = pm_next()
                                k.pe([mmf(PS(bp, w=n), wb[:, kk, wo_:wo_ + 128], SL(ysl[b3] + kk, c0, n), kk == 0, kk == 3) for kk in range(4)],
                                     [PS(bp, w=n)], [wb] + [SL(ysl[b3] + kk, c0, n) for kk in range(4)])
                                if b3 == 0:
                                    k.tt(a_, s_[:, 0:n], PS(bp, w=n), ALU.mult)
                                elif b3 == 1:
                                    k.tt(t_[:, 0:n], s_[:, 0:n], PS(bp, w=n), ALU.mult)
                                    k.tt(a_, a_, t_[:, 0:n], ALU.add)
                                else:
                                    k.tt(t_[:, 0:n], s_[:, 0:n], PS(bp, w=n), ALU.mult)
                                    k.tt(SL(SMG + m, c0, n), a_, t_[:, 0:n], ALU.add)
                for nm_, sl_ in (("ya", SYA), ("yd", SYD), ("ym", SYM), ("mg", SMG)):
                    tap(nm_ + "0", SL(sl_, 0, T))
                    tap(nm_ + "3", SL(sl_ + 3, 0, T))
                k.barrier()

            SX = 16
            def _ph6(p2):
                x1s = k.sb(p2, [128, D], F32, "x1s")
                xs = [k.sb(p2, [128, D], BF16, "xs2") for _ in range(2)]
                junk = k.sb(p2, [128, D], BF16, "junk2")
                ssq = [k.sb(p2, [128, 1], F32, "ssq2") for _ in range(2)]
                sgl = [k.sb(p2, [128, 512], F32, "sgl") for _ in range(2)]
                yt = [k.sb(p2, [128, D], F32, "yt") for _ in range(2)]
                gfin = k.sb(p2, [128, D], F32, "gfin")
                k.dma("sp", gfin, gfin_d)
                X1BASE = SX * (SLOTW // 2)

                def X1(tt_, n, c0=0, w=D):
                    if tt_ < 16:
                        a0 = X1BASE + tt_ * D + c0
                        return V(arena32[0:n, a0:a0 + w], cells_bytes(SX, (tt_ * D + c0) * 4, (tt_ * D + c0 + w) * 4))
                    return x1s[0:n, c0:c0 + w]

                (woA,) = w_next()
                (woB,) = w_next(prefetch=False)
                wo2 = [woA, woB]
                def p2tile(tt_):
                    n = 128 if tt_ < 16 else 64
                    t0 = tt_ * 128
                    for half in range(2):
                        b = (PM[half] if tt_ % 2 == 0 else PD[half])
                        k.pe([mmf(PS(b, n=n), SL(SMG + kk, t0, n), wo2[half][:, kk, :], kk == 0, kk == 7) for kk in range(8)],
                             [PS(b)], [wo2[half]] + [SL(SMG + kk, t0, n) for kk in range(8)])
                        xv = X1(tt_, n, half * 512, 512)
                        k.tt(xv, xv, PS(b, n=n), ALU.add)
                        yield
                    q_, s_ = ssq[tt_ % 2], xs[tt_ % 2]
                    k.act(junk[0:n, :], X1(tt_, n), AF.Square, accum=q_[0:n, :])
                    k.act(q_[0:n, :], q_[0:n, :], AF.Sqrt, scale=1.0 / D, bias=EPS)
                    yield
                    k.recip(q_[0:n, :], q_[0:n, :])
                    k.ts(s_[0:n, :], X1(tt_, n), q_[0:n, 0:1], ALU.mult)
                    yield
                    pt = PSB(PT if tt_ % 2 == 0 else PD[2])
                    k.pe([trf(pt[:, kk * 128: kk * 128 + n], s_[0:n, kk * 128:(kk + 1) * 128], identb[0:n, 0:n]) for kk in range(8)],
                         [pt], [s_, identb])
                    ptv = V(pt.ap.rearrange("p (k t) -> p k t", t=128)[:, :, 0:n], pt.cells)
                    k.tt(SL(XN, t0, n, nslots=8), ptv, vecs[:, V_GFFN:V_GFFN + 8].us(2).bc([128, 8, n]), ALU.mult)
                    yield

                for tt_ in range(17):
                    n_ = 128 if tt_ < 16 else 64
                    k.dma("sp", X1(tt_, n_), V(xin.ap[tt_ * 128: tt_ * 128 + n_, :], []))
                run_window([p2tile(tt_) for tt_ in range(17)], 2)
                w_issue(wstate["used"] + 1)
                SACT = 8
                it = 0
                for (j0, nj) in FFN_PARTS:
                    for jj in range(nj):
                        if jj % 2 == 0:
                            wg, wu = w_next()
                        wo_ = (jj % 2) * 128
                        for bi, (c0, n) in enumerate(BLKS):
                            bg = pm_next()
                            proj_fm(bg, wg, wo_, XN, 8, c0, n)
                            s_ = sgl[it % 2]
                            it += 1
                            k.act(s_[:, 0:n], PS(bg, w=n), AF.Silu)
                            bu = pm_next()
                            proj_fm(bu, wu, wo_, XN, 8, c0, n)
                            k.tt(SL(SACT + jj, c0, n), s_[:, 0:n], PS(bu, w=n), ALU.mult)
                    for half in range(2):
                        (wd,) = w_next()
                        for tt_ in range(17):
                            n = 128 if tt_ < 16 else 64
                            t0 = tt_ * 128
                            b = pm_next()
                            k.pe([mmf(PS(b, n=n), SL(SACT + kk, t0, n), wd[:, kk, :], kk == 0, kk == nj - 1) for kk in range(nj)],
                                 [PS(b)], [wd] + [SL(SACT + kk, t0, n) for kk in range(nj)])
                            xv = X1(tt_, n, half * 512, 512)
                            k.tt(xv, xv, PS(b, n=n), ALU.add)
                def p4tile(tt_):
                    n = 128 if tt_ < 16 else 64
                    q_, y_ = ssq[tt_ % 2], yt[tt_ % 2]
                    k.act(junk[0:n, :], X1(tt_, n), AF.Square, accum=q_[0:n, :])
                    k.act(q_[0:n, :], q_[0:n, :], AF.Sqrt, scale=1.0 / D, bias=EPS)
                    yield
                    k.recip(q_[0:n, :], q_[0:n, :])
                    k.stt(y_[0:n, :], X1(tt_, n), q_[0:n, 0:1], gfin[0:n, :], ALU.mult, ALU.mult)
                    yield
                    k.dma("sp", V(y_o.ap[tt_ * 128: tt_ * 128 + n, :], []), y_[0:n, :], is_output=True)
                    yield

                run_window([p4tile(tt_) for tt_ in range(17)], 2)

            for _i, _ph in enumerate([_ph0, _ph1, _ph2, _ph3, _ph4, _ph5, _ph6]):
                with contextlib.ExitStack() as _pes:
                    _ph(_pes)
                PHASE_MARKS.append((_i, k.npe, dict(k.cnt)))
                if STOP == _i + 1:
                    break
            else:
                assert wstate["used"] == len(jobs), (wstate, len(jobs))
        except _Stop:
            pass
        for ev in k.out_events:
            k.need("sp", ev)
        k.barrier(engines=("sp",))
    return nc


_NC = None


def _consts():
    c = np.zeros((128, NCST), np.float32)
    i = np.arange(128)
    c[:, K_ID:K_ID + 128] = np.eye(128)
    c[:, K_ONE:K_ONE + 128] = 1.0
    c[:, K_ML:K_ML + 128] = (i[:, None] <= i[None, :])
    c[:, K_MG:K_MG + 128] = (i[:, None] > i[None, :])
    c[:, K_MUS:K_MUS + 128] = (i[:, None] < i[None, :])
    for l in range(7):
        m = ((i[:, None] >> (l + 1)) == (i[None, :] >> (l + 1))) & (((i[:, None] >> l) & 1) == 0) & (((i[None, :] >> l) & 1) == 1)
        c[:, K_LV + l * 128: K_LV + (l + 1) * 128] = -m.astype(np.float32)
        c[:, K_LV + (7 + l) * 128: K_LV + (8 + l) * 128] = -m.T.astype(np.float32)
    return c


def kernel(x_prompt, x_sample, mem_prompt, state_conv_a, state_dn_conv, state_dn, cache_mem_k,
           cache_mem_v, norm_mix, w_in, conv_a_w, dn_conv_w, dn_a_log, dn_dt_bias, dn_norm,
           norm_mem, w_mem_kv, w_branch, w_o, norm_ffn, w_ffn_up, w_ffn_down, norm_final):
    global _NC
    f = lambda a: np.ascontiguousarray(np.asarray(a, dtype=np.float32))
    x_prompt, x_sample, mem_prompt = f(x_prompt), f(x_sample), f(mem_prompt)
    if _NC is None:
        _NC = build_nc()
    vec = np.zeros((128, NVEC), np.float32)
    vec[:, V_GMIX:V_GMIX + 8] = f(norm_mix)[0].reshape(8, 128).T
    vec[:, V_GMEM:V_GMEM + 8] = f(norm_mem)[0].reshape(8, 128).T
    vec[:, V_GFFN:V_GFFN + 8] = f(norm_ffn)[0].reshape(8, 128).T
    vec[:, V_CAW:V_CAW + 12] = f(conv_a_w)[0].reshape(3, 4, 128).transpose(2, 0, 1).reshape(128, 12)
    vec[:, V_DCW:V_DCW + 48] = f(dn_conv_w)[0].reshape(4, 12, 128).transpose(2, 0, 1).reshape(128, 48)
    vec[:, V_ALOG:V_ALOG + 4] = f(dn_a_log)[0][None, :]
    vec[:, V_DTB:V_DTB + 4] = f(dn_dt_bias)[0][None, :]
    vec[:, V_DNN:V_DNN + 128] = f(dn_norm)[0][None, :]
    cst = _consts()
    shared = dict(w_in=f(w_in)[0], w_kv=f(w_mem_kv)[0], w_br=f(w_branch)[0], w_o=f(w_o)[0],
                  w_up=f(w_ffn_up)[0], w_dn=f(w_ffn_down)[0], vecs=vec, cst=cst,
                  gfin=np.ascontiguousarray(np.broadcast_to(f(norm_final)[None, :], (128, D))))
    sca, sdc, sdn = f(state_conv_a)[0], f(state_dn_conv)[0], f(state_dn)[0]
    ck, cv = f(cache_mem_k)[0], f(cache_mem_v)[0]
    in_maps = []
    for c in range(NCORES):
        s0, s1 = c * NS, (c + 1) * NS
        m = dict(shared)
        m["xin"] = np.ascontiguousarray(np.concatenate([x_prompt[c], x_sample[s0:s1].reshape(NS * LS, D)], axis=0))
        m["mem"] = mem_prompt[c]
        m["sca"] = np.ascontiguousarray(sca[s0:s1].reshape(NS * 2, 512))
        m["sdc"] = np.ascontiguousarray(sdc[s0:s1].reshape(NS * 3, 1536))
        m["sdn"] = np.ascontiguousarray(sdn[s0:s1].reshape(NS * 512, 128))
        m["ckv"] = np.ascontiguousarray(np.concatenate([ck[s0:s1].reshape(NS * 256, 512), cv[s0:s1].reshape(NS * 256, 512)], axis=1))
        in_maps.append(m)
    if DBG_CORES:
        res = run_bass_kernel_spmd(_NC, in_maps[:DBG_CORES], core_ids=list(range(DBG_CORES)))
        R = list(res.results) + [res.results[0]] * (NCORES - DBG_CORES)
        global LAST_R
        LAST_R = res.results[0]
    else:
        res = run_bass_kernel_spmd(_NC, in_maps, core_ids=list(range(NCORES)))
        R = res.results
    y_prompt = np.stack([R[c]["y"][:TP] for c in range(NCORES)])
    y_sample = np.concatenate([R[c]["y"][TP:].reshape(NS, LS, D) for c in range(NCORES)], axis=0)
    ca_p = np.stack([R[c]["ca_p"] for c in range(NCORES)])[None]
    dc_p = np.stack([R[c]["dc_p"] for c in range(NCORES)])[None]
    s_p = np.stack([R[c]["s_p"].reshape(4, 128, 128) for c in range(NCORES)])[None]
    mk_p = np.stack([R[c]["mk"].reshape(256, 4, 128) for c in range(NCORES)])[None]
    mv_p = np.stack([R[c]["mv"].reshape(256, 4, 128) for c in range(NCORES)])[None]
    ca_s = np.concatenate([R[c]["ca_s"].reshape(NS, 2, 512) for c in range(NCORES)], axis=0)[None]
    dc_s = np.concatenate([R[c]["dc_s"].reshape(NS, 3, 1536) for c in range(NCORES)], axis=0)[None]
    s_s = np.concatenate([R[c]["s_s"].reshape(NS, 4, 128, 128) for c in range(NCORES)], axis=0)[None]
    return tuple(np.ascontiguousarray(a, dtype=np.float32) for a in
                 (y_prompt, y_sample, ca_p, dc_p, s_p, mk_p, mv_p, ca_s, dc_s, s_s))
```

```python
import contextlib
import numpy as np
import concourse.bass as bass
import concourse.mybir as mybir
from concourse.bass_utils import run_bass_kernel_spmd

F32 = mybir.dt.float32
BF16 = mybir.dt.bfloat16
F32R = mybir.dt.float32r
AF = mybir.ActivationFunctionType
ALU = mybir.AluOpType
AX = mybir.AxisListType

NCORES = 8
D = 1024
TP = 2048
NS = 16
LS = 4
T = TP + NS * LS
SLOTW = T
NSLOT = 32
BLKS = [(0, 512), (512, 512), (1024, 512), (1536, 512), (2048, 64)]
EPS = 1e-6
NRING = 32
WBUF = 6144
INW = 7176
C_B, C_C, C_H, C_Q, C_K, C_V, C_Z, C_AB, C_XQ, C_GA = 0, 512, 1024, 1536, 2048, 2560, 3072, 3584, 3592, 4104
V_GMIX, V_GMEM, V_GFFN, V_CAW, V_DCW, V_ALOG, V_DTB, V_DNN = 0, 8, 16, 24, 36, 84, 88, 92
NVEC = 220
K_ID, K_ONE, K_ML, K_MG, K_MUS, K_LV = 0, 128, 256, 384, 512, 640
NCST = 640 + 14 * 128


class Cell:
    __slots__ = ("w", "r", "x")

    def __init__(self, x=False):
        self.w = None
        self.r = {}
        self.x = x


class V:
    def __init__(self, ap, cells):
        self.ap = ap
        self.cells = cells

    def __getitem__(self, idx):
        return V(self.ap[idx], self.cells)

    def bc(self, shape):
        return V(self.ap.to_broadcast(shape), self.cells)

    def us(self, ax):
        return V(self.ap.unsqueeze(ax), self.cells)


class KB:
    def __init__(self, nc, es):
        self.nc = nc
        self.es = es
        self.eng = dict(pe=nc.tensor, act=nc.scalar, dve=nc.vector, pool=nc.gpsimd, sp=nc.sync)
        self.sem = {e: es.enter_context(nc.semaphore("s_" + e)) for e in ("pe", "act", "dve", "pool")}
        self.cnt = dict.fromkeys(self.sem, 0)
        self.known = {e: {} for e in self.eng}
        self.ring = {q: [es.enter_context(nc.semaphore("r_%s%d" % (q, i))) for i in range(NRING)] for q in ("sp", "pool")}
        self.ring_val = {q: [0] * NRING for q in self.ring}
        self.ring_pos = {q: 0 for q in self.ring}
        self.out_events = []
        self.snap = {}
        self.npe = 0
        self.nalloc = 0

    def sb(self, es, shape, dt, name=None):
        self.nalloc += 1
        t = es.enter_context(self.nc.sbuf_tensor("%s_%d" % (name or "t", self.nalloc), list(shape), dt))
        return V(t[:], [Cell()])

    def need(self, e, ev):
        if ev is None:
            return
        key, sem, val = ev
        k = self.known[e]
        if k.get(key, 0) >= val:
            return
        self.eng[e].wait_ge(sem, val)
        k[key] = val
        sn = self.snap.get((key, val))
        if sn:
            for kk, vv in sn.items():
                if k.get(kk, 0) < vv:
                    k[kk] = vv

    def deps(self, e, outs, ins):
        for v in ins:
            for c in v.cells:
                self.need(e, c.w)
                if c.x:
                    for ev in c.r.values():
                        if ev[0] != e:
                            self.need(e, ev)
        for v in outs:
            for c in v.cells:
                if c.w is not None and not (c.w[0] == e and e in ("pe", "act", "dve")):
                    self.need(e, c.w)
                for ev in c.r.values():
                    if not (ev[0] == e and e in ("pe", "act", "dve")):
                        self.need(e, ev)

    def commit(self, ev, outs, ins, issuer=None):
        if issuer is not None:
            self.snap[(ev[0], ev[2])] = dict(self.known[issuer])
        for v in ins:
            for c in v.cells:
                c.r[ev[0]] = ev
        for v in outs:
            for c in v.cells:
                c.w = ev
                c.r = {}

    def op(self, e, fn, outs, ins):
        self.deps(e, outs, ins)
        i = fn(self.eng[e])
        self.cnt[e] += 1
        i.then_inc(self.sem[e], 1)
        self.commit((e, self.sem[e], self.cnt[e]), outs, ins, issuer=e)

    def pe(self, fns, outs, ins):
        self.deps("pe", outs, ins)
        i = None
        for fn in fns:
            i = fn(self.nc.tensor)
            self.npe += 1
        self.cnt["pe"] += 1
        i.then_inc(self.sem["pe"], 1)
        self.commit(("pe", self.sem["pe"], self.cnt["pe"]), outs, ins, issuer="pe")

    def dma(self, q, out, in_, is_output=False, **kw):
        self.deps(q, [out], [in_])
        i = self.ring_pos[q]
        self.ring_pos[q] = (i + 1) % NRING
        sem = self.ring[q][i]
        key = "r_%s%d" % (q, i)
        prev = self.ring_val[q][i]
        if prev:
            self.need(q, (key, sem, prev))
        val = prev + 16
        self.eng[q].dma_start(out=out.ap, in_=in_.ap, **kw).then_inc(sem, 16)
        self.ring_val[q][i] = val
        ev = (key, sem, val)
        self.commit(ev, [out], [in_], issuer=q)
        if is_output:
            self.out_events.append(ev)
        return ev

    def barrier(self, engines=("pe", "act", "dve", "sp", "pool")):
        evs = [(e, self.sem[e], self.cnt[e]) for e in ("pe", "act", "dve") if self.cnt[e]]
        for q in ("sp", "pool"):
            for i in range(NRING):
                if self.ring_val[q][i]:
                    evs.append(("r_%s%d" % (q, i), self.ring[q][i], self.ring_val[q][i]))
        for e in engines:
            for ev in evs:
                if ev[0] != e:
                    self.need(e, ev)

    def act(self, out, in_, func, scale=1.0, bias=0.0, accum=None, extra_in=()):
        kw = {}
        ins = [in_] + list(extra_in)
        if isinstance(scale, V):
            ins.append(scale)
            scale = scale.ap
        if isinstance(bias, V):
            ins.append(bias)
            bias = bias.ap
        outs = [out]
        if accum is not None:
            outs.append(accum)
            kw["accum_out"] = accum.ap
        self.op("act", lambda e: e.activation(out=out.ap, in_=in_.ap, func=func, scale=scale, bias=bias, **kw), outs, ins)

    def tt(self, out, a, b, op, e="dve"):
        self.op(e, lambda en: en.tensor_tensor(out=out.ap, in0=a.ap, in1=b.ap, op=op), [out], [a, b])

    def ts(self, out, a, s1, op0, s2=None, op1=None, e="dve"):
        ins = [a]
        if isinstance(s1, V):
            ins.append(s1)
            s1 = s1.ap
        if isinstance(s2, V):
            ins.append(s2)
            s2 = s2.ap
        kw = {}
        if op1 is not None:
            kw["op1"] = op1
        self.op(e, lambda en: en.tensor_scalar(out=out.ap, in0=a.ap, scalar1=s1, scalar2=s2, op0=op0, **kw), [out], ins)

    def stt(self, out, a, s, b, op0, op1):
        ins = [a, b]
        if isinstance(s, V):
            ins.append(s)
            s = s.ap
        self.op("dve", lambda en: en.scalar_tensor_tensor(out=out.ap, in0=a.ap, scalar=s, in1=b.ap, op0=op0, op1=op1), [out], ins)

    def copy(self, out, in_, e="dve"):
        if e == "act":
            self.op("act", lambda en: en.copy(out=out.ap, in_=in_.ap), [out], [in_])
        else:
            self.op(e, lambda en: en.tensor_copy(out=out.ap, in_=in_.ap), [out], [in_])

    def recip(self, out, in_):
        self.op("dve", lambda en: en.reciprocal(out=out.ap, in_=in_.ap), [out], [in_])

    def memset(self, out, val, e="dve"):
        self.op(e, lambda en: en.memset(out.ap, val), [out], [])

    def rsum(self, out, in_):
        self.op("dve", lambda en: en.reduce_sum(out=out.ap, in_=in_.ap, axis=AX.X), [out], [in_])


def run_window(gens, width):
    pending = list(gens)
    active = []
    while pending or active:
        while pending and len(active) < width:
            active.append(pending.pop(0))
        for g in list(active):
            try:
                next(g)
            except StopIteration:
                active.remove(g)


def mmf(out, lhsT, rhs, start, stop):
    return lambda pe: pe.matmul(out.ap, lhsT=lhsT.ap, rhs=rhs.ap, start=start, stop=stop)


def trf(out, in_, ident):
    return lambda pe: pe.transpose(out=out.ap, in_=in_.ap, identity=ident.ap)


class _Stop(Exception):
    pass


STOP = None
SUB = None
DBG_CORES = None
DBG_TAPS = False
LAST_R = None
PHASE_MARKS = []


def build_nc():
    nc = bass.Bass("TRN2", target_bir_lowering=False)

    def din(name, shape):
        return V(nc.dram_tensor(name, list(shape), F32, kind="ExternalInput").ap(), [])

    def dout(name, shape):
        return V(nc.dram_tensor(name, list(shape), F32, kind="ExternalOutput").ap(), [])

    xin = din("xin", [T, D])
    mem = din("mem", [256, D])
    sca = din("sca", [NS * 2, 512])
    sdc = din("sdc", [NS * 3, 1536])
    sdn = din("sdn", [NS * 4 * 128, 128])
    ckv = din("ckv", [NS * 256, 1024])
    w_in = din("w_in", [D, INW])
    w_kv = din("w_kv", [D, 1024])
    w_br = din("w_br", [1536, D])
    w_o = din("w_o", [D, D])
    w_up = din("w_up", [D, 5632])
    w_dn = din("w_dn", [2816, D])
    vecs_d = din("vecs", [128, NVEC])
    cst_d = din("cst", [128, NCST])
    gfin_d = din("gfin", [128, D])
    y_o = dout("y", [T, D])
    cap_o = dout("ca_p", [2, 512])
    dcp_o = dout("dc_p", [3, 1536])
    sp_o = dout("s_p", [512, 128])
    mk_o = dout("mk", [256, 512])
    mv_o = dout("mv", [256, 512])
    cas_o = dout("ca_s", [NS * 2, 512])
    dcs_o = dout("dc_s", [NS * 3, 1536])
    ss_o = dout("s_s", [NS * 512, 128])

    with contextlib.ExitStack() as es:
        k = KB(nc, es)
        arena_t = es.enter_context(nc.sbuf_tensor("arena", [128, NSLOT * SLOTW], BF16))
        arena = arena_t[:]
        arena3 = arena.rearrange("p (s t) -> p s t", t=SLOTW)
        arena32 = arena.bitcast(F32)
        acells = [[Cell() for _ in range(5)] for _ in range(NSLOT)]

        def cells_bytes(slot, b0, b1):
            out = []
            while b0 < b1:
                s = slot + b0 // (SLOTW * 2)
                bb = b0 % (SLOTW * 2)
                ci = min(bb // 1024, 4)
                out.append(acells[s][ci])
                nxt = (bb // 1024 + 1) * 1024 if ci < 4 else SLOTW * 2
                b0 += nxt - bb
            return out

        def SL(slot, c0, n, nslots=1):
            cs = []
            for s in range(slot, slot + nslots):
                cs += cells_bytes(s, c0 * 2, (c0 + n) * 2)
            if nslots == 1:
                return V(arena3[:, slot, c0:c0 + n], cs)
            return V(arena3[:, slot:slot + nslots, c0:c0 + n], cs)

        def SL32(slot, c0, n):
            base = slot * (SLOTW // 2)
            return V(arena32[:, base + c0: base + c0 + n], cells_bytes(slot, c0 * 4, (c0 + n) * 4))

        ps_t = es.enter_context(nc.psum_tensor("ps", [128, 8, 512], F32))
        psap = ps_t[:]
        pcells = [Cell(True) for _ in range(8)]

        def PS(bank, n=128, w=512, c0=0):
            return V(psap[0:n, bank, c0:c0 + w], [pcells[bank]])

        def PSB(bank, n=128):
            return V(psap[0:n, bank, :].bitcast(BF16), [pcells[bank]])

        def PS2(bank, n=128):
            return V(psap[0:n, bank:bank + 2, :], [pcells[bank], pcells[bank + 1]])

        PM = [0, 1, 2]
        PT = 3
        PD = [4, 5, 6, 7]

        ring = [k.sb(es, [128, WBUF], BF16, "wring") for _ in range(2)]
        vecs = k.sb(es, [128, NVEC], F32, "vecs")
        cst = k.sb(es, [128, NCST], F32, "cst")
        cstb = k.sb(es, [128, 256], BF16, "cstb")
        gb = k.sb(es, [128, 17, 8], F32, "gb")
        gbs = k.sb(es, [LS, NS, 8], F32, "gbs")
        rq = k.sb(es, [128, 17, 4], F32, "rq")
        rqs = k.sb(es, [LS, NS, 4], F32, "rqs")
        negA = k.sb(es, [128, 4], F32, "negA")
        KT = k.sb(es, [128, 4, 256], BF16, "KT")
        Vtok = k.sb(es, [128, 2, 512], BF16, "Vtok")

        k.dma("sp", vecs, vecs_d)
        k.dma("sp", cst, cst_d)
        k.copy(cstb, cst[:, 0:256])
        identf = cst[:, K_ID:K_ID + 128]
        onesf = cst[:, K_ONE:K_ONE + 128]
        ML = cst[:, K_ML:K_ML + 128]
        MG = cst[:, K_MG:K_MG + 128]
        MUS = cst[:, K_MUS:K_MUS + 128]
        identb = cstb[:, 0:128]
        onesb = cstb[:, 128:256]

        def LV(l):
            return cst[:, K_LV + l * 128: K_LV + (l + 1) * 128]

        def LVT(l):
            return cst[:, K_LV + (7 + l) * 128: K_LV + (8 + l) * 128]

        k.act(negA, vecs[:, V_ALOG:V_ALOG + 4], AF.Exp)
        k.ts(negA, negA, -1.0, ALU.mult)

        jobs = []

        def wsrc(w, r0, nk, c0, nc_):
            return V(w.ap[r0:r0 + nk * 128, c0:c0 + nc_].rearrange("(kc p) n -> p kc n", p=128), [])

        jobs.append([(wsrc(w_kv, 0, 8, 0, 512), 8, 512)])
        jobs.append([(wsrc(w_kv, 0, 8, 512, 512), 8, 512)])
        for mp in range(2):
            jobs.append([(wsrc(w_in, 0, 8, C_H + mp * 256, 256), 8, 256),
                         (wsrc(w_in, 0, 8, C_C + mp * 256, 256), 8, 256),
                         (wsrc(w_in, 0, 8, C_B + mp * 256, 256), 8, 256)])
        for j in range(3):
            jobs.append([(wsrc(w_in, 0, 8, C_Q + j * 512, 512), 8, 512)])
        jobs.append([(wsrc(w_in, 0, 8, C_XQ, 512), 8, 512)])
        jobs.append([(wsrc(w_in, 0, 8, C_Z, 520), 8, 520)])
        for mp in range(4):
            for b in range(3):
                jobs.append([(wsrc(w_in, 0, 8, C_GA + b * 1024 + mp * 256, 256), 8, 256),
                             (wsrc(w_br, b * 512, 4, mp * 256, 256), 4, 256)])
        for j in range(2):
            jobs.append([(wsrc(w_o, 0, 8, j * 512, 512), 8, 512)])
        FFN_PARTS = [(0, 8), (8, 8), (16, 6)]
        for (j0, nj) in FFN_PARTS:
            for j in range(j0, j0 + nj, 2):
                jobs.append([(wsrc(w_up, 0, 8, j * 128, 256), 8, 256),
                             (wsrc(w_up, 0, 8, 2816 + j * 128, 256), 8, 256)])
            for half in range(2):
                jobs.append([(wsrc(w_dn, j0 * 128, nj, half * 512, 512), nj, 512)])
        wstate = dict(issued=0, used=0)

        def w_issue(upto):
            while wstate["issued"] < min(upto, len(jobs)):
                ji = wstate["issued"]
                buf = ring[ji % 2]
                off = 0
                for (src, nk, ncol) in jobs[ji]:
                    dst = V(buf.ap[:, off:off + nk * ncol].rearrange("p (k n) -> p k n", n=ncol), buf.cells)
                    k.dma("pool", dst, src)
                    off += nk * ncol
                wstate["issued"] += 1

        def w_next(prefetch=True):
            ji = wstate["used"]
            w_issue(ji + (2 if prefetch else 1))
            buf = ring[ji % 2]
            views = []
            off = 0
            for (src, nk, ncol) in jobs[ji]:
                views.append(V(buf.ap[:, off:off + nk * ncol].rearrange("p (k n) -> p k n", n=ncol), buf.cells))
                off += nk * ncol
            wstate["used"] += 1
            return views

        def w_prefetch():
            w_issue(wstate["used"] + 1)

        w_issue(2)

        def tap(name, v):
            if not DBG_TAPS:
                return
            shp = list(v.ap.shape)
            d_ = V(nc.dram_tensor("tap_" + name, shp, v.ap.dtype, kind="ExternalOutput").ap(), [])
            k.dma("sp", d_, v, is_output=True)

        pm_rr = [0]

        def pm_next():
            b = PM[pm_rr[0] % 3]
            pm_rr[0] += 1
            return b

        try:
            XN = 0
            def _ph0(p0):
                xt = [k.sb(p0, [128, D], F32, "xt") for _ in range(4)]
                xs = [k.sb(p0, [128, D], BF16, "xs") for _ in range(2)]
                junk = k.sb(p0, [128, D], BF16, "junk")
                ssq = [k.sb(p0, [128, 1], F32, "ssq") for _ in range(2)]
                rst = [k.sb(p0, [128, 1], F32, "rst") for _ in range(2)]
                memT = k.sb(p0, [128, 8, 256], BF16, "memT")
                mko = k.sb(p0, [128, 2, 1024], F32, "mko")

                def norm_T(src_d, r0, n, gcol, dst, i):
                    x_, s_, q_, r_ = xt[i % 4], xs[i % 2], ssq[i % 2], rst[i % 2]
                    k.act(junk[0:n, :], x_[0:n, :], AF.Square, accum=q_[0:n, :])
                    k.act(r_[0:n, :], q_[0:n, :], AF.Sqrt, scale=1.0 / D, bias=EPS)
                    yield
                    k.recip(r_[0:n, :], r_[0:n, :])
                    k.ts(s_[0:n, :], x_[0:n, :], r_[0:n, 0:1], ALU.mult)
                    yield
                    pt = PSB(PT if i % 2 == 0 else PD[0])
                    k.pe([trf(pt[:, kk * 128: kk * 128 + n], s_[0:n, kk * 128:(kk + 1) * 128], identb[0:n, 0:n]) for kk in range(8)],
                         [pt], [s_, identb])
                    ptv = V(pt.ap.rearrange("p (k t) -> p k t", t=128)[:, :, 0:n], pt.cells)
                    k.tt(dst, ptv, vecs[:, gcol:gcol + 8].us(2).bc([128, 8, n]), ALU.mult)
                    xload(i + 4)
                    yield

                srcs0 = [(xin, tt_ * 128, 128 if tt_ < 16 else 64) for tt_ in range(17)] + [(mem, mc * 128, 128) for mc in range(2)]

                def xload(i):
                    if i < len(srcs0):
                        sd, r0, n = srcs0[i]
                        k.dma("sp", xt[i % 4][0:n, :], V(sd.ap[r0:r0 + n, :], []))

                for i_ in range(4):
                    xload(i_)
                if SUB and '0' in SUB:
                    return
                gens0 = []
                for tt_ in range(17):
                    n = 128 if tt_ < 16 else 64
                    gens0.append(norm_T(xin, tt_ * 128, n, V_GMIX, SL(XN, tt_ * 128, n, nslots=8), tt_))
                if SUB and 'A' in SUB:
                    return
                for mc in range(2):
                    gens0.append(norm_T(mem, mc * 128, 128, V_GMEM, memT[:, :, mc * 128:(mc + 1) * 128], 17 + mc))
                run_window(gens0, 2)
                if SUB and 'B' in SUB:
                    return

                for part in range(2):
                    (wv,) = w_next()
                    if SUB and 'F' in SUB and part == 1:
                        return
                    for mc in range(2):
                        b = pm_next()
                        k.pe([mmf(PS(b), memT[:, kk, mc * 128:(mc + 1) * 128], wv[:, kk, :], kk == 0, kk == 7) for kk in range(8)],
                             [PS(b)], [memT, wv])
                        if not (SUB and 'H' in SUB and part == 1):
                            k.copy(mko[:, mc, part * 512:(part + 1) * 512], PS(b), e=("dve" if (SUB and 'I' in SUB) else "act"))
                        if part == 1 and not (SUB and 'G' in SUB):
                            k.copy(Vtok[:, mc, :], mko[:, mc, 512:1024])
                    if SUB and 'C' in SUB:
                        return
                    if part == 0:
                        for hp in range(2):
                            b = pm_next()
                            fns = []
                            for hh in range(2):
                                h = hp * 2 + hh
                                fns += [mmf(PS(b, w=256, c0=hh * 256), wv[:, kk, h * 128:(h + 1) * 128], memT[:, kk, :], kk == 0, kk == 7) for kk in range(8)]
                            k.pe(fns, [PS(b)], [memT, wv])
                            k.copy(V(KT.ap[:, hp * 2:hp * 2 + 2, :], KT.cells), V(PS(b).ap.rearrange("p (h m) -> p h m", m=256), PS(b).cells))
                        if SUB and 'D' in SUB:
                            return
                if SUB and 'E' in SUB:
                    return
                for mc in range(2):
                    k.dma("sp", V(mk_o.ap[mc * 128:(mc + 1) * 128, :], []), mko[:, mc, 0:512], is_output=True)
                    k.dma("sp", V(mv_o.ap[mc * 128:(mc + 1) * 128, :], []), mko[:, mc, 512:1024], is_output=True)
                k.barrier()

            def proj_fm(b, wv, wc0, src_slot, nk, c0, n):
                k.pe([mmf(PS(b, w=n), wv[:, kk, wc0:wc0 + 128], SL(src_slot + kk, c0, n), kk == 0, kk == nk - 1) for kk in range(nk)],
                     [PS(b, w=n)], [wv] + [SL(src_slot + kk, c0, n) for kk in range(nk)])

            SQ, SK, SV_ = 8, 12, 16
            SYA, SYD, SYM = 20, 24, 28

            def _ph1(p1):
                ub = k.sb(p1, [128, 2 + TP], F32, "ub")
                ue = k.sb(p1, [128, NS, 6], F32, "ue")
                hb = [k.sb(p1, [128, 512], F32, "hb") for _ in range(2)]
                cvb = [k.sb(p1, [128, 512], F32, "cvb") for _ in range(2)]
                sca_sb = k.sb(p1, [NS * 2, 512], F32, "sca_sb")
                cap_sb = k.sb(p1, [2, 512], F32, "cap_sb")
                cas_sb = k.sb(p1, [NS * 2, 512], F32, "cas_sb")
                tl = k.sb(p1, [128, 32], F32, "tl")
                k.dma("sp", sca_sb, sca)
                k.memset(ub[:, 0:2], 0.0)
                for m in range(4):
                    if m % 2 == 0:
                        wh, wc, wb = w_next()
                    wo_ = (m % 2) * 128
                    k.pe([trf(PS(PT, w=32), sca_sb[:, m * 128:(m + 1) * 128], identf[0:32, 0:32])], [PS(PT)], [sca_sb, identf])
                    k.copy(ue[:, :, 0:2], V(PS(PT, w=32).ap.rearrange("p (s i) -> p s i", i=2), [pcells[PT]]))
                    for bi, (c0, n) in enumerate(BLKS):
                        bh = pm_next()
                        proj_fm(bh, wh, wo_, XN, 8, c0, n)
                        h_ = hb[bi % 2]
                        k.copy(h_[:, 0:n], PS(bh, w=n), e="act")
                        bc_ = pm_next()
                        proj_fm(bc_, wc, wo_, XN, 8, c0, n)
                        if bi < 4:
                            k.tt(ub[:, 2 + c0: 2 + c0 + n], PS(bc_, w=n), h_[:, 0:n], ALU.mult)
                        else:
                            k.tt(ue[:, :, 2:6], V(PS(bc_, w=n).ap.rearrange("p (s t) -> p s t", t=LS), [pcells[bc_]]),
                                 V(h_.ap[:, 0:n].rearrange("p (s t) -> p s t", t=LS), h_.cells), ALU.mult)
                    w0 = vecs[:, V_CAW + 0 * 4 + m: V_CAW + 0 * 4 + m + 1]
                    w1 = vecs[:, V_CAW + 1 * 4 + m: V_CAW + 1 * 4 + m + 1]
                    w2 = vecs[:, V_CAW + 2 * 4 + m: V_CAW + 2 * 4 + m + 1]
                    for bi, (c0, n) in enumerate(BLKS):
                        cv_ = cvb[bi % 2]
                        if bi < 4:
                            k.ts(cv_[:, 0:n], ub[:, c0 + 2: c0 + 2 + n], w2, ALU.mult)
                            k.stt(cv_[:, 0:n], ub[:, c0 + 1: c0 + 1 + n], w1, cv_[:, 0:n], ALU.mult, ALU.add)
                            k.stt(cv_[:, 0:n], ub[:, c0: c0 + n], w0, cv_[:, 0:n], ALU.mult, ALU.add)
                        else:
                            cv3 = V(cv_.ap[:, 0:n].rearrange("p (s t) -> p s t", t=LS), cv_.cells)
                            k.ts(cv3, ue[:, :, 2:6], w2, ALU.mult)
                            k.stt(cv3, ue[:, :, 1:5], w1, cv3, ALU.mult, ALU.add)
                            k.stt(cv3, ue[:, :, 0:4], w0, cv3, ALU.mult, ALU.add)
                        bb = pm_next()
                        proj_fm(bb, wb, wo_, XN, 8, c0, n)
                        k.tt(SL(SYA + m, c0, n), PS(bb, w=n), cv_[:, 0:n], ALU.mult)
                    k.pe([trf(PS(PT, n=2, w=128), ub[:, TP:TP + 2], identf)], [PS(PT)], [ub, identf])
                    k.copy(cap_sb[:, m * 128:(m + 1) * 128], PS(PT, n=2, w=128), e="act")
                    k.copy(V(tl.ap.rearrange("p (s i) -> p s i", i=2), tl.cells), ue[:, :, 4:6])
                    k.pe([trf(PS(PT, n=32, w=128), tl, identf)], [PS(PT)], [tl, identf])
                    k.copy(cas_sb[:, m * 128:(m + 1) * 128], PS(PT, n=32, w=128), e="act")
                k.dma("sp", cap_o, cap_sb, is_output=True)
                k.dma("sp", cas_o, cas_sb, is_output=True)
                k.barrier()

            def _ph2(p1):
                xeb = [k.sb(p1, [128, 3 + TP], BF16, "xe") for _ in range(2)]
                seb = [k.sb(p1, [128, NS, 7], BF16, "se") for _ in range(2)]
                dg = [k.sb(p1, [128, 4, 128], BF16, "dg") for _ in range(2)]
                ktf = k.sb(p1, [128, T], F32, "ktf")
                sqb = [k.sb(p1, [128, 512], BF16, "sqb") for _ in range(2)]
                rr = [k.sb(p1, [128, 512], F32, "rr") for _ in range(2)]
                sdc_sb = k.sb(p1, [NS * 3, 1536], F32, "sdc_sb")
                dcp_t = [k.sb(p1, [3, 128], F32, "dcp_t") for _ in range(2)]
                dcs_t = [k.sb(p1, [NS * 3, 128], F32, "dcs_t") for _ in range(2)]
                x3 = [k.sb(p1, [128, 3], F32, "x3") for _ in range(2)]
                tl = [k.sb(p1, [128, 48], F32, "tl") for _ in range(2)]
                k.dma("sp", sdc_sb, sdc)
                k.memset(xeb[0][:, 0:3], 0.0)
                k.memset(xeb[1][:, 0:3], 0.0)
                PA = [PM[0], PM[1]]
                PB = [PM[2], PD[0]]
                wq_of = {}

                def stageA(j):
                    if j % 4 == 0:
                        (wq_of[j // 4],) = w_next()
                    wq = wq_of[j // 4]
                    jc = (j % 4) * 128
                    xe, se, dg_ = xeb[j % 2], seb[j % 2], dg[j % 2]
                    for i in range(4):
                        k.act(dg_[:, i, :], identf, AF.Copy, scale=vecs[:, V_DCW + i * 12 + j: V_DCW + i * 12 + j + 1])
                    k.pe([trf(PS(PT, w=48), sdc_sb[:, j * 128:(j + 1) * 128], identf[0:48, 0:48])], [PS(PT)], [sdc_sb, identf])
                    k.copy(se[:, :, 0:3], V(PS(PT, w=48).ap.rearrange("p (s i) -> p s i", i=3), [pcells[PT]]))
                    yield
                    for bi, (c0, n) in enumerate(BLKS):
                        b = PA[bi % 2]
                        proj_fm(b, wq, jc, XN, 8, c0, n)
                        if bi < 4:
                            k.copy(xe[:, 3 + c0: 3 + c0 + n], PS(b, w=n), e=("dve" if bi % 2 else "act"))
                            if bi == 3:
                                k.copy(x3[j % 2], PS(b, w=3, c0=n - 3))
                        else:
                            pv = V(PS(b, w=n).ap.rearrange("p (s t) -> p s t", t=LS), [pcells[b]])
                            k.copy(se[:, :, 3:7], pv, e="act")
                            k.copy(V(tl[j % 2].ap.rearrange("p (s i) -> p s i", i=3), tl[j % 2].cells), pv[:, :, 1:4])
                        yield
                    k.pe([trf(PS(PT, n=3, w=128), x3[j % 2], identf)], [PS(PT)], [x3[j % 2], identf])
                    k.copy(dcp_t[j % 2], PS(PT, n=3, w=128), e="act")
                    k.dma("sp", V(dcp_o.ap[:, j * 128:(j + 1) * 128], []), dcp_t[j % 2], is_output=True)
                    k.pe([trf(PS(PT, n=48, w=128), tl[j % 2], identf)], [PS(PT)], [tl[j % 2], identf])
                    k.copy(dcs_t[j % 2], PS(PT, n=48, w=128), e="act")
                    k.dma("sp", V(dcs_o.ap[:, j * 128:(j + 1) * 128], []), dcs_t[j % 2], is_output=True)
                    yield

                def stageB(j):
                    kind = j // 4
                    dst_slot = SQ + j
                    xe, se, dg_ = xeb[j % 2], seb[j % 2], dg[j % 2]
                    for bi, (c0, n) in enumerate(BLKS):
                        b = PB[bi % 2]
                        if bi < 4:
                            k.pe([mmf(PS(b, w=n), dg_[:, i, :], xe[:, c0 + i: c0 + i + n], i == 0, i == 3) for i in range(4)],
                                 [PS(b, w=n)], [dg_, xe])
                        else:
                            k.pe([mmf(V(PS(b, w=n).ap.rearrange("p (s t) -> p s t", t=LS), [pcells[b]]), dg_[:, i, :], se[:, :, i:i + 4], i == 0, i == 3) for i in range(4)],
                                 [PS(b, w=n)], [dg_, se])
                        if kind == 1:
                            k.act(ktf[:, c0:c0 + n], PS(b, w=n), AF.Silu)
                        else:
                            k.act(SL(dst_slot, c0, n), PS(b, w=n), AF.Silu)
                        yield
                    for bi, (c0, n) in enumerate(BLKS):
                        sq_ = sqb[bi % 2]
                        if kind == 0:
                            k.tt(sq_[:, 0:n], SL(dst_slot, c0, n), SL(dst_slot, c0, n), ALU.mult)
                            h = j
                            if bi < 4:
                                fns = [mmf(PS(PD[1], w=1, c0=t4), sq_[:, t4 * 128:(t4 + 1) * 128], onesb[:, 0:1], True, True) for t4 in range(4)]
                                k.pe(fns, [PS(PD[1])], [sq_, onesb])
                                k.copy(rq[:, bi * 4:(bi + 1) * 4, h], PS(PD[1], w=4))
                            else:
                                fns = [mmf(PS(PD[1], n=LS, w=1, c0=s), sq_[:, s * LS:(s + 1) * LS], onesb[:, 0:1], True, True) for s in range(NS)]
                                k.pe(fns, [PS(PD[1])], [sq_, onesb])
                                k.copy(rqs[:, :, h], PS(PD[1], n=LS, w=NS))
                            yield
                        elif kind == 1:
                            k.tt(sq_[:, 0:n], ktf[:, c0:c0 + n], ktf[:, c0:c0 + n], ALU.mult)
                            k.pe([mmf(PS(PD[1], w=n), onesb, sq_[:, 0:n], True, True)], [PS(PD[1])], [sq_, onesb])
                            r_ = rr[bi % 2]
                            k.act(r_[:, 0:n], PS(PD[1], w=n), AF.Ln, bias=EPS)
                            k.act(r_[:, 0:n], r_[:, 0:n], AF.Exp, scale=-0.5)
                            k.tt(SL(dst_slot, c0, n), ktf[:, c0:c0 + n], r_[:, 0:n], ALU.mult)
                            yield

                def run_il(gens):
                    gens = [g for g in gens if g is not None]
                    while gens:
                        for g in list(gens):
                            try:
                                next(g)
                            except StopIteration:
                                gens.remove(g)

                run_il([stageA(0)])
                for j in range(12):
                    run_il([stageA(j + 1) if j + 1 < 12 else None, stageB(j)])
                for r_, in ((V(rq.ap[:, 0:16, :], rq.cells),), (rqs,)):
                    k.act(r_, r_, AF.Ln, bias=EPS)
                    k.act(r_, r_, AF.Exp, scale=-0.5)
                    k.ts(r_, r_, 128.0 ** -0.5, ALU.mult)
                k.barrier()

            def _ph3(p1):
                eT = [k.sb(p1, [128, 2, 512], BF16, "eT") for _ in range(2)]
                rden = [k.sb(p1, [128, 512], F32, "rden") for _ in range(2)]
                kvb = [k.sb(p1, [128, 2, 1024], BF16, "kvb") for _ in range(3)]

                def kvload(s):
                    if s < NS:
                        k.dma("pool", kvb[s % 3], V(ckv.ap[s * 256:(s + 1) * 256, :].rearrange("(mc p) n -> p mc n", p=128), []))

                for s in range(3):
                    kvload(s)
                KTs = [k.sb(p1, [128, 4, 256], BF16, "KTs") for _ in range(3)]
                eTs = [k.sb(p1, [128, 4, 2, LS], BF16, "eTs") for _ in range(3)]
                rds = [k.sb(p1, [128, 4, LS], F32, "rds") for _ in range(3)]
                (wx,) = w_next()
                for h in range(4):
                    for bi, (c0, n) in enumerate(BLKS):
                        b = pm_next()
                        proj_fm(b, wx, h * 128, XN, 8, c0, n)
                        k.copy(SL(SYM + h, c0, n), PS(b, w=n), e="act")
                w_prefetch()
                sc = 128.0 ** -0.5
                it = 0
                for h in range(4):
                    for bi, (c0, n) in enumerate(BLKS[:4]):
                        e_ = eT[it % 2]
                        r_ = rden[it % 2]
                        BK = PD if it % 2 == 0 else [PM[0], PM[1], PM[2], PT]
                        it += 1
                        for mc in range(2):
                            k.pe([mmf(PS(BK[mc]), KT[:, h, mc * 128:(mc + 1) * 128], SL(SYM + h, c0, n), True, True)],
                                 [PS(BK[mc])], [KT, SL(SYM + h, c0, n)])
                            k.act(e_[:, mc, :], PS(BK[mc]), AF.Exp, scale=sc)
                        k.pe([mmf(PS(BK[2]), Vtok[:, mc, h * 128:(h + 1) * 128], e_[:, mc, :], mc == 0, mc == 1) for mc in range(2)],
                             [PS(BK[2])], [Vtok, e_])
                        k.pe([mmf(PS(BK[3]), onesb, e_[:, mc, :], mc == 0, mc == 1) for mc in range(2)],
                             [PS(BK[3])], [onesb, e_])
                        k.act(r_, PS(BK[3]), AF.Ln)
                        k.act(r_, r_, AF.Exp, scale=-1.0)
                        k.tt(SL(SYM + h, c0, n), PS(BK[2]), r_, ALU.mult)
                def satt(s, BK):
                    kv_, kts, es_, rd_ = kvb[s % 3], KTs[s % 3], eTs[s % 3], rds[s % 3]
                    ck_ = V(kv_.ap[:, :, 0:512], kv_.cells)
                    cv_ = V(kv_.ap[:, :, 512:1024], kv_.cells)
                    bT, bS = BK
                    bO, oc = bS, 64
                    for hp in range(2):
                        pt = PSB(bT)
                        fns = []
                        for hh in range(2):
                            for mc in range(2):
                                h = hp * 2 + hh
                                fns.append(trf(pt[:, hh * 256 + mc * 128: hh * 256 + (mc + 1) * 128], ck_[:, mc, h * 128:(h + 1) * 128], identb))
                        k.pe(fns, [pt], [ck_, identb])
                        k.copy(V(kts.ap[:, hp * 2:hp * 2 + 2, :], kts.cells), V(pt.ap[:, 0:512].rearrange("p (h m) -> p h m", m=256), pt.cells),
                               e=("act" if hp else "dve"))
                        yield
                    c0 = TP + s * LS
                    fns = []
                    for h in range(4):
                        for mc in range(2):
                            fns.append(mmf(PS(bS, w=LS, c0=(h * 2 + mc) * LS), kts[:, h, mc * 128:(mc + 1) * 128], SL(SYM + h, c0, LS), True, True))
                    k.pe(fns, [PS(bS)], [kts] + [SL(SYM + h, c0, LS) for h in range(4)])
                    k.act(V(es_.ap.rearrange("p h m t -> p (h m t)"), es_.cells), PS(bS, w=8 * LS), AF.Exp, scale=sc)
                    yield
                    fns = []
                    for h in range(4):
                        for mc in range(2):
                            fns.append(mmf(PS(bO, w=LS, c0=oc + h * LS), cv_[:, mc, h * 128:(h + 1) * 128], es_[:, h, mc, :], mc == 0, mc == 1))
                        for mc in range(2):
                            fns.append(mmf(PS(bO, w=LS, c0=oc + 4 * LS + h * LS), onesb, es_[:, h, mc, :], mc == 0, mc == 1))
                    k.pe(fns, [PS(bO)], [cv_, es_, onesb])
                    k.recip(V(rd_.ap.rearrange("p h t -> p (h t)"), rd_.cells), PS(bO, w=4 * LS, c0=oc + 4 * LS))
                    k.tt(SL(SYM, c0, LS, nslots=4), V(PS(bO, w=4 * LS, c0=oc).ap.rearrange("p (h t) -> p h t", t=LS), [pcells[bO]]), rd_, ALU.mult)
                    kvload(s + 3)
                    yield

                BKS = [(PT, PD[0]), (PM[0], PM[1]), (PM[2], PD[1])]
                pending = [satt(s, BKS[s % 3]) for s in range(NS)]
                active = []
                while pending or active:
                    while pending and len(active) < 3:
                        active.append(pending.pop(0))
                    for g in list(active):
                        try:
                            next(g)
                        except StopIteration:
                            active.remove(g)
                k.barrier()

            def _ph4(p1):
                (wz,) = w_next()

                def f32t(nm):
                    return k.sb(p1, [128, 4, 128], F32, nm)

                def b16t(nm):
                    return k.sb(p1, [128, 4, 128], BF16, nm)

                Gt, Gam, U_ = b16t("Gt"), b16t("Gam"), b16t("U")
                G4 = f32t("G4")
                def b16h(nm):
                    return k.sb(p1, [128, 2, 128], BF16, nm)

                Lb = [[b16h("L"), b16h("L")] for _ in range(2)]
                Fb = [[b16h("F"), b16h("F")] for _ in range(2)]
                nGh = [b16h("nG"), b16h("nG")]
                Eb = [[b16h("E"), b16h("E")] for _ in range(3)]
                QKm = [b16t("QKm") for _ in range(3)]
                kd = [b16t("kd") for _ in range(3)]
                vtok = [b16t("vtok") for _ in range(3)]
                sm = [k.sb(p1, [128, 16], F32, "sm") for _ in range(3)]
                vS, vnew, ydn = b16t("vS"), b16t("vnew"), b16t("ydn")
                o_ = f32t("o")
                S_, Sb = f32t("S"), b16t("Sb")
                S2_, Sb2 = f32t("S2"), b16t("Sb2")
                sm2 = k.sb(p1, [128, 16], F32, "sm2")
                ab = k.sb(p1, [128, 17, 8], F32, "ab")
                abs_ = k.sb(p1, [LS, NS, 8], F32, "abs")
                tmp17 = k.sb(p1, [128, 17, 4], F32, "tmp17")

                for h in range(4):
                    for bi, (c0, n) in enumerate(BLKS):
                        b = pm_next()
                        proj_fm(b, wz, h * 128, XN, 8, c0, n)
                        k.act(SL(SYD + h, c0, n), PS(b, w=n), AF.Silu)
                for tt_ in range(16):
                    k.pe([mmf(PS(PT, w=8, c0=tt_ * 8), SL(XN + kk, tt_ * 128, 128), wz[:, kk, 512:520], kk == 0, kk == 7) for kk in range(8)],
                         [PS(PT)], [wz] + [SL(XN + kk, tt_ * 128, 128) for kk in range(8)])
                k.copy(V(ab.ap[:, 0:16, :], ab.cells), V(PS(PT, w=128).ap.rearrange("p (t c) -> p t c", c=8), [pcells[PT]]))
                for s in range(NS):
                    k.pe([mmf(PS(PT, n=LS, w=8, c0=s * 8), SL(XN + kk, TP + s * LS, LS), wz[:, kk, 512:520], kk == 0, kk == 7) for kk in range(8)],
                         [PS(PT)], [wz] + [SL(XN + kk, TP + s * LS, LS) for kk in range(8)])
                k.copy(abs_, V(PS(PT, n=LS, w=128).ap.rearrange("p (t c) -> p t c", c=8), [pcells[PT]]))
                for (a_, g_, nt, npart) in ((ab, gb, 16, 128), (abs_, gbs, NS, LS)):
                    av = V(a_.ap[0:npart, 0:nt, 0:4], a_.cells)
                    bv = V(a_.ap[0:npart, 0:nt, 4:8], a_.cells)
                    gv = V(g_.ap[0:npart, 0:nt, 0:4], g_.cells)
                    gbv = V(g_.ap[0:npart, 0:nt, 4:8], g_.cells)
                    tv = V(tmp17.ap[0:npart, 0:nt, :], tmp17.cells)
                    k.act(gbv, bv, AF.Sigmoid)
                    k.tt(tv, av, vecs[0:npart, V_DTB:V_DTB + 4].us(1).bc([npart, nt, 4]), ALU.add)
                    k.act(tv, tv, AF.Exp)
                    k.act(tv, tv, AF.Ln, bias=1.0)
                    k.tt(gv, tv, negA[0:npart, :].us(1).bc([npart, nt, 4]), ALU.mult)

                def bc4(v, n, w=128):
                    return v.us(2).bc([n, 4, w])

                def mbc(m, n):
                    return V(m.ap[0:n, 0:n].unsqueeze(1).to_broadcast([n, 4, n]), m.cells)

                def P4(bank, n, w):
                    return V(psap[0:n, bank, 0:4 * w].rearrange("p (h w) -> p h w", w=w), [pcells[bank]])

                F0, F1, SA, SB = PD[0], PD[1], PD[2], PD[3]

                def front(n, c0, gtok, btok, i):
                    E_, Q_, kd_, vt_, s_ = Eb[i % 3], QKm[i % 3], kd[i % 3], vtok[i % 3], sm[i % 3]
                    L_, F_ = Lb[i % 2], Fb[i % 2]
                    kT = [SL(SK + h, c0, n) for h in range(4)]
                    qT = [SL(SQ + h, c0, n) for h in range(4)]
                    vT = [SL(SV_ + h, c0, n) for h in range(4)]
                    k.pe([mmf(PS(F0, n=n, w=4), ML[0:n, 0:n], gtok, True, True),
                          mmf(PS(F0, n=128, w=4, c0=8), onesf[0:n, :], gtok, True, True)], [PS(F0)], [ML, onesf, gtok])
                    k.copy(s_[0:n, 0:4], PS(F0, n=n, w=4), e="act")
                    k.act(s_[:, 12:16], PS(F0, w=4, c0=8), AF.Exp)
                    k.tt(s_[0:n, 8:12], PS(F0, n=n, w=4, c0=8), s_[0:n, 0:4], ALU.subtract)
                    k.act(s_[0:n, 4:8], s_[0:n, 0:4], AF.Exp)
                    k.act(s_[0:n, 8:12], s_[0:n, 8:12], AF.Exp)
                    yield
                    for h in range(4):
                        k.act(G4[0:n, h, 0:n], MG[0:n, 0:n], AF.Copy, scale=gtok[:, h:h + 1])
                    yield
                    k.pe([mmf(P4(F0, n, n)[:, h, :], G4[0:n, h, 0:n], ML[0:n, 0:n], True, True) for h in range(4)],
                         [PS(F0)], [G4, ML])
                    k.pe([mmf(P4(F1, n, n)[:, h, :], kT[h], kT[h], True, True) for h in range(4)], [PS(F1)], kT)
                    k.act(Gam[0:n, :, 0:n], P4(F0, n, n), AF.Exp)
                    yield
                    pt = PSB(PT, n)
                    k.pe([trf(pt[:, h * 128:(h + 1) * 128], kT[h], identb) for h in range(4)] +
                         [trf(pt[:, 512 + h * 128: 512 + (h + 1) * 128], vT[h], identb) for h in range(4)], [pt], kT + vT + [identb])
                    p8 = V(pt.ap.rearrange("p (a h w) -> p a h w", a=2, w=128), pt.cells)
                    k.tt(kd_[0:n], p8[:, 0], bc4(s_[0:n, 8:12], n), ALU.mult)
                    k.copy(vt_[0:n], p8[:, 1], e="act")
                    yield
                    k.pe([mmf(P4(F0, n, n)[:, h, :], kT[h], qT[h], True, True) for h in range(4)], [PS(F0)], kT + qT)
                    k.tt(Gt[0:n, :, 0:n], P4(F0, n, n), Gam[0:n, :, 0:n], ALU.mult)
                    yield
                    k.tt(Q_[0:n, :, 0:n], Gt[0:n, :, 0:n], mbc(ML, n), ALU.mult)
                    k.tt(Gam[0:n, :, 0:n], Gam[0:n, :, 0:n], mbc(MUS, n), ALU.mult)
                    yield
                    for h in range(4):
                        k.stt(U_[0:n, h, 0:n], P4(F1, n, n)[:, h, :], btok[:, h:h + 1], Gam[0:n, h, 0:n], ALU.mult, ALU.mult)
                        if h == 1:
                            yield
                    yield
                    ptl = PSB(PT, n)
                    k.pe([trf(ptl[:, h * 128: h * 128 + n], U_[0:n, h, 0:n], identb[0:n, 0:n]) for h in range(4)], [ptl], [U_, identb])
                    ptl4 = V(ptl.ap[:, 0:512].rearrange("p (h w) -> p h w", w=128)[:, :, 0:n], ptl.cells)
                    idb2 = V(identf.ap[0:n, 0:n].unsqueeze(1).to_broadcast([n, 2, n]), identf.cells)
                    k.tt(Gt[0:n, :, 0:n], U_[0:n, :, 0:n], mbc(LV(0), n), ALU.mult)
                    for hh in range(2):
                        k.copy(L_[hh][0:n, :, 0:n], ptl4[:, 2 * hh:2 * hh + 2, :], e="act")
                        k.tt(E_[hh][0:n, :, 0:n], idb2, Gt[0:n, 2 * hh:2 * hh + 2, 0:n], ALU.add)
                    k.tt(Gam[0:n, :, 0:n], ptl4, mbc(LVT(0), n), ALU.mult)
                    yield
                    for hh in range(2):
                        k.tt(F_[hh][0:n, :, 0:n], idb2, Gam[0:n, 2 * hh:2 * hh + 2, 0:n], ALU.add)
                    yield

                def solve(n, i, hh):
                    E_, L_, F_, nG = Eb[i % 3][hh], Lb[i % 2][hh], Fb[i % 2][hh], nGh[hh]
                    bank = SA if hh == 0 else SB
                    nl = n.bit_length() - 1

                    def H2(c):
                        return V(psap[0:n, bank, c * 256: c * 256 + 2 * n].rearrange("p (h w) -> p h w", w=n), [pcells[bank]])

                    m2 = lambda m: V(m.ap[0:n, 0:n].unsqueeze(1).to_broadcast([n, 2, n]), m.cells)
                    for l in range(1, nl):
                        k.pe([mmf(H2(0)[:, h, :], L_[0:n, h, 0:n], E_[0:n, h, 0:n], True, True) for h in range(2)],
                             [PS(bank)], [L_, E_])
                        k.tt(nG[0:n, :, 0:n], H2(0), m2(LV(l)), ALU.mult)
                        yield
                        fns = []
                        for h in range(2):
                            fns.append(mmf(H2(1)[:, h, :], F_[0:n, h, 0:n], identb[0:n, 0:n], True, False))
                            fns.append(mmf(H2(1)[:, h, :], F_[0:n, h, 0:n], nG[0:n, h, 0:n], False, True))
                        if l < nl - 1:
                            for h in range(2):
                                fns.append(mmf(H2(0)[:, h, :], identb[0:n, 0:n], F_[0:n, h, 0:n], True, False))
                                fns.append(mmf(H2(0)[:, h, :], nG[0:n, h, 0:n], F_[0:n, h, 0:n], False, True))
                        k.pe(fns, [PS(bank)], [F_, nG, identb])
                        k.copy(E_[0:n, :, 0:n], H2(1), e="act")
                        if l < nl - 1:
                            k.copy(F_[0:n, :, 0:n], H2(0), e="act")
                        yield

                def seq(n, c0, btok, rqtok, i, S_=S_, Sb=Sb, need_sb=True):
                    E_, Q_, kd_, vt_, s_ = Eb[i % 3], QKm[i % 3], kd[i % 3], vtok[i % 3], sm[i % 3]
                    kT = [SL(SK + h, c0, n) for h in range(4)]
                    qT = [SL(SQ + h, c0, n) for h in range(4)]
                    b1, b2, b3 = PM[0], PM[1], PM[2]
                    k.pe([mmf(P4(b1, n, 128)[:, h, :], kT[h], Sb[:, h, :], True, True) for h in range(4)], [PS(b1)], kT + [Sb])
                    k.pe([mmf(P4(b2, n, 128)[:, h, :], qT[h], Sb[:, h, :], True, True) for h in range(4)], [PS(b2)], qT + [Sb])
                    k.tt(o_[0:n], P4(b1, n, 128), bc4(s_[0:n, 4:8], n), ALU.mult)
                    yield
                    k.tt(vS[0:n], vt_[0:n], o_[0:n], ALU.subtract)
                    yield
                    k.pe([mmf(P4(b3, n, 128)[:, h, :], E_[h // 2][0:n, h % 2, 0:n], vS[0:n, h, :], True, True) for h in range(4)], [PS(b3)], [E_[0], E_[1], vS])
                    k.tt(vnew[0:n], P4(b3, n, 128), bc4(btok, n), ALU.mult)
                    k.tt(o_[0:n], P4(b2, n, 128), bc4(s_[0:n, 4:8], n), ALU.mult)
                    yield
                    k.pe([mmf(P4(b1, n, 128)[:, h, :], Q_[0:n, h, 0:n], vnew[0:n, h, :], True, True) for h in range(4)], [PS(b1)], [Q_, vnew])
                    k.pe([mmf(P4(b3, 128, 128)[:, h, :], kd_[0:n, h, :], vnew[0:n, h, :], True, True) for h in range(4)], [PS(b3)], [kd_, vnew])
                    k.tt(S_, S_, bc4(s_[:, 12:16], 128), ALU.mult)
                    yield
                    k.tt(S_, S_, P4(b3, 128, 128), ALU.add)
                    if need_sb:
                        k.copy(Sb, S_, e="act")
                    yield
                    k.tt(o_[0:n], o_[0:n], P4(b1, n, 128), ALU.add)
                    for h in range(4):
                        k.act(ydn[0:n, h, :], o_[0:n, h, :], AF.Square, accum=sm2[0:n, h:h + 1])
                    yield
                    k.tt(sm2[0:n, 4:8], rqtok, rqtok, ALU.mult)
                    k.tt(sm2[0:n, 0:4], sm2[0:n, 0:4], sm2[0:n, 4:8], ALU.mult)
                    k.act(sm2[0:n, 0:4], sm2[0:n, 0:4], AF.Ln, scale=1.0 / 128, bias=EPS)
                    k.act(sm2[0:n, 0:4], sm2[0:n, 0:4], AF.Exp, scale=-0.5)
                    k.tt(sm2[0:n, 0:4], sm2[0:n, 0:4], rqtok, ALU.mult)
                    yield
                    k.tt(o_[0:n], o_[0:n], bc4(sm2[0:n, 0:4], n), ALU.mult)
                    k.tt(ydn[0:n], o_[0:n], V(vecs.ap[0:n, V_DNN:V_DNN + 128].unsqueeze(1).to_broadcast([n, 4, 128]), vecs.cells), ALU.mult)
                    pt = PSB(PT)
                    k.pe([trf(pt[:, h * 128: h * 128 + n], ydn[0:n, h, :], identb[0:n, 0:n]) for h in range(4)], [pt], [ydn, identb])
                    k.tt(SL(SYD, c0, n, nslots=4), V(pt.ap[:, 0:512].rearrange("p (h w) -> p h w", w=128)[:, :, 0:n], pt.cells),
                         SL(SYD, c0, n, nslots=4), ALU.mult)
                    yield

                def run_interleaved(gens, strides=None):
                    items = [(g, (strides[i] if strides else 1)) for i, g in enumerate(gens) if g is not None]
                    r = 0
                    while items:
                        for it_ in list(items):
                            g, st = it_
                            if r % st:
                                continue
                            try:
                                next(g)
                            except StopIteration:
                                items.remove(it_)
                        r += 1

                k.memset(S_, 0.0)
                k.memset(Sb, 0.0)
                NCH = TP // 128

                def pfront(c):
                    return front(128, c * 128, gb[:, c, 0:4], gb[:, c, 4:8], c) if c < NCH else None

                def psolve(c, hh):
                    return solve(128, c, hh) if c < NCH else None

                run_interleaved([pfront(0)])
                run_interleaved([pfront(1), psolve(0, 0), psolve(0, 1)])
                for c in range(NCH):
                    run_interleaved([pfront(c + 2), psolve(c + 1, 0), psolve(c + 1, 1), seq(128, c * 128, gb[:, c, 4:8], rq[:, c, :], c)], [2, 1, 1, 1])
                k.dma("sp", V(sp_o.ap.rearrange("(h k) v -> k h v", k=128), []), S_, is_output=True)
                def sfront(s_i):
                    return front(LS, TP + s_i * LS, gbs[:, s_i, 0:4], gbs[:, s_i, 4:8], s_i) if s_i < NS else None

                def ssolve(s_i, hh):
                    return solve(LS, s_i, hh) if s_i < NS else None

                Sd, Sbd = [S_, S2_], [Sb, Sb2]

                def sload(s_i):
                    if s_i < NS:
                        k.dma("sp", Sd[s_i % 2], V(sdn.ap[s_i * 512:(s_i + 1) * 512, :].rearrange("(h k) v -> k h v", k=128), []))

                def sseq(s_i):
                    Sa, Sba = Sd[s_i % 2], Sbd[s_i % 2]
                    sload(s_i + 1)
                    k.copy(Sba, Sa, e="act")
                    yield
                    yield from seq(LS, TP + s_i * LS, gbs[:, s_i, 4:8], rqs[:, s_i, :], s_i, Sa, Sba, need_sb=False)
                    k.dma("sp", V(ss_o.ap[s_i * 512:(s_i + 1) * 512, :].rearrange("(h k) v -> k h v", k=128), []), Sa, is_output=True)
                    yield

                sload(0)

                run_interleaved([sfront(0)])
                run_interleaved([sfront(1), ssolve(0, 0), ssolve(0, 1)])
                for s_i in range(NS):
                    run_interleaved([sfront(s_i + 2), ssolve(s_i + 1, 0), ssolve(s_i + 1, 1), sseq(s_i)])
                k.barrier()

            SMG = 8

            def _ph5(p1):
                sg = [k.sb(p1, [128, 512], F32, "sg") for _ in range(2)]
                tm = [k.sb(p1, [128, 512], F32, "tm") for _ in range(2)]
                accf = [k.sb(p1, [128, T], F32, "accf") for _ in range(2)]
                it = 0
                ysl = [SYA, SYD, SYM]
                for mp in range(4):
                    for b3 in range(3):
                        wg, wb = w_next()
                        for mm in range(2):
                            m = mp * 2 + mm
                            wo_ = mm * 128
                            for bi, (c0, n) in enumerate(BLKS):
                                a_ = accf[mm][:, c0:c0 + n]
                                s_ = sg[it % 2]
                                t_ = tm[it % 2]
                                it += 1
                                bg = pm_next()
                                proj_fm(bg, wg, wo_, XN, 8, c0, n)
                                k.act(s_[:, 0:n], PS(bg, w=n), AF.Sigmoid)
                                bp = pm_next()
                                k.pe([mmf(PS(bp, w=n), wb[:, kk, wo_:wo_ + 128], SL(ysl[b3] + kk, c0, n), kk == 0, kk == 3) for kk in range(4)],
                                     [PS(bp, w=n)], [wb] + [SL(ysl[b3] + kk, c0, n) for kk in range(4)])
                                if b3 == 0:
                                    k.tt(a_, s_[:, 0:n], PS(bp, w=n), ALU.mult)
                                elif b3 == 1:
                                    k.tt(t_[:, 0:n], s_[:, 0:n], PS(bp, w=n), ALU.mult)
                                    k.tt(a_, a_, t_[:, 0:n], ALU.add)
                                else:
                                    k.tt(t_[:, 0:n], s_[:, 0:n], PS(bp, w=n), ALU.mult)
                                    k.tt(SL(SMG + m, c0, n), a_, t_[:, 0:n], ALU.add)
                for nm_, sl_ in (("ya", SYA), ("yd", SYD), ("ym", SYM), ("mg", SMG)):
                    tap(nm_ + "0", SL(sl_, 0, T))
                    tap(nm_ + "3", SL(sl_ + 3, 0, T))
                k.barrier()

            SX = 16
            def _ph6(p2):
                x1s = k.sb(p2, [128, D], F32, "x1s")
                xs = [k.sb(p2, [128, D], BF16, "xs2") for _ in range(2)]
                junk = k.sb(p2, [128, D], BF16, "junk2")
                ssq = [k.sb(p2, [128, 1], F32, "ssq2") for _ in range(2)]
                sgl = [k.sb(p2, [128, 512], F32, "sgl") for _ in range(2)]
                yt = [k.sb(p2, [128, D], F32, "yt") for _ in range(2)]
                gfin = k.sb(p2, [128, D], F32, "gfin")
                k.dma("sp", gfin, gfin_d)
                X1BASE = SX * (SLOTW // 2)

                def X1(tt_, n, c0=0, w=D):
                    if tt_ < 16:
                        a0 = X1BASE + tt_ * D + c0
                        return V(arena32[0:n, a0:a0 + w], cells_bytes(SX, (tt_ * D + c0) * 4, (tt_ * D + c0 + w) * 4))
                    return x1s[0:n, c0:c0 + w]

                (woA,) = w_next()
                (woB,) = w_next(prefetch=False)
                wo2 = [woA, woB]
                def p2tile(tt_):
                    n = 128 if tt_ < 16 else 64
                    t0 = tt_ * 128
                    for half in range(2):
                        b = (PM[half] if tt_ % 2 == 0 else PD[half])
                        k.pe([mmf(PS(b, n=n), SL(SMG + kk, t0, n), wo2[half][:, kk, :], kk == 0, kk == 7) for kk in range(8)],
                             [PS(b)], [wo2[half]] + [SL(SMG + kk, t0, n) for kk in range(8)])
                        xv = X1(tt_, n, half * 512, 512)
                        k.tt(xv, xv, PS(b, n=n), ALU.add)
                        yield
                    q_, s_ = ssq[tt_ % 2], xs[tt_ % 2]
                    k.act(junk[0:n, :], X1(tt_, n), AF.Square, accum=q_[0:n, :])
                    k.act(q_[0:n, :], q_[0:n, :], AF.Sqrt, scale=1.0 / D, bias=EPS)
                    yield
                    k.recip(q_[0:n, :], q_[0:n, :])
                    k.ts(s_[0:n, :], X1(tt_, n), q_[0:n, 0:1], ALU.mult)
                    yield
                    pt = PSB(PT if tt_ % 2 == 0 else PD[2])
                    k.pe([trf(pt[:, kk * 128: kk * 128 + n], s_[0:n, kk * 128:(kk + 1) * 128], identb[0:n, 0:n]) for kk in range(8)],
                         [pt], [s_, identb])
                    ptv = V(pt.ap.rearrange("p (k t) -> p k t", t=128)[:, :, 0:n], pt.cells)
                    k.tt(SL(XN, t0, n, nslots=8), ptv, vecs[:, V_GFFN:V_GFFN + 8].us(2).bc([128, 8, n]), ALU.mult)
                    yield

                for tt_ in range(17):
                    n_ = 128 if tt_ < 16 else 64
                    k.dma("sp", X1(tt_, n_), V(xin.ap[tt_ * 128: tt_ * 128 + n_, :], []))
                run_window([p2tile(tt_) for tt_ in range(17)], 2)
                w_issue(wstate["used"] + 1)
                SACT = 8
                it = 0
                for (j0, nj) in FFN_PARTS:
                    for jj in range(nj):
                        if jj % 2 == 0:
                            wg, wu = w_next()
                        wo_ = (jj % 2) * 128
                        for bi, (c0, n) in enumerate(BLKS):
                            bg = pm_next()
                            proj_fm(bg, wg, wo_, XN, 8, c0, n)
                            s_ = sgl[it % 2]
                            it += 1
                            k.act(s_[:, 0:n], PS(bg, w=n), AF.Silu)
                            bu = pm_next()
                            proj_fm(bu, wu, wo_, XN, 8, c0, n)
                            k.tt(SL(SACT + jj, c0, n), s_[:, 0:n], PS(bu, w=n), ALU.mult)
                    for half in range(2):
                        (wd,) = w_next()
                        for tt_ in range(17):
                            n = 128 if tt_ < 16 else 64
                            t0 = tt_ * 128
                            b = pm_next()
                            k.pe([mmf(PS(b, n=n), SL(SACT + kk, t0, n), wd[:, kk, :], kk == 0, kk == nj - 1) for kk in range(nj)],
                                 [PS(b)], [wd] + [SL(SACT + kk, t0, n) for kk in range(nj)])
                            xv = X1(tt_, n, half * 512, 512)
                            k.tt(xv, xv, PS(b, n=n), ALU.add)
                def p4tile(tt_):
                    n = 128 if tt_ < 16 else 64
                    q_, y_ = ssq[tt_ % 2], yt[tt_ % 2]
                    k.act(junk[0:n, :], X1(tt_, n), AF.Square, accum=q_[0:n, :])
                    k.act(q_[0:n, :], q_[0:n, :], AF.Sqrt, scale=1.0 / D, bias=EPS)
                    yield
                    k.recip(q_[0:n, :], q_[0:n, :])
                    k.stt(y_[0:n, :], X1(tt_, n), q_[0:n, 0:1], gfin[0:n, :], ALU.mult, ALU.mult)
                    yield
                    k.dma("sp", V(y_o.ap[tt_ * 128: tt_ * 128 + n, :], []), y_[0:n, :], is_output=True)
                    yield

                run_window([p4tile(tt_) for tt_ in range(17)], 2)

            for _i, _ph in enumerate([_ph0, _ph1, _ph2, _ph3, _ph4, _ph5, _ph6]):
                with contextlib.ExitStack() as _pes:
                    _ph(_pes)
                PHASE_MARKS.append((_i, k.npe, dict(k.cnt)))
                if STOP == _i + 1:
                    break
            else:
                assert wstate["used"] == len(jobs), (wstate, len(jobs))
        except _Stop:
            pass
        for ev in k.out_events:
            k.need("sp", ev)
        k.barrier(engines=("sp",))
    return nc


_NC = None


def _consts():
    c = np.zeros((128, NCST), np.float32)
    i = np.arange(128)
    c[:, K_ID:K_ID + 128] = np.eye(128)
    c[:, K_ONE:K_ONE + 128] = 1.0
    c[:, K_ML:K_ML + 128] = (i[:, None] <= i[None, :])
    c[:, K_MG:K_MG + 128] = (i[:, None] > i[None, :])
    c[:, K_MUS:K_MUS + 128] = (i[:, None] < i[None, :])
    for l in range(7):
        m = ((i[:, None] >> (l + 1)) == (i[None, :] >> (l + 1))) & (((i[:, None] >> l) & 1) == 0) & (((i[None, :] >> l) & 1) == 1)
        c[:, K_LV + l * 128: K_LV + (l + 1) * 128] = -m.astype(np.float32)
        c[:, K_LV + (7 + l) * 128: K_LV + (8 + l) * 128] = -m.T.astype(np.float32)
    return c


def kernel(x_prompt, x_sample, mem_prompt, state_conv_a, state_dn_conv, state_dn, cache_mem_k,
           cache_mem_v, norm_mix, w_in, conv_a_w, dn_conv_w, dn_a_log, dn_dt_bias, dn_norm,
           norm_mem, w_mem_kv, w_branch, w_o, norm_ffn, w_ffn_up, w_ffn_down, norm_final):
    global _NC
    f = lambda a: np.ascontiguousarray(np.asarray(a, dtype=np.float32))
    x_prompt, x_sample, mem_prompt = f(x_prompt), f(x_sample), f(mem_prompt)
    if _NC is None:
        _NC = build_nc()
    vec = np.zeros((128, NVEC), np.float32)
    vec[:, V_GMIX:V_GMIX + 8] = f(norm_mix)[0].reshape(8, 128).T
    vec[:, V_GMEM:V_GMEM + 8] = f(norm_mem)[0].reshape(8, 128).T
    vec[:, V_GFFN:V_GFFN + 8] = f(norm_ffn)[0].reshape(8, 128).T
    vec[:, V_CAW:V_CAW + 12] = f(conv_a_w)[0].reshape(3, 4, 128).transpose(2, 0, 1).reshape(128, 12)
    vec[:, V_DCW:V_DCW + 48] = f(dn_conv_w)[0].reshape(4, 12, 128).transpose(2, 0, 1).reshape(128, 48)
    vec[:, V_ALOG:V_ALOG + 4] = f(dn_a_log)[0][None, :]
    vec[:, V_DTB:V_DTB + 4] = f(dn_dt_bias)[0][None, :]
    vec[:, V_DNN:V_DNN + 128] = f(dn_norm)[0][None, :]
    cst = _consts()
    shared = dict(w_in=f(w_in)[0], w_kv=f(w_mem_kv)[0], w_br=f(w_branch)[0], w_o=f(w_o)[0],
                  w_up=f(w_ffn_up)[0], w_dn=f(w_ffn_down)[0], vecs=vec, cst=cst,
                  gfin=np.ascontiguousarray(np.broadcast_to(f(norm_final)[None, :], (128, D))))
    sca, sdc, sdn = f(state_conv_a)[0], f(state_dn_conv)[0], f(state_dn)[0]
    ck, cv = f(cache_mem_k)[0], f(cache_mem_v)[0]
    in_maps = []
    for c in range(NCORES):
        s0, s1 = c * NS, (c + 1) * NS
        m = dict(shared)
        m["xin"] = np.ascontiguousarray(np.concatenate([x_prompt[c], x_sample[s0:s1].reshape(NS * LS, D)], axis=0))
        m["mem"] = mem_prompt[c]
        m["sca"] = np.ascontiguousarray(sca[s0:s1].reshape(NS * 2, 512))
        m["sdc"] = np.ascontiguousarray(sdc[s0:s1].reshape(NS * 3, 1536))
        m["sdn"] = np.ascontiguousarray(sdn[s0:s1].reshape(NS * 512, 128))
        m["ckv"] = np.ascontiguousarray(np.concatenate([ck[s0:s1].reshape(NS * 256, 512), cv[s0:s1].reshape(NS * 256, 512)], axis=1))
        in_maps.append(m)
    if DBG_CORES:
        res = run_bass_kernel_spmd(_NC, in_maps[:DBG_CORES], core_ids=list(range(DBG_CORES)))
        R = list(res.results) + [res.results[0]] * (NCORES - DBG_CORES)
        global LAST_R
        LAST_R = res.results[0]
    else:
        res = run_bass_kernel_spmd(_NC, in_maps, core_ids=list(range(NCORES)))
        R = res.results
    y_prompt = np.stack([R[c]["y"][:TP] for c in range(NCORES)])
    y_sample = np.concatenate([R[c]["y"][TP:].reshape(NS, LS, D) for c in range(NCORES)], axis=0)
    ca_p = np.stack([R[c]["ca_p"] for c in range(NCORES)])[None]
    dc_p = np.stack([R[c]["dc_p"] for c in range(NCORES)])[None]
    s_p = np.stack([R[c]["s_p"].reshape(4, 128, 128) for c in range(NCORES)])[None]
    mk_p = np.stack([R[c]["mk"].reshape(256, 4, 128) for c in range(NCORES)])[None]
    mv_p = np.stack([R[c]["mv"].reshape(256, 4, 128) for c in range(NCORES)])[None]
    ca_s = np.concatenate([R[c]["ca_s"].reshape(NS, 2, 512) for c in range(NCORES)], axis=0)[None]
    dc_s = np.concatenate([R[c]["dc_s"].reshape(NS, 3, 1536) for c in range(NCORES)], axis=0)[None]
    s_s = np.concatenate([R[c]["s_s"].reshape(NS, 4, 128, 128) for c in range(NCORES)], axis=0)[None]
    return tuple(np.ascontiguousarray(a, dtype=np.float32) for a in
                 (y_prompt, y_sample, ca_p, dc_p, s_p, mk_p, mv_p, ca_s, dc_s, s_s))
```

```python
import contextlib
import numpy as np
import concourse.bass as bass
import concourse.mybir as mybir
from concourse.bass_utils import run_bass_kernel_spmd

F32 = mybir.dt.float32
BF16 = mybir.dt.bfloat16
F32R = mybir.dt.float32r
AF = mybir.ActivationFunctionType
ALU = mybir.AluOpType
AX = mybir.AxisListType

NCORES = 8
D = 1024
TP = 2048
NS = 16
LS = 4
T = TP + NS * LS
SLOTW = T
NSLOT = 32
BLKS = [(0, 512), (512, 512), (1024, 512), (1536, 512), (2048, 64)]
EPS = 1e-6
NRING = 32
WBUF = 6144
INW = 7176
C_B, C_C, C_H, C_Q, C_K, C_V, C_Z, C_AB, C_XQ, C_GA = 0, 512, 1024, 1536, 2048, 2560, 3072, 3584, 3592, 4104
V_GMIX, V_GMEM, V_GFFN, V_CAW, V_DCW, V_ALOG, V_DTB, V_DNN = 0, 8, 16, 24, 36, 84, 88, 92
NVEC = 220
K_ID, K_ONE, K_ML, K_MG, K_MUS, K_LV = 0, 128, 256, 384, 512, 640
NCST = 640 + 14 * 128


class Cell:
    __slots__ = ("w", "r", "x")

    def __init__(self, x=False):
        self.w = None
        self.r = {}
        self.x = x


class V:
    def __init__(self, ap, cells):
        self.ap = ap
        self.cells = cells

    def __getitem__(self, idx):
        return V(self.ap[idx], self.cells)

    def bc(self, shape):
        return V(self.ap.to_broadcast(shape), self.cells)

    def us(self, ax):
        return V(self.ap.unsqueeze(ax), self.cells)


class KB:
    def __init__(self, nc, es):
        self.nc = nc
        self.es = es
        self.eng = dict(pe=nc.tensor, act=nc.scalar, dve=nc.vector, pool=nc.gpsimd, sp=nc.sync)
        self.sem = {e: es.enter_context(nc.semaphore("s_" + e)) for e in ("pe", "act", "dve", "pool")}
        self.cnt = dict.fromkeys(self.sem, 0)
        self.known = {e: {} for e in self.eng}
        self.ring = {q: [es.enter_context(nc.semaphore("r_%s%d" % (q, i))) for i in range(NRING)] for q in ("sp", "pool")}
        self.ring_val = {q: [0] * NRING for q in self.ring}
        self.ring_pos = {q: 0 for q in self.ring}
        self.out_events = []
        self.snap = {}
        self.npe = 0
        self.nalloc = 0

    def sb(self, es, shape, dt, name=None):
        self.nalloc += 1
        t = es.enter_context(self.nc.sbuf_tensor("%s_%d" % (name or "t", self.nalloc), list(shape), dt))
        return V(t[:], [Cell()])

    def need(self, e, ev):
        if ev is None:
            return
        key, sem, val = ev
        k = self.known[e]
        if k.get(key, 0) >= val:
            return
        self.eng[e].wait_ge(sem, val)
        k[key] = val
        sn = self.snap.get((key, val))
        if sn:
            for kk, vv in sn.items():
                if k.get(kk, 0) < vv:
                    k[kk] = vv

    def deps(self, e, outs, ins):
        for v in ins:
            for c in v.cells:
                self.need(e, c.w)
                if c.x:
                    for ev in c.r.values():
                        if ev[0] != e:
                            self.need(e, ev)
        for v in outs:
            for c in v.cells:
                if c.w is not None and not (c.w[0] == e and e in ("pe", "act", "dve")):
                    self.need(e, c.w)
                for ev in c.r.values():
                    if not (ev[0] == e and e in ("pe", "act", "dve")):
                        self.need(e, ev)

    def commit(self, ev, outs, ins, issuer=None):
        if issuer is not None:
            self.snap[(ev[0], ev[2])] = dict(self.known[issuer])
        for v in ins:
            for c in v.cells:
                c.r[ev[0]] = ev
        for v in outs:
            for c in v.cells:
                c.w = ev
                c.r = {}

    def op(self, e, fn, outs, ins):
        self.deps(e, outs, ins)
        i = fn(self.eng[e])
        self.cnt[e] += 1
        i.then_inc(self.sem[e], 1)
        self.commit((e, self.sem[e], self.cnt[e]), outs, ins, issuer=e)

    def pe(self, fns, outs, ins):
        self.deps("pe", outs, ins)
        i = None
        for fn in fns:
            i = fn(self.nc.tensor)
            self.npe += 1
        self.cnt["pe"] += 1
        i.then_inc(self.sem["pe"], 1)
        self.commit(("pe", self.sem["pe"], self.cnt["pe"]), outs, ins, issuer="pe")

    def dma(self, q, out, in_, is_output=False, **kw):
        self.deps(q, [out], [in_])
        i = self.ring_pos[q]
        self.ring_pos[q] = (i + 1) % NRING
        sem = self.ring[q][i]
        key = "r_%s%d" % (q, i)
        prev = self.ring_val[q][i]
        if prev:
            self.need(q, (key, sem, prev))
        val = prev + 16
        self.eng[q].dma_start(out=out.ap, in_=in_.ap, **kw).then_inc(sem, 16)
        self.ring_val[q][i] = val
        ev = (key, sem, val)
        self.commit(ev, [out], [in_], issuer=q)
        if is_output:
            self.out_events.append(ev)
        return ev

    def barrier(self, engines=("pe", "act", "dve", "sp", "pool"), pool_dma=False):
        evs = [(e, self.sem[e], self.cnt[e]) for e in ("pe", "act", "dve") if self.cnt[e]]
        for q in (("sp", "pool") if pool_dma else ("sp",)):
            for i in range(NRING):
                if self.ring_val[q][i]:
                    evs.append(("r_%s%d" % (q, i), self.ring[q][i], self.ring_val[q][i]))
        for e in engines:
            for ev in evs:
                if ev[0] != e:
                    self.need(e, ev)

    def act(self, out, in_, func, scale=1.0, bias=0.0, accum=None, extra_in=()):
        kw = {}
        ins = [in_] + list(extra_in)
        if isinstance(scale, V):
            ins.append(scale)
            scale = scale.ap
        if isinstance(bias, V):
            ins.append(bias)
            bias = bias.ap
        outs = [out]
        if accum is not None:
            outs.append(accum)
            kw["accum_out"] = accum.ap
        self.op("act", lambda e: e.activation(out=out.ap, in_=in_.ap, func=func, scale=scale, bias=bias, **kw), outs, ins)

    def tt(self, out, a, b, op, e="dve"):
        self.op(e, lambda en: en.tensor_tensor(out=out.ap, in0=a.ap, in1=b.ap, op=op), [out], [a, b])

    def ts(self, out, a, s1, op0, s2=None, op1=None, e="dve"):
        ins = [a]
        if isinstance(s1, V):
            ins.append(s1)
            s1 = s1.ap
        if isinstance(s2, V):
            ins.append(s2)
            s2 = s2.ap
        kw = {}
        if op1 is not None:
            kw["op1"] = op1
        self.op(e, lambda en: en.tensor_scalar(out=out.ap, in0=a.ap, scalar1=s1, scalar2=s2, op0=op0, **kw), [out], ins)

    def stt(self, out, a, s, b, op0, op1):
        ins = [a, b]
        if isinstance(s, V):
            ins.append(s)
            s = s.ap
        self.op("dve", lambda en: en.scalar_tensor_tensor(out=out.ap, in0=a.ap, scalar=s, in1=b.ap, op0=op0, op1=op1), [out], ins)

    def copy(self, out, in_, e="dve"):
        if e == "act":
            self.op("act", lambda en: en.copy(out=out.ap, in_=in_.ap), [out], [in_])
        else:
            self.op(e, lambda en: en.tensor_copy(out=out.ap, in_=in_.ap), [out], [in_])

    def recip(self, out, in_):
        self.op("dve", lambda en: en.reciprocal(out=out.ap, in_=in_.ap), [out], [in_])

    def memset(self, out, val, e="dve"):
        self.op(e, lambda en: en.memset(out.ap, val), [out], [])

    def rsum(self, out, in_):
        self.op("dve", lambda en: en.reduce_sum(out=out.ap, in_=in_.ap, axis=AX.X), [out], [in_])


def run_window(gens, width):
    pending = list(gens)
    active = []
    while pending or active:
        while pending and len(active) < width:
            active.append(pending.pop(0))
        for g in list(active):
            try:
                next(g)
            except StopIteration:
                active.remove(g)


def mmf(out, lhsT, rhs, start, stop):
    return lambda pe: pe.matmul(out.ap, lhsT=lhsT.ap, rhs=rhs.ap, start=start, stop=stop)


def trf(out, in_, ident):
    return lambda pe: pe.transpose(out=out.ap, in_=in_.ap, identity=ident.ap)


class _Stop(Exception):
    pass


STOP = None
SUB = None
DBG_CORES = None
DBG_TAPS = False
LAST_R = None
PHASE_MARKS = []


def build_nc():
    nc = bass.Bass("TRN2", target_bir_lowering=False)

    def din(name, shape):
        return V(nc.dram_tensor(name, list(shape), F32, kind="ExternalInput").ap(), [])

    def dout(name, shape):
        return V(nc.dram_tensor(name, list(shape), F32, kind="ExternalOutput").ap(), [])

    xin = din("xin", [T, D])
    mem = din("mem", [256, D])
    sca = din("sca", [NS * 2, 512])
    sdc = din("sdc", [NS * 3, 1536])
    sdn = din("sdn", [NS * 4 * 128, 128])
    ckv = din("ckv", [NS * 256, 1024])
    w_in = din("w_in", [D, INW])
    w_kv = din("w_kv", [D, 1024])
    w_br = din("w_br", [1536, D])
    w_o = din("w_o", [D, D])
    w_up = din("w_up", [D, 5632])
    w_dn = din("w_dn", [2816, D])
    vecs_d = din("vecs", [128, NVEC])
    cst_d = din("cst", [128, NCST])
    gfin_d = din("gfin", [128, D])
    y_o = dout("y", [T, D])
    cap_o = dout("ca_p", [2, 512])
    dcp_o = dout("dc_p", [3, 1536])
    sp_o = dout("s_p", [512, 128])
    mk_o = dout("mk", [256, 512])
    mv_o = dout("mv", [256, 512])
    cas_o = dout("ca_s", [NS * 2, 512])
    dcs_o = dout("dc_s", [NS * 3, 1536])
    ss_o = dout("s_s", [NS * 512, 128])

    with contextlib.ExitStack() as es:
        k = KB(nc, es)
        arena_t = es.enter_context(nc.sbuf_tensor("arena", [128, NSLOT * SLOTW], BF16))
        arena = arena_t[:]
        arena3 = arena.rearrange("p (s t) -> p s t", t=SLOTW)
        arena32 = arena.bitcast(F32)
        acells = [[Cell() for _ in range(5)] for _ in range(NSLOT)]

        def cells_bytes(slot, b0, b1):
            out = []
            while b0 < b1:
                s = slot + b0 // (SLOTW * 2)
                bb = b0 % (SLOTW * 2)
                ci = min(bb // 1024, 4)
                out.append(acells[s][ci])
                nxt = (bb // 1024 + 1) * 1024 if ci < 4 else SLOTW * 2
                b0 += nxt - bb
            return out

        def SL(slot, c0, n, nslots=1):
            cs = []
            for s in range(slot, slot + nslots):
                cs += cells_bytes(s, c0 * 2, (c0 + n) * 2)
            if nslots == 1:
                return V(arena3[:, slot, c0:c0 + n], cs)
            return V(arena3[:, slot:slot + nslots, c0:c0 + n], cs)

        def SL32(slot, c0, n):
            base = slot * (SLOTW // 2)
            return V(arena32[:, base + c0: base + c0 + n], cells_bytes(slot, c0 * 4, (c0 + n) * 4))

        ps_t = es.enter_context(nc.psum_tensor("ps", [128, 8, 512], F32))
        psap = ps_t[:]
        pcells = [Cell(True) for _ in range(8)]

        def PS(bank, n=128, w=512, c0=0):
            return V(psap[0:n, bank, c0:c0 + w], [pcells[bank]])

        def PSB(bank, n=128):
            return V(psap[0:n, bank, :].bitcast(BF16), [pcells[bank]])

        def PS2(bank, n=128):
            return V(psap[0:n, bank:bank + 2, :], [pcells[bank], pcells[bank + 1]])

        PM = [0, 1, 2]
        PT = 3
        PD = [4, 5, 6, 7]

        ring = [k.sb(es, [128, WBUF], BF16, "wring") for _ in range(2)]
        vecs = k.sb(es, [128, NVEC], F32, "vecs")
        cst = k.sb(es, [128, NCST], F32, "cst")
        cstb = k.sb(es, [128, 256], BF16, "cstb")
        gb = k.sb(es, [128, 17, 8], F32, "gb")
        gbs = k.sb(es, [LS, NS, 8], F32, "gbs")
        rq = k.sb(es, [128, 17, 4], F32, "rq")
        rqs = k.sb(es, [LS, NS, 4], F32, "rqs")
        negA = k.sb(es, [128, 4], F32, "negA")
        KT = k.sb(es, [128, 4, 256], BF16, "KT")
        Vtok = k.sb(es, [128, 2, 512], BF16, "Vtok")

        k.dma("sp", vecs, vecs_d)
        k.dma("sp", cst, cst_d)
        k.copy(cstb, cst[:, 0:256])
        identf = cst[:, K_ID:K_ID + 128]
        onesf = cst[:, K_ONE:K_ONE + 128]
        ML = cst[:, K_ML:K_ML + 128]
        MG = cst[:, K_MG:K_MG + 128]
        MUS = cst[:, K_MUS:K_MUS + 128]
        identb = cstb[:, 0:128]
        onesb = cstb[:, 128:256]

        def LV(l):
            return cst[:, K_LV + l * 128: K_LV + (l + 1) * 128]

        def LVT(l):
            return cst[:, K_LV + (7 + l) * 128: K_LV + (8 + l) * 128]

        k.act(negA, vecs[:, V_ALOG:V_ALOG + 4], AF.Exp)
        k.ts(negA, negA, -1.0, ALU.mult)

        jobs = []

        def wsrc(w, r0, nk, c0, nc_):
            return V(w.ap[r0:r0 + nk * 128, c0:c0 + nc_].rearrange("(kc p) n -> p kc n", p=128), [])

        jobs.append([(wsrc(w_kv, 0, 8, 0, 512), 8, 512)])
        jobs.append([(wsrc(w_kv, 0, 8, 512, 512), 8, 512)])
        for mp in range(2):
            jobs.append([(wsrc(w_in, 0, 8, C_H + mp * 256, 256), 8, 256),
                         (wsrc(w_in, 0, 8, C_C + mp * 256, 256), 8, 256),
                         (wsrc(w_in, 0, 8, C_B + mp * 256, 256), 8, 256)])
        for j in range(3):
            jobs.append([(wsrc(w_in, 0, 8, C_Q + j * 512, 512), 8, 512)])
        jobs.append([(wsrc(w_in, 0, 8, C_XQ, 512), 8, 512)])
        jobs.append([(wsrc(w_in, 0, 8, C_Z, 520), 8, 520)])
        for mp in range(4):
            for b in range(3):
                jobs.append([(wsrc(w_in, 0, 8, C_GA + b * 1024 + mp * 256, 256), 8, 256),
                             (wsrc(w_br, b * 512, 4, mp * 256, 256), 4, 256)])
        for j in range(2):
            jobs.append([(wsrc(w_o, 0, 8, j * 512, 512), 8, 512)])
        FFN_PARTS = [(0, 8), (8, 8), (16, 6)]
        for (j0, nj) in FFN_PARTS:
            for j in range(j0, j0 + nj, 2):
                jobs.append([(wsrc(w_up, 0, 8, j * 128, 256), 8, 256),
                             (wsrc(w_up, 0, 8, 2816 + j * 128, 256), 8, 256)])
            for half in range(2):
                jobs.append([(wsrc(w_dn, j0 * 128, nj, half * 512, 512), nj, 512)])
        wstate = dict(issued=0, used=0)

        def w_issue(upto):
            while wstate["issued"] < min(upto, len(jobs)):
                ji = wstate["issued"]
                buf = ring[ji % 2]
                off = 0
                for (src, nk, ncol) in jobs[ji]:
                    dst = V(buf.ap[:, off:off + nk * ncol].rearrange("p (k n) -> p k n", n=ncol), buf.cells)
                    k.dma("pool", dst, src)
                    off += nk * ncol
                wstate["issued"] += 1

        def w_next(prefetch=True):
            ji = wstate["used"]
            w_issue(ji + (2 if prefetch else 1))
            buf = ring[ji % 2]
            views = []
            off = 0
            for (src, nk, ncol) in jobs[ji]:
                views.append(V(buf.ap[:, off:off + nk * ncol].rearrange("p (k n) -> p k n", n=ncol), buf.cells))
                off += nk * ncol
            wstate["used"] += 1
            return views

        def w_prefetch():
            w_issue(wstate["used"] + 1)

        w_issue(2)

        def tap(name, v):
            if not DBG_TAPS:
                return
            shp = list(v.ap.shape)
            d_ = V(nc.dram_tensor("tap_" + name, shp, v.ap.dtype, kind="ExternalOutput").ap(), [])
            k.dma("sp", d_, v, is_output=True)

        pm_rr = [0]

        def pm_next():
            b = PM[pm_rr[0] % 3]
            pm_rr[0] += 1
            return b

        try:
            XN = 0
            def _ph0(p0):
                xt = [k.sb(p0, [128, D], F32, "xt") for _ in range(4)]
                xs = [k.sb(p0, [128, D], BF16, "xs") for _ in range(2)]
                junk = k.sb(p0, [128, D], BF16, "junk")
                ssq = [k.sb(p0, [128, 1], F32, "ssq") for _ in range(2)]
                rst = [k.sb(p0, [128, 1], F32, "rst") for _ in range(2)]
                memT = k.sb(p0, [128, 8, 256], BF16, "memT")
                mko = k.sb(p0, [128, 2, 1024], F32, "mko")

                def norm_T(src_d, r0, n, gcol, dst, i):
                    x_, s_, q_, r_ = xt[i % 4], xs[i % 2], ssq[i % 2], rst[i % 2]
                    k.act(junk[0:n, :], x_[0:n, :], AF.Square, accum=q_[0:n, :])
                    k.act(r_[0:n, :], q_[0:n, :], AF.Sqrt, scale=1.0 / D, bias=EPS)
                    yield
                    k.recip(r_[0:n, :], r_[0:n, :])
                    k.ts(s_[0:n, :], x_[0:n, :], r_[0:n, 0:1], ALU.mult)
                    yield
                    pt = PSB(PT if i % 2 == 0 else PD[0])
                    k.pe([trf(pt[:, kk * 128: kk * 128 + n], s_[0:n, kk * 128:(kk + 1) * 128], identb[0:n, 0:n]) for kk in range(8)],
                         [pt], [s_, identb])
                    ptv = V(pt.ap.rearrange("p (k t) -> p k t", t=128)[:, :, 0:n], pt.cells)
                    k.tt(dst, ptv, vecs[:, gcol:gcol + 8].us(2).bc([128, 8, n]), ALU.mult)
                    xload(i + 4)
                    yield

                srcs0 = [(xin, tt_ * 128, 128 if tt_ < 16 else 64) for tt_ in range(17)] + [(mem, mc * 128, 128) for mc in range(2)]

                def xload(i):
                    if i < len(srcs0):
                        sd, r0, n = srcs0[i]
                        k.dma("sp", xt[i % 4][0:n, :], V(sd.ap[r0:r0 + n, :], []))

                for i_ in range(4):
                    xload(i_)
                if SUB and '0' in SUB:
                    return
                gens0 = []
                for tt_ in range(17):
                    n = 128 if tt_ < 16 else 64
                    gens0.append(norm_T(xin, tt_ * 128, n, V_GMIX, SL(XN, tt_ * 128, n, nslots=8), tt_))
                if SUB and 'A' in SUB:
                    return
                for mc in range(2):
                    gens0.append(norm_T(mem, mc * 128, 128, V_GMEM, memT[:, :, mc * 128:(mc + 1) * 128], 17 + mc))
                run_window(gens0, 2)
                if SUB and 'B' in SUB:
                    return

                for part in range(2):
                    (wv,) = w_next()
                    if SUB and 'F' in SUB and part == 1:
                        return
                    for mc in range(2):
                        b = pm_next()
                        k.pe([mmf(PS(b), memT[:, kk, mc * 128:(mc + 1) * 128], wv[:, kk, :], kk == 0, kk == 7) for kk in range(8)],
                             [PS(b)], [memT, wv])
                        if not (SUB and 'H' in SUB and part == 1):
                            k.copy(mko[:, mc, part * 512:(part + 1) * 512], PS(b), e=("dve" if (SUB and 'I' in SUB) else "act"))
                        if part == 1 and not (SUB and 'G' in SUB):
                            k.copy(Vtok[:, mc, :], mko[:, mc, 512:1024])
                    if SUB and 'C' in SUB:
                        return
                    if part == 0:
                        for hp in range(2):
                            b = pm_next()
                            fns = []
                            for hh in range(2):
                                h = hp * 2 + hh
                                fns += [mmf(PS(b, w=256, c0=hh * 256), wv[:, kk, h * 128:(h + 1) * 128], memT[:, kk, :], kk == 0, kk == 7) for kk in range(8)]
                            k.pe(fns, [PS(b)], [memT, wv])
                            k.copy(V(KT.ap[:, hp * 2:hp * 2 + 2, :], KT.cells), V(PS(b).ap.rearrange("p (h m) -> p h m", m=256), PS(b).cells))
                        if SUB and 'D' in SUB:
                            return
                if SUB and 'E' in SUB:
                    return
                for mc in range(2):
                    k.dma("sp", V(mk_o.ap[mc * 128:(mc + 1) * 128, :], []), mko[:, mc, 0:512], is_output=True)
                    k.dma("sp", V(mv_o.ap[mc * 128:(mc + 1) * 128, :], []), mko[:, mc, 512:1024], is_output=True)
                k.barrier()

            def proj_fm(b, wv, wc0, src_slot, nk, c0, n):
                k.pe([mmf(PS(b, w=n), wv[:, kk, wc0:wc0 + 128], SL(src_slot + kk, c0, n), kk == 0, kk == nk - 1) for kk in range(nk)],
                     [PS(b, w=n)], [wv] + [SL(src_slot + kk, c0, n) for kk in range(nk)])

            SQ, SK, SV_ = 8, 12, 16
            SYA, SYD, SYM = 20, 24, 28

            def _ph1(p1):
                ub = k.sb(p1, [128, 2 + TP], F32, "ub")
                ue = k.sb(p1, [128, NS, 6], F32, "ue")
                hb = [k.sb(p1, [128, 512], F32, "hb") for _ in range(2)]
                cvb = [k.sb(p1, [128, 512], F32, "cvb") for _ in range(2)]
                sca_sb = k.sb(p1, [NS * 2, 512], F32, "sca_sb")
                cap_sb = k.sb(p1, [2, 512], F32, "cap_sb")
                cas_sb = k.sb(p1, [NS * 2, 512], F32, "cas_sb")
                tl = k.sb(p1, [128, 32], F32, "tl")
                k.dma("sp", sca_sb, sca)
                k.memset(ub[:, 0:2], 0.0)
                for m in range(4):
                    if m % 2 == 0:
                        wh, wc, wb = w_next()
                    wo_ = (m % 2) * 128
                    k.pe([trf(PS(PT, w=32), sca_sb[:, m * 128:(m + 1) * 128], identf[0:32, 0:32])], [PS(PT)], [sca_sb, identf])
                    k.copy(ue[:, :, 0:2], V(PS(PT, w=32).ap.rearrange("p (s i) -> p s i", i=2), [pcells[PT]]))
                    for bi, (c0, n) in enumerate(BLKS):
                        bh = pm_next()
                        proj_fm(bh, wh, wo_, XN, 8, c0, n)
                        h_ = hb[bi % 2]
                        k.copy(h_[:, 0:n], PS(bh, w=n), e="act")
                        bc_ = pm_next()
                        proj_fm(bc_, wc, wo_, XN, 8, c0, n)
                        if bi < 4:
                            k.tt(ub[:, 2 + c0: 2 + c0 + n], PS(bc_, w=n), h_[:, 0:n], ALU.mult)
                        else:
                            k.tt(ue[:, :, 2:6], V(PS(bc_, w=n).ap.rearrange("p (s t) -> p s t", t=LS), [pcells[bc_]]),
                                 V(h_.ap[:, 0:n].rearrange("p (s t) -> p s t", t=LS), h_.cells), ALU.mult)
                    w0 = vecs[:, V_CAW + 0 * 4 + m: V_CAW + 0 * 4 + m + 1]
                    w1 = vecs[:, V_CAW + 1 * 4 + m: V_CAW + 1 * 4 + m + 1]
                    w2 = vecs[:, V_CAW + 2 * 4 + m: V_CAW + 2 * 4 + m + 1]
                    for bi, (c0, n) in enumerate(BLKS):
                        cv_ = cvb[bi % 2]
                        if bi < 4:
                            k.ts(cv_[:, 0:n], ub[:, c0 + 2: c0 + 2 + n], w2, ALU.mult)
                            k.stt(cv_[:, 0:n], ub[:, c0 + 1: c0 + 1 + n], w1, cv_[:, 0:n], ALU.mult, ALU.add)
                            k.stt(cv_[:, 0:n], ub[:, c0: c0 + n], w0, cv_[:, 0:n], ALU.mult, ALU.add)
                        else:
                            cv3 = V(cv_.ap[:, 0:n].rearrange("p (s t) -> p s t", t=LS), cv_.cells)
                            k.ts(cv3, ue[:, :, 2:6], w2, ALU.mult)
                            k.stt(cv3, ue[:, :, 1:5], w1, cv3, ALU.mult, ALU.add)
                            k.stt(cv3, ue[:, :, 0:4], w0, cv3, ALU.mult, ALU.add)
                        bb = pm_next()
                        proj_fm(bb, wb, wo_, XN, 8, c0, n)
                        k.tt(SL(SYA + m, c0, n), PS(bb, w=n), cv_[:, 0:n], ALU.mult)
                    k.pe([trf(PS(PT, n=2, w=128), ub[:, TP:TP + 2], identf)], [PS(PT)], [ub, identf])
                    k.copy(cap_sb[:, m * 128:(m + 1) * 128], PS(PT, n=2, w=128), e="act")
                    k.copy(V(tl.ap.rearrange("p (s i) -> p s i", i=2), tl.cells), ue[:, :, 4:6])
                    k.pe([trf(PS(PT, n=32, w=128), tl, identf)], [PS(PT)], [tl, identf])
                    k.copy(cas_sb[:, m * 128:(m + 1) * 128], PS(PT, n=32, w=128), e="act")
                k.dma("sp", cap_o, cap_sb, is_output=True)
                k.dma("sp", cas_o, cas_sb, is_output=True)
                k.barrier()

            def _ph2(p1):
                xeb = [k.sb(p1, [128, 3 + TP], BF16, "xe") for _ in range(2)]
                seb = [k.sb(p1, [128, NS, 7], BF16, "se") for _ in range(2)]
                dg = [k.sb(p1, [128, 4, 128], BF16, "dg") for _ in range(2)]
                ktf = k.sb(p1, [128, T], F32, "ktf")
                sqb = [k.sb(p1, [128, 512], BF16, "sqb") for _ in range(2)]
                rr = [k.sb(p1, [128, 512], F32, "rr") for _ in range(2)]
                sdc_sb = k.sb(p1, [NS * 3, 1536], F32, "sdc_sb")
                dcp_t = [k.sb(p1, [3, 128], F32, "dcp_t") for _ in range(2)]
                dcs_t = [k.sb(p1, [NS * 3, 128], F32, "dcs_t") for _ in range(2)]
                x3 = [k.sb(p1, [128, 3], F32, "x3") for _ in range(2)]
                tl = [k.sb(p1, [128, 48], F32, "tl") for _ in range(2)]
                k.dma("sp", sdc_sb, sdc)
                k.memset(xeb[0][:, 0:3], 0.0)
                k.memset(xeb[1][:, 0:3], 0.0)
                PA = [PM[0], PM[1]]
                PB = [PM[2], PD[0]]
                wq_of = {}

                def stageA(j):
                    if j % 4 == 0:
                        (wq_of[j // 4],) = w_next()
                    wq = wq_of[j // 4]
                    jc = (j % 4) * 128
                    xe, se, dg_ = xeb[j % 2], seb[j % 2], dg[j % 2]
                    for i in range(4):
                        k.act(dg_[:, i, :], identf, AF.Copy, scale=vecs[:, V_DCW + i * 12 + j: V_DCW + i * 12 + j + 1])
                    k.pe([trf(PS(PT, w=48), sdc_sb[:, j * 128:(j + 1) * 128], identf[0:48, 0:48])], [PS(PT)], [sdc_sb, identf])
                    k.copy(se[:, :, 0:3], V(PS(PT, w=48).ap.rearrange("p (s i) -> p s i", i=3), [pcells[PT]]))
                    yield
                    for bi, (c0, n) in enumerate(BLKS):
                        b = PA[bi % 2]
                        proj_fm(b, wq, jc, XN, 8, c0, n)
                        if bi < 4:
                            k.copy(xe[:, 3 + c0: 3 + c0 + n], PS(b, w=n), e=("dve" if bi % 2 else "act"))
                            if bi == 3:
                                k.copy(x3[j % 2], PS(b, w=3, c0=n - 3))
                        else:
                            pv = V(PS(b, w=n).ap.rearrange("p (s t) -> p s t", t=LS), [pcells[b]])
                            k.copy(se[:, :, 3:7], pv, e="act")
                            k.copy(V(tl[j % 2].ap.rearrange("p (s i) -> p s i", i=3), tl[j % 2].cells), pv[:, :, 1:4])
                        yield
                    k.pe([trf(PS(PT, n=3, w=128), x3[j % 2], identf)], [PS(PT)], [x3[j % 2], identf])
                    k.copy(dcp_t[j % 2], PS(PT, n=3, w=128), e="act")
                    k.dma("sp", V(dcp_o.ap[:, j * 128:(j + 1) * 128], []), dcp_t[j % 2], is_output=True)
                    k.pe([trf(PS(PT, n=48, w=128), tl[j % 2], identf)], [PS(PT)], [tl[j % 2], identf])
                    k.copy(dcs_t[j % 2], PS(PT, n=48, w=128), e="act")
                    k.dma("sp", V(dcs_o.ap[:, j * 128:(j + 1) * 128], []), dcs_t[j % 2], is_output=True)
                    yield

                def stageB(j):
                    kind = j // 4
                    dst_slot = SQ + j
                    xe, se, dg_ = xeb[j % 2], seb[j % 2], dg[j % 2]
                    for bi, (c0, n) in enumerate(BLKS):
                        b = PB[bi % 2]
                        if bi < 4:
                            k.pe([mmf(PS(b, w=n), dg_[:, i, :], xe[:, c0 + i: c0 + i + n], i == 0, i == 3) for i in range(4)],
                                 [PS(b, w=n)], [dg_, xe])
                        else:
                            k.pe([mmf(V(PS(b, w=n).ap.rearrange("p (s t) -> p s t", t=LS), [pcells[b]]), dg_[:, i, :], se[:, :, i:i + 4], i == 0, i == 3) for i in range(4)],
                                 [PS(b, w=n)], [dg_, se])
                        if kind == 1:
                            k.act(ktf[:, c0:c0 + n], PS(b, w=n), AF.Silu)
                        else:
                            k.act(SL(dst_slot, c0, n), PS(b, w=n), AF.Silu)
                        yield
                    for bi, (c0, n) in enumerate(BLKS):
                        sq_ = sqb[bi % 2]
                        if kind == 0:
                            k.tt(sq_[:, 0:n], SL(dst_slot, c0, n), SL(dst_slot, c0, n), ALU.mult)
                            h = j
                            if bi < 4:
                                fns = [mmf(PS(PD[1], w=1, c0=t4), sq_[:, t4 * 128:(t4 + 1) * 128], onesb[:, 0:1], True, True) for t4 in range(4)]
                                k.pe(fns, [PS(PD[1])], [sq_, onesb])
                                k.copy(rq[:, bi * 4:(bi + 1) * 4, h], PS(PD[1], w=4))
                            else:
                                fns = [mmf(PS(PD[1], n=LS, w=1, c0=s), sq_[:, s * LS:(s + 1) * LS], onesb[:, 0:1], True, True) for s in range(NS)]
                                k.pe(fns, [PS(PD[1])], [sq_, onesb])
                                k.copy(rqs[:, :, h], PS(PD[1], n=LS, w=NS))
                            yield
                        elif kind == 1:
                            k.tt(sq_[:, 0:n], ktf[:, c0:c0 + n], ktf[:, c0:c0 + n], ALU.mult)
                            k.pe([mmf(PS(PD[1], w=n), onesb, sq_[:, 0:n], True, True)], [PS(PD[1])], [sq_, onesb])
                            r_ = rr[bi % 2]
                            k.act(r_[:, 0:n], PS(PD[1], w=n), AF.Ln, bias=EPS)
                            k.act(r_[:, 0:n], r_[:, 0:n], AF.Exp, scale=-0.5)
                            k.tt(SL(dst_slot, c0, n), ktf[:, c0:c0 + n], r_[:, 0:n], ALU.mult)
                            yield

                def run_il(gens):
                    gens = [g for g in gens if g is not None]
                    while gens:
                        for g in list(gens):
                            try:
                                next(g)
                            except StopIteration:
                                gens.remove(g)

                run_il([stageA(0)])
                for j in range(12):
                    run_il([stageA(j + 1) if j + 1 < 12 else None, stageB(j)])
                for r_, in ((V(rq.ap[:, 0:16, :], rq.cells),), (rqs,)):
                    k.act(r_, r_, AF.Ln, bias=EPS)
                    k.act(r_, r_, AF.Exp, scale=-0.5)
                    k.ts(r_, r_, 128.0 ** -0.5, ALU.mult)
                k.barrier()

            def _ph3(p1):
                eT = [k.sb(p1, [128, 2, 512], BF16, "eT") for _ in range(2)]
                rden = [k.sb(p1, [128, 512], F32, "rden") for _ in range(2)]
                kvb = [k.sb(p1, [128, 2, 1024], BF16, "kvb") for _ in range(3)]

                def kvload(s):
                    if s < NS:
                        k.dma("pool", kvb[s % 3], V(ckv.ap[s * 256:(s + 1) * 256, :].rearrange("(mc p) n -> p mc n", p=128), []))

                for s in range(3):
                    kvload(s)
                KTs = [k.sb(p1, [128, 4, 256], BF16, "KTs") for _ in range(3)]
                eTs = [k.sb(p1, [128, 4, 2, LS], BF16, "eTs") for _ in range(3)]
                rds = [k.sb(p1, [128, 4, LS], F32, "rds") for _ in range(3)]
                (wx,) = w_next()
                for h in range(4):
                    for bi, (c0, n) in enumerate(BLKS):
                        b = pm_next()
                        proj_fm(b, wx, h * 128, XN, 8, c0, n)
                        k.copy(SL(SYM + h, c0, n), PS(b, w=n), e="act")
                w_prefetch()
                sc = 128.0 ** -0.5
                it = 0
                for h in range(4):
                    for bi, (c0, n) in enumerate(BLKS[:4]):
                        e_ = eT[it % 2]
                        r_ = rden[it % 2]
                        BK = PD if it % 2 == 0 else [PM[0], PM[1], PM[2], PT]
                        it += 1
                        for mc in range(2):
                            k.pe([mmf(PS(BK[mc]), KT[:, h, mc * 128:(mc + 1) * 128], SL(SYM + h, c0, n), True, True)],
                                 [PS(BK[mc])], [KT, SL(SYM + h, c0, n)])
                            k.act(e_[:, mc, :], PS(BK[mc]), AF.Exp, scale=sc)
                        k.pe([mmf(PS(BK[2]), Vtok[:, mc, h * 128:(h + 1) * 128], e_[:, mc, :], mc == 0, mc == 1) for mc in range(2)],
                             [PS(BK[2])], [Vtok, e_])
                        k.pe([mmf(PS(BK[3]), onesb, e_[:, mc, :], mc == 0, mc == 1) for mc in range(2)],
                             [PS(BK[3])], [onesb, e_])
                        k.act(r_, PS(BK[3]), AF.Ln)
                        k.act(r_, r_, AF.Exp, scale=-1.0)
                        k.tt(SL(SYM + h, c0, n), PS(BK[2]), r_, ALU.mult)
                def satt(s, BK):
                    kv_, kts, es_, rd_ = kvb[s % 3], KTs[s % 3], eTs[s % 3], rds[s % 3]
                    ck_ = V(kv_.ap[:, :, 0:512], kv_.cells)
                    cv_ = V(kv_.ap[:, :, 512:1024], kv_.cells)
                    bT, bS = BK
                    bO, oc = bS, 64
                    for hp in range(2):
                        pt = PSB(bT)
                        fns = []
                        for hh in range(2):
                            for mc in range(2):
                                h = hp * 2 + hh
                                fns.append(trf(pt[:, hh * 256 + mc * 128: hh * 256 + (mc + 1) * 128], ck_[:, mc, h * 128:(h + 1) * 128], identb))
                        k.pe(fns, [pt], [ck_, identb])
                        k.copy(V(kts.ap[:, hp * 2:hp * 2 + 2, :], kts.cells), V(pt.ap[:, 0:512].rearrange("p (h m) -> p h m", m=256), pt.cells),
                               e=("act" if hp else "dve"))
                        yield
                    c0 = TP + s * LS
                    fns = []
                    for h in range(4):
                        for mc in range(2):
                            fns.append(mmf(PS(bS, w=LS, c0=(h * 2 + mc) * LS), kts[:, h, mc * 128:(mc + 1) * 128], SL(SYM + h, c0, LS), True, True))
                    k.pe(fns, [PS(bS)], [kts] + [SL(SYM + h, c0, LS) for h in range(4)])
                    k.act(V(es_.ap.rearrange("p h m t -> p (h m t)"), es_.cells), PS(bS, w=8 * LS), AF.Exp, scale=sc)
                    yield
                    fns = []
                    for h in range(4):
                        for mc in range(2):
                            fns.append(mmf(PS(bO, w=LS, c0=oc + h * LS), cv_[:, mc, h * 128:(h + 1) * 128], es_[:, h, mc, :], mc == 0, mc == 1))
                        for mc in range(2):
                            fns.append(mmf(PS(bO, w=LS, c0=oc + 4 * LS + h * LS), onesb, es_[:, h, mc, :], mc == 0, mc == 1))
                    k.pe(fns, [PS(bO)], [cv_, es_, onesb])
                    k.recip(V(rd_.ap.rearrange("p h t -> p (h t)"), rd_.cells), PS(bO, w=4 * LS, c0=oc + 4 * LS))
                    k.tt(SL(SYM, c0, LS, nslots=4), V(PS(bO, w=4 * LS, c0=oc).ap.rearrange("p (h t) -> p h t", t=LS), [pcells[bO]]), rd_, ALU.mult)
                    kvload(s + 3)
                    yield

                BKS = [(PT, PD[0]), (PM[0], PM[1]), (PM[2], PD[1])]
                pending = [satt(s, BKS[s % 3]) for s in range(NS)]
                active = []
                while pending or active:
                    while pending and len(active) < 3:
                        active.append(pending.pop(0))
                    for g in list(active):
                        try:
                            next(g)
                        except StopIteration:
                            active.remove(g)
                k.barrier()

            def _ph4(p1):
                (wz,) = w_next()

                def f32t(nm):
                    return k.sb(p1, [128, 4, 128], F32, nm)

                def b16t(nm):
                    return k.sb(p1, [128, 4, 128], BF16, nm)

                Gt, Gam, U_ = b16t("Gt"), b16t("Gam"), b16t("U")
                G4 = f32t("G4")
                def b16h(nm):
                    return k.sb(p1, [128, 2, 128], BF16, nm)

                Lb = [[b16h("L"), b16h("L")] for _ in range(2)]
                Fb = [[b16h("F"), b16h("F")] for _ in range(2)]
                nGh = [b16h("nG"), b16h("nG")]
                Eb = [[b16h("E"), b16h("E")] for _ in range(3)]
                QKm = [b16t("QKm") for _ in range(3)]
                kd = [b16t("kd") for _ in range(3)]
                vtok = [b16t("vtok") for _ in range(3)]
                sm = [k.sb(p1, [128, 16], F32, "sm") for _ in range(3)]
                vS, vnew, ydn = b16t("vS"), b16t("vnew"), b16t("ydn")
                o_ = f32t("o")
                S_, Sb = f32t("S"), b16t("Sb")
                S2_, Sb2 = f32t("S2"), b16t("Sb2")
                sm2 = k.sb(p1, [128, 16], F32, "sm2")
                ab = k.sb(p1, [128, 17, 8], F32, "ab")
                abs_ = k.sb(p1, [LS, NS, 8], F32, "abs")
                tmp17 = k.sb(p1, [128, 17, 4], F32, "tmp17")

                for h in range(4):
                    for bi, (c0, n) in enumerate(BLKS):
                        b = pm_next()
                        proj_fm(b, wz, h * 128, XN, 8, c0, n)
                        k.act(SL(SYD + h, c0, n), PS(b, w=n), AF.Silu)
                for tt_ in range(16):
                    k.pe([mmf(PS(PT, w=8, c0=tt_ * 8), SL(XN + kk, tt_ * 128, 128), wz[:, kk, 512:520], kk == 0, kk == 7) for kk in range(8)],
                         [PS(PT)], [wz] + [SL(XN + kk, tt_ * 128, 128) for kk in range(8)])
                k.copy(V(ab.ap[:, 0:16, :], ab.cells), V(PS(PT, w=128).ap.rearrange("p (t c) -> p t c", c=8), [pcells[PT]]))
                for s in range(NS):
                    k.pe([mmf(PS(PT, n=LS, w=8, c0=s * 8), SL(XN + kk, TP + s * LS, LS), wz[:, kk, 512:520], kk == 0, kk == 7) for kk in range(8)],
                         [PS(PT)], [wz] + [SL(XN + kk, TP + s * LS, LS) for kk in range(8)])
                k.copy(abs_, V(PS(PT, n=LS, w=128).ap.rearrange("p (t c) -> p t c", c=8), [pcells[PT]]))
                for (a_, g_, nt, npart) in ((ab, gb, 16, 128), (abs_, gbs, NS, LS)):
                    av = V(a_.ap[0:npart, 0:nt, 0:4], a_.cells)
                    bv = V(a_.ap[0:npart, 0:nt, 4:8], a_.cells)
                    gv = V(g_.ap[0:npart, 0:nt, 0:4], g_.cells)
                    gbv = V(g_.ap[0:npart, 0:nt, 4:8], g_.cells)
                    tv = V(tmp17.ap[0:npart, 0:nt, :], tmp17.cells)
                    k.act(gbv, bv, AF.Sigmoid)
                    k.tt(tv, av, vecs[0:npart, V_DTB:V_DTB + 4].us(1).bc([npart, nt, 4]), ALU.add)
                    k.act(tv, tv, AF.Exp)
                    k.act(tv, tv, AF.Ln, bias=1.0)
                    k.tt(gv, tv, negA[0:npart, :].us(1).bc([npart, nt, 4]), ALU.mult)

                def bc4(v, n, w=128):
                    return v.us(2).bc([n, 4, w])

                def mbc(m, n):
                    return V(m.ap[0:n, 0:n].unsqueeze(1).to_broadcast([n, 4, n]), m.cells)

                def P4(bank, n, w):
                    return V(psap[0:n, bank, 0:4 * w].rearrange("p (h w) -> p h w", w=w), [pcells[bank]])

                F0, F1, SA, SB = PD[0], PD[1], PD[2], PD[3]

                def front(n, c0, gtok, btok, i):
                    E_, Q_, kd_, vt_, s_ = Eb[i % 3], QKm[i % 3], kd[i % 3], vtok[i % 3], sm[i % 3]
                    L_, F_ = Lb[i % 2], Fb[i % 2]
                    kT = [SL(SK + h, c0, n) for h in range(4)]
                    qT = [SL(SQ + h, c0, n) for h in range(4)]
                    vT = [SL(SV_ + h, c0, n) for h in range(4)]
                    k.pe([mmf(PS(F0, n=n, w=4), ML[0:n, 0:n], gtok, True, True),
                          mmf(PS(F0, n=128, w=4, c0=8), onesf[0:n, :], gtok, True, True)], [PS(F0)], [ML, onesf, gtok])
                    k.copy(s_[0:n, 0:4], PS(F0, n=n, w=4), e="act")
                    k.act(s_[:, 12:16], PS(F0, w=4, c0=8), AF.Exp)
                    k.tt(s_[0:n, 8:12], PS(F0, n=n, w=4, c0=8), s_[0:n, 0:4], ALU.subtract)
                    k.act(s_[0:n, 4:8], s_[0:n, 0:4], AF.Exp)
                    k.act(s_[0:n, 8:12], s_[0:n, 8:12], AF.Exp)
                    yield
                    for h in range(4):
                        k.act(G4[0:n, h, 0:n], MG[0:n, 0:n], AF.Copy, scale=gtok[:, h:h + 1])
                    yield
                    k.pe([mmf(P4(F0, n, n)[:, h, :], G4[0:n, h, 0:n], ML[0:n, 0:n], True, True) for h in range(4)],
                         [PS(F0)], [G4, ML])
                    k.pe([mmf(P4(F1, n, n)[:, h, :], kT[h], kT[h], True, True) for h in range(4)], [PS(F1)], kT)
                    k.act(Gam[0:n, :, 0:n], P4(F0, n, n), AF.Exp)
                    yield
                    pt = PSB(PT, n)
                    k.pe([trf(pt[:, h * 128:(h + 1) * 128], kT[h], identb) for h in range(4)] +
                         [trf(pt[:, 512 + h * 128: 512 + (h + 1) * 128], vT[h], identb) for h in range(4)], [pt], kT + vT + [identb])
                    p8 = V(pt.ap.rearrange("p (a h w) -> p a h w", a=2, w=128), pt.cells)
                    k.tt(kd_[0:n], p8[:, 0], bc4(s_[0:n, 8:12], n), ALU.mult)
                    k.copy(vt_[0:n], p8[:, 1], e="act")
                    yield
                    k.pe([mmf(P4(F0, n, n)[:, h, :], kT[h], qT[h], True, True) for h in range(4)], [PS(F0)], kT + qT)
                    k.tt(Gt[0:n, :, 0:n], P4(F0, n, n), Gam[0:n, :, 0:n], ALU.mult)
                    yield
                    k.tt(Q_[0:n, :, 0:n], Gt[0:n, :, 0:n], mbc(ML, n), ALU.mult)
                    k.tt(Gam[0:n, :, 0:n], Gam[0:n, :, 0:n], mbc(MUS, n), ALU.mult)
                    yield
                    for h in range(4):
                        k.stt(U_[0:n, h, 0:n], P4(F1, n, n)[:, h, :], btok[:, h:h + 1], Gam[0:n, h, 0:n], ALU.mult, ALU.mult)
                        if h == 1:
                            yield
                    yield
                    ptl = PSB(PT, n)
                    k.pe([trf(ptl[:, h * 128: h * 128 + n], U_[0:n, h, 0:n], identb[0:n, 0:n]) for h in range(4)], [ptl], [U_, identb])
                    ptl4 = V(ptl.ap[:, 0:512].rearrange("p (h w) -> p h w", w=128)[:, :, 0:n], ptl.cells)
                    idb2 = V(identf.ap[0:n, 0:n].unsqueeze(1).to_broadcast([n, 2, n]), identf.cells)
                    k.tt(Gt[0:n, :, 0:n], U_[0:n, :, 0:n], mbc(LV(0), n), ALU.mult)
                    for hh in range(2):
                        k.copy(L_[hh][0:n, :, 0:n], ptl4[:, 2 * hh:2 * hh + 2, :], e="act")
                        k.tt(E_[hh][0:n, :, 0:n], idb2, Gt[0:n, 2 * hh:2 * hh + 2, 0:n], ALU.add)
                    k.tt(Gam[0:n, :, 0:n], ptl4, mbc(LVT(0), n), ALU.mult)
                    yield
                    for hh in range(2):
                        k.tt(F_[hh][0:n, :, 0:n], idb2, Gam[0:n, 2 * hh:2 * hh + 2, 0:n], ALU.add)
                    yield

                def solve(n, i, hh):
                    E_, L_, F_, nG = Eb[i % 3][hh], Lb[i % 2][hh], Fb[i % 2][hh], nGh[hh]
                    bank = SA if hh == 0 else SB
                    nl = n.bit_length() - 1

                    def H2(c):
                        return V(psap[0:n, bank, c * 256: c * 256 + 2 * n].rearrange("p (h w) -> p h w", w=n), [pcells[bank]])

                    m2 = lambda m: V(m.ap[0:n, 0:n].unsqueeze(1).to_broadcast([n, 2, n]), m.cells)
                    for l in range(1, nl):
                        k.pe([mmf(H2(0)[:, h, :], L_[0:n, h, 0:n], E_[0:n, h, 0:n], True, True) for h in range(2)],
                             [PS(bank)], [L_, E_])
                        k.tt(nG[0:n, :, 0:n], H2(0), m2(LV(l)), ALU.mult)
                        yield
                        fns = []
                        for h in range(2):
                            fns.append(mmf(H2(1)[:, h, :], F_[0:n, h, 0:n], identb[0:n, 0:n], True, False))
                            fns.append(mmf(H2(1)[:, h, :], F_[0:n, h, 0:n], nG[0:n, h, 0:n], False, True))
                        if l < nl - 1:
                            for h in range(2):
                                fns.append(mmf(H2(0)[:, h, :], identb[0:n, 0:n], F_[0:n, h, 0:n], True, False))
                                fns.append(mmf(H2(0)[:, h, :], nG[0:n, h, 0:n], F_[0:n, h, 0:n], False, True))
                        k.pe(fns, [PS(bank)], [F_, nG, identb])
                        k.copy(E_[0:n, :, 0:n], H2(1), e="act")
                        if l < nl - 1:
                            k.copy(F_[0:n, :, 0:n], H2(0), e="act")
                        yield

                def seq(n, c0, btok, rqtok, i, S_=S_, Sb=Sb, need_sb=True):
                    E_, Q_, kd_, vt_, s_ = Eb[i % 3], QKm[i % 3], kd[i % 3], vtok[i % 3], sm[i % 3]
                    kT = [SL(SK + h, c0, n) for h in range(4)]
                    qT = [SL(SQ + h, c0, n) for h in range(4)]
                    b1, b2, b3 = PM[0], PM[1], PM[2]
                    k.pe([mmf(P4(b1, n, 128)[:, h, :], kT[h], Sb[:, h, :], True, True) for h in range(4)], [PS(b1)], kT + [Sb])
                    k.pe([mmf(P4(b2, n, 128)[:, h, :], qT[h], Sb[:, h, :], True, True) for h in range(4)], [PS(b2)], qT + [Sb])
                    k.tt(o_[0:n], P4(b1, n, 128), bc4(s_[0:n, 4:8], n), ALU.mult)
                    yield
                    k.tt(vS[0:n], vt_[0:n], o_[0:n], ALU.subtract)
                    yield
                    k.pe([mmf(P4(b3, n, 128)[:, h, :], E_[h // 2][0:n, h % 2, 0:n], vS[0:n, h, :], True, True) for h in range(4)], [PS(b3)], [E_[0], E_[1], vS])
                    k.tt(vnew[0:n], P4(b3, n, 128), bc4(btok, n), ALU.mult)
                    k.tt(o_[0:n], P4(b2, n, 128), bc4(s_[0:n, 4:8], n), ALU.mult)
                    yield
                    k.pe([mmf(P4(b1, n, 128)[:, h, :], Q_[0:n, h, 0:n], vnew[0:n, h, :], True, True) for h in range(4)], [PS(b1)], [Q_, vnew])
                    k.pe([mmf(P4(b3, 128, 128)[:, h, :], kd_[0:n, h, :], vnew[0:n, h, :], True, True) for h in range(4)], [PS(b3)], [kd_, vnew])
                    k.tt(S_, S_, bc4(s_[:, 12:16], 128), ALU.mult)
                    yield
                    k.tt(S_, S_, P4(b3, 128, 128), ALU.add)
                    if need_sb:
                        k.copy(Sb, S_, e="act")
                    yield
                    k.tt(o_[0:n], o_[0:n], P4(b1, n, 128), ALU.add)
                    for h in range(4):
                        k.act(ydn[0:n, h, :], o_[0:n, h, :], AF.Square, accum=sm2[0:n, h:h + 1])
                    yield
                    k.tt(sm2[0:n, 4:8], rqtok, rqtok, ALU.mult)
                    k.tt(sm2[0:n, 0:4], sm2[0:n, 0:4], sm2[0:n, 4:8], ALU.mult)
                    k.act(sm2[0:n, 0:4], sm2[0:n, 0:4], AF.Ln, scale=1.0 / 128, bias=EPS)
                    k.act(sm2[0:n, 0:4], sm2[0:n, 0:4], AF.Exp, scale=-0.5)
                    k.tt(sm2[0:n, 0:4], sm2[0:n, 0:4], rqtok, ALU.mult)
                    yield
                    k.tt(o_[0:n], o_[0:n], bc4(sm2[0:n, 0:4], n), ALU.mult)
                    k.tt(ydn[0:n], o_[0:n], V(vecs.ap[0:n, V_DNN:V_DNN + 128].unsqueeze(1).to_broadcast([n, 4, 128]), vecs.cells), ALU.mult)
                    pt = PSB(PT)
                    k.pe([trf(pt[:, h * 128: h * 128 + n], ydn[0:n, h, :], identb[0:n, 0:n]) for h in range(4)], [pt], [ydn, identb])
                    k.tt(SL(SYD, c0, n, nslots=4), V(pt.ap[:, 0:512].rearrange("p (h w) -> p h w", w=128)[:, :, 0:n], pt.cells),
                         SL(SYD, c0, n, nslots=4), ALU.mult)
                    yield

                def run_interleaved(gens, strides=None):
                    items = [(g, (strides[i] if strides else 1)) for i, g in enumerate(gens) if g is not None]
                    r = 0
                    while items:
                        for it_ in list(items):
                            g, st = it_
                            if r % st:
                                continue
                            try:
                                next(g)
                            except StopIteration:
                                items.remove(it_)
                        r += 1

                k.memset(S_, 0.0)
                k.memset(Sb, 0.0)
                NCH = TP // 128

                def pfront(c):
                    return front(128, c * 128, gb[:, c, 0:4], gb[:, c, 4:8], c) if c < NCH else None

                def psolve(c, hh):
                    return solve(128, c, hh) if c < NCH else None

                run_interleaved([pfront(0)])
                run_interleaved([pfront(1), psolve(0, 0), psolve(0, 1)])
                for c in range(NCH):
                    run_interleaved([pfront(c + 2), psolve(c + 1, 0), psolve(c + 1, 1), seq(128, c * 128, gb[:, c, 4:8], rq[:, c, :], c)], [2, 1, 1, 1])
                k.dma("sp", V(sp_o.ap.rearrange("(h k) v -> k h v", k=128), []), S_, is_output=True)
                def sfront(s_i):
                    return front(LS, TP + s_i * LS, gbs[:, s_i, 0:4], gbs[:, s_i, 4:8], s_i) if s_i < NS else None

                def ssolve(s_i, hh):
                    return solve(LS, s_i, hh) if s_i < NS else None

                Sd, Sbd = [S_, S2_], [Sb, Sb2]

                def sload(s_i):
                    if s_i < NS:
                        k.dma("sp", Sd[s_i % 2], V(sdn.ap[s_i * 512:(s_i + 1) * 512, :].rearrange("(h k) v -> k h v", k=128), []))

                def sseq(s_i):
                    Sa, Sba = Sd[s_i % 2], Sbd[s_i % 2]
                    sload(s_i + 1)
                    k.copy(Sba, Sa, e="act")
                    yield
                    yield from seq(LS, TP + s_i * LS, gbs[:, s_i, 4:8], rqs[:, s_i, :], s_i, Sa, Sba, need_sb=False)
                    k.dma("sp", V(ss_o.ap[s_i * 512:(s_i + 1) * 512, :].rearrange("(h k) v -> k h v", k=128), []), Sa, is_output=True)
                    yield

                sload(0)

                run_interleaved([sfront(0)])
                run_interleaved([sfront(1), ssolve(0, 0), ssolve(0, 1)])
                for s_i in range(NS):
                    run_interleaved([sfront(s_i + 2), ssolve(s_i + 1, 0), ssolve(s_i + 1, 1), sseq(s_i)])
                k.barrier()

            SMG = 8

            def _ph5(p1):
                sg = [k.sb(p1, [128, 512], F32, "sg") for _ in range(2)]
                tm = [k.sb(p1, [128, 512], F32, "tm") for _ in range(2)]
                accf = [k.sb(p1, [128, T], F32, "accf") for _ in range(2)]
                it = 0
                ysl = [SYA, SYD, SYM]
                for mp in range(4):
                    for b3 in range(3):
                        wg, wb = w_next()
                        for mm in range(2):
                            m = mp * 2 + mm
                            wo_ = mm * 128
                            for bi, (c0, n) in enumerate(BLKS):
                                a_ = accf[mm][:, c0:c0 + n]
                                s_ = sg[it % 2]
                                t_ = tm[it % 2]
                                it += 1
                                bg = pm_next()
                                proj_fm(bg, wg, wo_, XN, 8, c0, n)
                                k.act(s_[:, 0:n], PS(bg, w=n), AF.Sigmoid)
                                bp = pm_next()
                                k.pe([mmf(PS(bp, w=n), wb[:, kk, wo_:wo_ + 128], SL(ysl[b3] + kk, c0, n), kk == 0, kk == 3) for kk in range(4)],
                                     [PS(bp, w=n)], [wb] + [SL(ysl[b3] + kk, c0, n) for kk in range(4)])
                                if b3 == 0:
                                    k.tt(a_, s_[:, 0:n], PS(bp, w=n), ALU.mult)
                                elif b3 == 1:
                                    k.tt(t_[:, 0:n], s_[:, 0:n], PS(bp, w=n), ALU.mult)
                                    k.tt(a_, a_, t_[:, 0:n], ALU.add)
                                else:
                                    k.tt(t_[:, 0:n], s_[:, 0:n], PS(bp, w=n), ALU.mult)
                                    k.tt(SL(SMG + m, c0, n), a_, t_[:, 0:n], ALU.add)
                for nm_, sl_ in (("ya", SYA), ("yd", SYD), ("ym", SYM), ("mg", SMG)):
                    tap(nm_ + "0", SL(sl_, 0, T))
                    tap(nm_ + "3", SL(sl_ + 3, 0, T))
                k.barrier()

            SX = 16
            def _ph6(p2):
                x1s = k.sb(p2, [128, D], F32, "x1s")
                xs = [k.sb(p2, [128, D], BF16, "xs2") for _ in range(2)]
                junk = k.sb(p2, [128, D], BF16, "junk2")
                ssq = [k.sb(p2, [128, 1], F32, "ssq2") for _ in range(2)]
                sgl = [k.sb(p2, [128, 512], F32, "sgl") for _ in range(2)]
                yt = [k.sb(p2, [128, D], F32, "yt") for _ in range(2)]
                gfin = k.sb(p2, [128, D], F32, "gfin")
                k.dma("sp", gfin, gfin_d)
                X1BASE = SX * (SLOTW // 2)

                def X1(tt_, n, c0=0, w=D):
                    if tt_ < 16:
                        a0 = X1BASE + tt_ * D + c0
                        return V(arena32[0:n, a0:a0 + w], cells_bytes(SX, (tt_ * D + c0) * 4, (tt_ * D + c0 + w) * 4))
                    return x1s[0:n, c0:c0 + w]

                (woA,) = w_next()
                (woB,) = w_next(prefetch=False)
                wo2 = [woA, woB]
                def p2tile(tt_):
                    n = 128 if tt_ < 16 else 64
                    t0 = tt_ * 128
                    for half in range(2):
                        b = (PM[half] if tt_ % 2 == 0 else PD[half])
                        k.pe([mmf(PS(b, n=n), SL(SMG + kk, t0, n), wo2[half][:, kk, :], kk == 0, kk == 7) for kk in range(8)],
                             [PS(b)], [wo2[half]] + [SL(SMG + kk, t0, n) for kk in range(8)])
                        xv = X1(tt_, n, half * 512, 512)
                        k.tt(xv, xv, PS(b, n=n), ALU.add)
                        yield
                    q_, s_ = ssq[tt_ % 2], xs[tt_ % 2]
                    k.act(junk[0:n, :], X1(tt_, n), AF.Square, accum=q_[0:n, :])
                    k.act(q_[0:n, :], q_[0:n, :], AF.Sqrt, scale=1.0 / D, bias=EPS)
                    yield
                    k.recip(q_[0:n, :], q_[0:n, :])
                    k.ts(s_[0:n, :], X1(tt_, n), q_[0:n, 0:1], ALU.mult)
                    yield
                    pt = PSB(PT if tt_ % 2 == 0 else PD[2])
                    k.pe([trf(pt[:, kk * 128: kk * 128 + n], s_[0:n, kk * 128:(kk + 1) * 128], identb[0:n, 0:n]) for kk in range(8)],
                         [pt], [s_, identb])
                    ptv = V(pt.ap.rearrange("p (k t) -> p k t", t=128)[:, :, 0:n], pt.cells)
                    k.tt(SL(XN, t0, n, nslots=8), ptv, vecs[:, V_GFFN:V_GFFN + 8].us(2).bc([128, 8, n]), ALU.mult)
                    yield

                for tt_ in range(17):
                    n_ = 128 if tt_ < 16 else 64
                    k.dma("sp", X1(tt_, n_), V(xin.ap[tt_ * 128: tt_ * 128 + n_, :], []))
                run_window([p2tile(tt_) for tt_ in range(17)], 2)
                w_issue(wstate["used"] + 1)
                SACT = 8
                it = 0
                for (j0, nj) in FFN_PARTS:
                    for jj in range(nj):
                        if jj % 2 == 0:
                            wg, wu = w_next()
                        wo_ = (jj % 2) * 128
                        for bi, (c0, n) in enumerate(BLKS):
                            bg = pm_next()
                            proj_fm(bg, wg, wo_, XN, 8, c0, n)
                            s_ = sgl[it % 2]
                            it += 1
                            k.act(s_[:, 0:n], PS(bg, w=n), AF.Silu)
                            bu = pm_next()
                            proj_fm(bu, wu, wo_, XN, 8, c0, n)
                            k.tt(SL(SACT + jj, c0, n), s_[:, 0:n], PS(bu, w=n), ALU.mult)
                    for half in range(2):
                        (wd,) = w_next()
                        for tt_ in range(17):
                            n = 128 if tt_ < 16 else 64
                            t0 = tt_ * 128
                            b = pm_next()
                            k.pe([mmf(PS(b, n=n), SL(SACT + kk, t0, n), wd[:, kk, :], kk == 0, kk == nj - 1) for kk in range(nj)],
                                 [PS(b)], [wd] + [SL(SACT + kk, t0, n) for kk in range(nj)])
                            xv = X1(tt_, n, half * 512, 512)
                            k.tt(xv, xv, PS(b, n=n), ALU.add)
                def p4tile(tt_):
                    n = 128 if tt_ < 16 else 64
                    q_, y_ = ssq[tt_ % 2], yt[tt_ % 2]
                    k.act(junk[0:n, :], X1(tt_, n), AF.Square, accum=q_[0:n, :])
                    k.act(q_[0:n, :], q_[0:n, :], AF.Sqrt, scale=1.0 / D, bias=EPS)
                    yield
                    k.recip(q_[0:n, :], q_[0:n, :])
                    k.stt(y_[0:n, :], X1(tt_, n), q_[0:n, 0:1], gfin[0:n, :], ALU.mult, ALU.mult)
                    yield
                    k.dma("sp", V(y_o.ap[tt_ * 128: tt_ * 128 + n, :], []), y_[0:n, :], is_output=True)
                    yield

                run_window([p4tile(tt_) for tt_ in range(17)], 2)

            for _i, _ph in enumerate([_ph0, _ph1, _ph2, _ph3, _ph4, _ph5, _ph6]):
                with contextlib.ExitStack() as _pes:
                    _ph(_pes)
                PHASE_MARKS.append((_i, k.npe, dict(k.cnt)))
                if STOP == _i + 1:
                    break
            else:
                assert wstate["used"] == len(jobs), (wstate, len(jobs))
        except _Stop:
            pass
        for ev in k.out_events:
            k.need("sp", ev)
        k.barrier(engines=("sp",), pool_dma=True)
    return nc


_NC = None


def _consts():
    c = np.zeros((128, NCST), np.float32)
    i = np.arange(128)
    c[:, K_ID:K_ID + 128] = np.eye(128)
    c[:, K_ONE:K_ONE + 128] = 1.0
    c[:, K_ML:K_ML + 128] = (i[:, None] <= i[None, :])
    c[:, K_MG:K_MG + 128] = (i[:, None] > i[None, :])
    c[:, K_MUS:K_MUS + 128] = (i[:, None] < i[None, :])
    for l in range(7):
        m = ((i[:, None] >> (l + 1)) == (i[None, :] >> (l + 1))) & (((i[:, None] >> l) & 1) == 0) & (((i[None, :] >> l) & 1) == 1)
        c[:, K_LV + l * 128: K_LV + (l + 1) * 128] = -m.astype(np.float32)
        c[:, K_LV + (7 + l) * 128: K_LV + (8 + l) * 128] = -m.T.astype(np.float32)
    return c


def kernel(x_prompt, x_sample, mem_prompt, state_conv_a, state_dn_conv, state_dn, cache_mem_k,
           cache_mem_v, norm_mix, w_in, conv_a_w, dn_conv_w, dn_a_log, dn_dt_bias, dn_norm,
           norm_mem, w_mem_kv, w_branch, w_o, norm_ffn, w_ffn_up, w_ffn_down, norm_final):
    global _NC
    f = lambda a: np.ascontiguousarray(np.asarray(a, dtype=np.float32))
    x_prompt, x_sample, mem_prompt = f(x_prompt), f(x_sample), f(mem_prompt)
    if _NC is None:
        _NC = build_nc()
    vec = np.zeros((128, NVEC), np.float32)
    vec[:, V_GMIX:V_GMIX + 8] = f(norm_mix)[0].reshape(8, 128).T
    vec[:, V_GMEM:V_GMEM + 8] = f(norm_mem)[0].reshape(8, 128).T
    vec[:, V_GFFN:V_GFFN + 8] = f(norm_ffn)[0].reshape(8, 128).T
    vec[:, V_CAW:V_CAW + 12] = f(conv_a_w)[0].reshape(3, 4, 128).transpose(2, 0, 1).reshape(128, 12)
    vec[:, V_DCW:V_DCW + 48] = f(dn_conv_w)[0].reshape(4, 12, 128).transpose(2, 0, 1).reshape(128, 48)
    vec[:, V_ALOG:V_ALOG + 4] = f(dn_a_log)[0][None, :]
    vec[:, V_DTB:V_DTB + 4] = f(dn_dt_bias)[0][None, :]
    vec[:, V_DNN:V_DNN + 128] = f(dn_norm)[0][None, :]
    cst = _consts()
    shared = dict(w_in=f(w_in)[0], w_kv=f(w_mem_kv)[0], w_br=f(w_branch)[0], w_o=f(w_o)[0],
                  w_up=f(w_ffn_up)[0], w_dn=f(w_ffn_down)[0], vecs=vec, cst=cst,
                  gfin=np.ascontiguousarray(np.broadcast_to(f(norm_final)[None, :], (128, D))))
    sca, sdc, sdn = f(state_conv_a)[0], f(state_dn_conv)[0], f(state_dn)[0]
    ck, cv = f(cache_mem_k)[0], f(cache_mem_v)[0]
    in_maps = []
    for c in range(NCORES):
        s0, s1 = c * NS, (c + 1) * NS
        m = dict(shared)
        m["xin"] = np.ascontiguousarray(np.concatenate([x_prompt[c], x_sample[s0:s1].reshape(NS * LS, D)], axis=0))
        m["mem"] = mem_prompt[c]
        m["sca"] = np.ascontiguousarray(sca[s0:s1].reshape(NS * 2, 512))
        m["sdc"] = np.ascontiguousarray(sdc[s0:s1].reshape(NS * 3, 1536))
        m["sdn"] = np.ascontiguousarray(sdn[s0:s1].reshape(NS * 512, 128))
        m["ckv"] = np.ascontiguousarray(np.concatenate([ck[s0:s1].reshape(NS * 256, 512), cv[s0:s1].reshape(NS * 256, 512)], axis=1))
        in_maps.append(m)
    if DBG_CORES:
        res = run_bass_kernel_spmd(_NC, in_maps[:DBG_CORES], core_ids=list(range(DBG_CORES)))
        R = list(res.results) + [res.results[0]] * (NCORES - DBG_CORES)
        global LAST_R
        LAST_R = res.results[0]
    else:
        res = run_bass_kernel_spmd(_NC, in_maps, core_ids=list(range(NCORES)))
        R = res.results
    y_prompt = np.stack([R[c]["y"][:TP] for c in range(NCORES)])
    y_sample = np.concatenate([R[c]["y"][TP:].reshape(NS, LS, D) for c in range(NCORES)], axis=0)
    ca_p = np.stack([R[c]["ca_p"] for c in range(NCORES)])[None]
    dc_p = np.stack([R[c]["dc_p"] for c in range(NCORES)])[None]
    s_p = np.stack([R[c]["s_p"].reshape(4, 128, 128) for c in range(NCORES)])[None]
    mk_p = np.stack([R[c]["mk"].reshape(256, 4, 128) for c in range(NCORES)])[None]
    mv_p = np.stack([R[c]["mv"].reshape(256, 4, 128) for c in range(NCORES)])[None]
    ca_s = np.concatenate([R[c]["ca_s"].reshape(NS, 2, 512) for c in range(NCORES)], axis=0)[None]
    dc_s = np.concatenate([R[c]["dc_s"].reshape(NS, 3, 1536) for c in range(NCORES)], axis=0)[None]
    s_s = np.concatenate([R[c]["s_s"].reshape(NS, 4, 128, 128) for c in range(NCORES)], axis=0)[None]
    return tuple(np.ascontiguousarray(a, dtype=np.float32) for a in
                 (y_prompt, y_sample, ca_p, dc_p, s_p, mk_p, mv_p, ca_s, dc_s, s_s))
```
